# Optimizing a Trainium2 kernel written in Bass

```python
import jax, jax.numpy as jnp
from jax import lax
import numpy as np

D_MODEL = 1024
BATCH = 8
SEQ = 8192
DEPTH = 2

N_MIXERS = 2
MLA_HEADS = 8
MLA_Q_LORA = 256
MLA_KV_LORA = 256
MLA_NOPE = 128
MLA_ROPE = 64
MLA_V = 128
ROPE_THETA = 10000.0
Q_BLOCK = 128
HGRN_HEADS = 8
HGRN_DK = D_MODEL // HGRN_HEADS
HGRN_DV = D_MODEL // HGRN_HEADS
HGRN_CHUNK = 64
D_FF = ((8 * D_MODEL // 3 + 255) // 256) * 256
D_PLE = 256
LN_EPS = 1e-5
RMS_EPS = 1e-6
DEEPNORM_ALPHA = (2 * DEPTH) ** 0.25
DEEPNORM_BETA = (8 * DEPTH) ** -0.25
N_MLA_LAYERS = (DEPTH + N_MIXERS - 1) // N_MIXERS
N_HGRN_LAYERS = DEPTH // N_MIXERS

kernel_name = 'hybrid_mla_hgrn2_deepnorm_ple'


def layer_norm(x, g, b):
    xf = x.astype(jnp.float32)
    mu = jnp.mean(xf, axis=-1, keepdims=True)
    var = jnp.mean(jnp.square(xf - mu), axis=-1, keepdims=True)
    y = (xf - mu) * lax.rsqrt(var + LN_EPS) * g.astype(jnp.float32) + b.astype(jnp.float32)
    return y.astype(x.dtype)


def rms_norm(x, g):
    xf = x.astype(jnp.float32)
    y = xf * lax.rsqrt(jnp.mean(jnp.square(xf), axis=-1, keepdims=True) + RMS_EPS) * g.astype(jnp.float32)
    return y.astype(x.dtype)


def rope_tables(positions):
    inv_freq = ROPE_THETA ** (-jnp.arange(0, MLA_ROPE, 2, dtype=jnp.float32) / MLA_ROPE)
    ang = positions.astype(jnp.float32)[..., None] * inv_freq
    return jnp.cos(ang), jnp.sin(ang)


def apply_rope(t, cos, sin):
    half = t.shape[-1] // 2
    t1, t2 = t[..., :half], t[..., half:]
    out = jnp.concatenate([t1 * cos - t2 * sin, t2 * cos + t1 * sin], axis=-1)
    return out.astype(t.dtype)


def mla_mixer(x, cos, sin, w_dqkv, q_norm, kv_norm, w_uq, w_ukv, w_o):
    B, S, _ = x.shape
    H = MLA_HEADS
    down = x @ w_dqkv
    c_q = rms_norm(down[..., :MLA_Q_LORA], q_norm)
    c_kv = rms_norm(down[..., MLA_Q_LORA:MLA_Q_LORA + MLA_KV_LORA], kv_norm)
    k_rope = apply_rope(down[..., MLA_Q_LORA + MLA_KV_LORA:], cos, sin)
    q = (c_q @ w_uq).reshape(B, S, H, MLA_NOPE + MLA_ROPE)
    q_nope = q[..., :MLA_NOPE]
    q_rope = apply_rope(q[..., MLA_NOPE:], cos[:, :, None, :], sin[:, :, None, :])
    kv = (c_kv @ w_ukv).reshape(B, S, H, MLA_NOPE + MLA_V)
    k_nope, v = kv[..., :MLA_NOPE], kv[..., MLA_NOPE:]

    nblk = S // Q_BLOCK
    scale = (MLA_NOPE + MLA_ROPE) ** -0.5
    kpos = jnp.arange(S, dtype=jnp.int32)
    qpos = kpos.reshape(nblk, Q_BLOCK)

    def to_blocks(t):
        return jnp.moveaxis(t.reshape((B, nblk, Q_BLOCK) + t.shape[2:]), 1, 0)

    def attend(blk):
        qn, qr, qp = blk
        s = (jnp.einsum('bqhd,bkhd->bhqk', qn, k_nope)
             + jnp.einsum('bqhr,bkr->bhqk', qr, k_rope)).astype(jnp.float32) * scale
        s = jnp.where(kpos[None, None, None, :] <= qp[None, None, :, None], s, -jnp.inf)
        w = jax.nn.softmax(s, axis=-1).astype(v.dtype)
        return jnp.einsum('bhqk,bkhd->bqhd', w, v)

    o = lax.map(attend, (to_blocks(q_nope), to_blocks(q_rope), qpos))
    o = jnp.moveaxis(o, 0, 1).reshape(B, S, H * MLA_V)
    return o @ w_o


def hgrn2_mixer(x, lb, w_in, out_norm, w_o):
    B, S, _ = x.shape
    H, DK, DV, C = HGRN_HEADS, HGRN_DK, HGRN_DV, HGRN_CHUNK
    proj = x @ w_in
    q = jax.nn.silu(proj[..., :H * DK].astype(jnp.float32)).reshape(B, S, H, DK)
    f = proj[..., H * DK:2 * H * DK].astype(jnp.float32).reshape(B, S, H, DK)
    v = proj[..., 2 * H * DK:2 * H * DK + H * DV].astype(jnp.float32).reshape(B, S, H, DV)
    gate = proj[..., 2 * H * DK + H * DV:]
    lb = lb.reshape(H, DK)
    log_f = jnp.logaddexp(jnp.log(lb), jnp.log1p(-lb) + jax.nn.log_sigmoid(f))
    k = (1.0 - lb) * jax.nn.sigmoid(-f)

    n = S // C

    def to_chunks(t):
        return t.reshape(B, n, C, H, t.shape[-1]).transpose(1, 0, 3, 2, 4)

    causal = jnp.tril(jnp.ones((C, C), dtype=bool))

    def step(state, chunk):
        qc, kc, gc, vc = chunk
        G = jnp.cumsum(gc, axis=2)
        inter = jnp.einsum('bhtd,bhde->bhte', qc * jnp.exp(G), state)
        diff = G[:, :, :, None, :] - G[:, :, None, :, :]
        decay = jnp.exp(jnp.where(causal[:, :, None], diff, -jnp.inf))
        A = jnp.einsum('bhtd,bhsd,bhtsd->bhts', qc, kc, decay)
        intra = jnp.einsum('bhts,bhse->bhte', A, vc)
        G_last = G[:, :, -1:, :]
        new_state = (jnp.exp(G_last[:, :, 0, :])[..., None] * state
                     + jnp.einsum('bhsd,bhse->bhde', kc * jnp.exp(G_last - G), vc))
        return new_state, inter + intra

    s0 = jnp.zeros((B, H, DK, DV), jnp.float32)
    _, o = lax.scan(step, s0, (to_chunks(q), to_chunks(k), to_chunks(log_f), to_chunks(v)))
    o = o.transpose(1, 0, 3, 2, 4).reshape(B, S, H, DV)
    o = rms_norm(o, out_norm.reshape(H, DV))
    o = o * jax.nn.silu(gate.astype(jnp.float32)).reshape(B, S, H, DV)
    return o.reshape(B, S, H * DV).astype(x.dtype) @ w_o


def swiglu_ffn(x, w_in, w_down):
    h = x @ w_in
    g, u = h[..., :D_FF], h[..., D_FF:]
    return (jax.nn.silu(g) * u) @ w_down


def setup_inputs(seed: int = 0) -> dict:
    key = jax.random.key(seed)
    ks = jax.random.split(key, 24)

    def w(k, shape, fan_in, scale=1.0):
        return jax.random.normal(k, shape, jnp.float32) * (fan_in ** -0.5) * scale

    def gain(k, shape):
        return 1.0 + 0.05 * jax.random.normal(k, shape, jnp.float32)

    def bias(k, shape):
        return 0.02 * jax.random.normal(k, shape, jnp.float32)

    NM, NH = N_MLA_LAYERS, N_HGRN_LAYERS
    return {
        'x': jax.random.normal(ks[0], (BATCH, SEQ, D_MODEL), jnp.float32),
        'p': jax.random.normal(ks[1], (DEPTH, BATCH, SEQ, D_PLE), jnp.float32),
        'positions': jnp.tile(jnp.arange(SEQ, dtype=jnp.int32)[None, :], (BATCH, 1)),
        'mla_w_dqkv': w(ks[2], (NM, D_MODEL, MLA_Q_LORA + MLA_KV_LORA + MLA_ROPE), D_MODEL),
        'mla_q_norm': gain(ks[3], (NM, MLA_Q_LORA)),
        'mla_kv_norm': gain(ks[4], (NM, MLA_KV_LORA)),
        'mla_w_uq': w(ks[5], (NM, MLA_Q_LORA, MLA_HEADS * (MLA_NOPE + MLA_ROPE)), MLA_Q_LORA),
        'mla_w_ukv': w(ks[6], (NM, MLA_KV_LORA, MLA_HEADS * (MLA_NOPE + MLA_V)), MLA_KV_LORA),
        'mla_w_o': w(ks[7], (NM, MLA_HEADS * MLA_V, D_MODEL), MLA_HEADS * MLA_V, DEEPNORM_BETA),
        'hgrn_w_in': w(ks[8], (NH, D_MODEL, 2 * HGRN_HEADS * HGRN_DK + 2 * HGRN_HEADS * HGRN_DV), D_MODEL),
        'hgrn_lb_logits': 0.1 * jax.random.normal(ks[9], (DEPTH, HGRN_HEADS * HGRN_DK), jnp.float32),
        'hgrn_out_norm': gain(ks[10], (NH, HGRN_HEADS * HGRN_DV)),
        'hgrn_w_o': w(ks[11], (NH, HGRN_HEADS * HGRN_DV, D_MODEL), HGRN_HEADS * HGRN_DV, DEEPNORM_BETA),
        'ffn_w_in': w(ks[12], (DEPTH, D_MODEL, 2 * D_FF), D_MODEL),
        'ffn_w_down': w(ks[13], (DEPTH, D_FF, D_MODEL), D_FF, DEEPNORM_BETA),
        'ln_mix_g': gain(ks[14], (DEPTH, D_MODEL)),
        'ln_mix_b': bias(ks[15], (DEPTH, D_MODEL)),
        'ln_ffn_g': gain(ks[16], (DEPTH, D_MODEL)),
        'ln_ffn_b': bias(ks[17], (DEPTH, D_MODEL)),
        'ple_w_proj': w(ks[18], (DEPTH, D_PLE, D_MODEL), D_PLE),
        'ple_w_gate': w(ks[19], (DEPTH, D_MODEL, D_MODEL), D_MODEL),
    }


def reference(x, p, positions, mla_w_dqkv, mla_q_norm, mla_kv_norm, mla_w_uq, mla_w_ukv, mla_w_o,
              hgrn_w_in, hgrn_lb_logits, hgrn_out_norm, hgrn_w_o, ffn_w_in, ffn_w_down,
              ln_mix_g, ln_mix_b, ln_ffn_g, ln_ffn_b, ple_w_proj, ple_w_gate):
    cos, sin = rope_tables(positions)
    lb_soft = jax.nn.softmax(hgrn_lb_logits.astype(jnp.float32), axis=0)
    lower_bounds = jnp.cumsum(lb_soft, axis=0) - lb_soft[0:1]

    for i in range(DEPTH):
        j = i // N_MIXERS
        if i % N_MIXERS == 0:
            h = mla_mixer(x, cos, sin, mla_w_dqkv[j], mla_q_norm[j], mla_kv_norm[j],
                          mla_w_uq[j], mla_w_ukv[j], mla_w_o[j])
        else:
            h = hgrn2_mixer(x, lower_bounds[i], hgrn_w_in[j], hgrn_out_norm[j], hgrn_w_o[j])
        x = layer_norm(DEEPNORM_ALPHA * x + h, ln_mix_g[i], ln_mix_b[i])
        x = layer_norm(DEEPNORM_ALPHA * x + swiglu_ffn(x, ffn_w_in[i], ffn_w_down[i]), ln_ffn_g[i], ln_ffn_b[i])
        x = x + jax.nn.sigmoid(x @ ple_w_gate[i]) * (p[i] @ ple_w_proj[i])
    return x
```

```python
import math
from contextlib import ExitStack

import numpy as np
import concourse.bass as bass
import concourse.mybir as mybir
from concourse.bass_utils import run_bass_kernel_spmd

F32 = mybir.dt.float32
BF16 = mybir.dt.bfloat16
I32 = mybir.dt.int32
AF = mybir.ActivationFunctionType
ALU = mybir.AluOpType

D = 1024
DFF = 2816
ALPHA = 4.0 ** 0.25
LN_EPS = 1e-5
RMS_EPS = 1e-6
TT = 512


class Buf:
    __slots__ = ("w", "r")

    def __init__(self):
        self.w = None
        self.r = {}


class T:
    def __init__(self, t, excl=False):
        self.t = t
        self.b = Buf()
        self.excl = excl

    def __getitem__(self, k):
        return self.t[k]


class Sched:
    def __init__(self, nc, es):
        self.nc = nc
        self.eng = {"pe": nc.tensor, "act": nc.scalar, "dve": nc.vector, "pool": nc.gpsimd, "sp": nc.sync}
        self.sems = {}
        self.cnt = {}
        self.seen = {k: {} for k in self.eng}
        for k in self.eng:
            self.sems[k] = es.enter_context(nc.semaphore("s_" + k))
            self.cnt[k] = 0
        self.ring = {}
        for q, n in (("sp", 8), ("pool", 8), ("act", 4)):
            lst = []
            for i in range(n):
                key = (q, i)
                self.sems[key] = es.enter_context(nc.semaphore("d_%s%d" % (q, i)))
                self.cnt[key] = 0
                lst.append(key)
            self.ring[q] = [lst, 0]
        self.nops = 0

    def _need(self, e, reads, writes):
        need = {}

        def add(ev, raw):
            if ev is None:
                return
            key, val = ev
            if key == e and e == "pe":
                return
            if need.get(key, 0) < val:
                need[key] = val

        for b in reads:
            add(b.b.w, True)
            if b.excl:
                for k, v in b.b.r.items():
                    if k != e:
                        add((k, v), False)
        for b in writes:
            add(b.b.w, False)
            for k, v in b.b.r.items():
                add((k, v), False)
        seen = self.seen[e]
        for key, val in need.items():
            assert val <= self.cnt[key], ("wait on a milestone not yet emitted", e, key, val, self.cnt[key])
            if seen.get(key, 0) < val:
                self.eng[e].wait_ge(self.sems[key], val)
                seen[key] = val

    def _record(self, ev, reads, writes):
        key, val = ev
        for b in reads:
            b.b.r[key] = val
        for b in writes:
            b.b.w = ev
            b.b.r = {}

    def op(self, e, fn, r=(), w=(), ms=True):
        self._need(e, r, w)
        inst = fn()
        val = self.cnt[e] + 1
        if ms:
            inst.then_inc(self.sems[e], 1)
            self.cnt[e] = val
        self._record((e, val), r, w)
        self.nops += 1
        return inst

    def dma(self, q, out, in_, r=(), w=()):
        self._need(q, r, w)
        lst, i = self.ring[q]
        key = lst[i % len(lst)]
        self.ring[q][1] = i + 1
        prior = self.cnt[key]
        if prior > 0 and self.seen[q].get(key, 0) < prior:
            self.eng[q].wait_ge(self.sems[key], prior)
            self.seen[q][key] = prior
        inst = self.eng[q].dma_start(out=out, in_=in_)
        inst.then_inc(self.sems[key], 16)
        self.cnt[key] = prior + 16
        self._record((key, prior + 16), r, w)
        self.nops += 1
        return inst

    def barrier(self):
        for e in self.eng:
            for key, val in self.cnt.items():
                if key == e or val == 0:
                    continue
                if self.seen[e].get(key, 0) < val:
                    self.eng[e].wait_ge(self.sems[key], val)
                    self.seen[e][key] = val


def build(S=8192, upto=99, debug=False):
    nc = bass.Bass("TRN2", target_bir_lowering=False)
    NT = S // TT
    NB = S // 128
    dbg_kind = "ExternalOutput" if debug else "Internal"

    def dram(name, shape, dt, kind=None):
        return nc.dram_tensor(name, list(shape), dt, kind=kind or dbg_kind).ap()

    def din(name, shape, dt=F32):
        return dram(name, shape, dt, "ExternalInput")

    x = din("x", [S, D])
    p_in = din("p", [2, S, 256])
    pos = din("pos", [1, S], I32)
    consts = din("consts", [128, 1408])
    pvec = din("pvec", [128, 96])
    w_dqkv = din("mla_w_dqkv", [D, 576])
    w_uq = din("mla_w_uq", [256, 1536])
    w_ukv = din("mla_w_ukv", [256, 2048])
    mla_w_o = din("mla_w_o", [D, D])
    hg_w_in = din("hgrn_w_in", [D, 4096])
    hg_w_o = din("hgrn_w_o", [D, D])
    ffn_w_in = din("ffn_w_in", [2, D, 2 * DFF])
    ffn_w_dn = din("ffn_w_down", [2, DFF, D])
    ple_w_proj = din("ple_w_proj", [2, 256, D])
    ple_w_gate = din("ple_w_gate", [2, D, D])
    out = dram("out", [S, D], F32, "ExternalOutput")

    sc_cc = T(dram("sc_cc", [64, S], F32))
    sc_ss = T(dram("sc_ss", [64, S], F32))
    sc_xT = T(dram("sc_xT", [D, S], F32))
    sc_x1 = T(dram("sc_x1", [D, S], F32))
    sc_qn = T(dram("sc_qn", [8, 128, S], BF16))
    sc_qr = T(dram("sc_qr", [8, 64, S], BF16))
    sc_kn = T(dram("sc_kn", [8, 128, S], BF16))
    sc_kr = T(dram("sc_kr", [64, S], BF16))
    sc_v = T(dram("sc_v", [S, D], BF16))
    sc_o = T(dram("sc_o", [D, S], BF16))
    sc_hqa = T(dram("sc_hqa", [8, 128, S], BF16))
    sc_hqs = T(dram("sc_hqs", [8, 128, S], BF16))
    sc_hka = T(dram("sc_hka", [8, 128, S], BF16))
    sc_hkh = T(dram("sc_hkh", [8, 128, S], BF16))
    sc_hg = T(dram("sc_hg", [8, 128, S], BF16))
    sc_hv = T(dram("sc_hv", [S, D], BF16))
    sc_hd = T(dram("sc_hd", [128, 8, S // 128], F32))
    dbg_og = T(dram("dbg_og", [D, S], BF16)) if debug else None

    es = ExitStack()
    with es:
        sch = Sched(nc, es)
        PE = lambda fn, r=(), w=(), ms=True: sch.op("pe", fn, r, w, ms)
        ACT = lambda fn, r=(), w=(): sch.op("act", fn, r, w)
        DVE = lambda fn, r=(), w=(): sch.op("dve", fn, r, w)
        POOL = lambda fn, r=(), w=(): sch.op("pool", fn, r, w)

        uniq = [0]

        def sb(es_, name, shape, dt=F32):
            uniq[0] += 1
            return T(es_.enter_context(nc.sbuf_tensor("%s_%d" % (name, uniq[0]), list(shape), dt)))

        banks = [T(es.enter_context(nc.psum_tensor("ps%d" % i, [128, 512], F32)), excl=True) for i in range(8)]
        bank_i = [0]

        def nextps(n=8):
            b = banks[bank_i[0] % n]
            bank_i[0] += 1
            return b

        cst = sb(es, "cst", [128, 1408])
        sch.dma("sp", cst[:], consts[:, :], w=[cst])
        ident_f = cst[:, 0:128]
        ident_b = sb(es, "ident_b", [128, 128], BF16)
        maskb = sb(es, "maskb", [128, 128], BF16)
        amask = sb(es, "amask", [64, 512], BF16)
        ones_b = sb(es, "ones_b", [128, 128], BF16)
        DVE(lambda: nc.vector.tensor_copy(out=ident_b[:], in_=cst[:, 0:128]), r=[cst], w=[ident_b])
        DVE(lambda: nc.vector.tensor_copy(out=maskb[:], in_=cst[:, 128:256]), r=[cst], w=[maskb])
        DVE(lambda: nc.vector.tensor_copy(out=amask[:], in_=cst[0:64, 256:768]), r=[cst], w=[amask])
        DVE(lambda: nc.vector.memset(ones_b[:], 1.0), w=[ones_b])
        rmask = cst[:, 768:1280]
        pv = sb(es, "pv", [128, 96])
        sch.dma("sp", pv[:], pvec[:, :], w=[pv])
        c_qn, c_kvn = 0, 2
        c_lng = [[4, 12], [20, 28]]
        c_lnb = [[36, 44], [52, 60]]
        c_on, c_l0, c_l1 = 68, 76, 84
        lbv = sb(es, "lbv", [128, 24])
        DVE(lambda: nc.vector.tensor_tensor(out=lbv[:, 16:24], in0=pv[:, c_l1:c_l1 + 8], in1=pv[:, c_l0:c_l0 + 8],
                                            op=ALU.subtract), r=[pv], w=[lbv])
        ACT(lambda: nc.scalar.activation(out=lbv[:, 0:8], in_=lbv[:, 16:24], func=AF.Sigmoid), r=[lbv], w=[lbv])
        DVE(lambda: nc.vector.tensor_scalar(out=lbv[:, 8:16], in0=lbv[:, 0:8], scalar1=-1.0, scalar2=1.0,
                                            op0=ALU.mult, op1=ALU.add), r=[lbv], w=[lbv])

        TWO_PI = 2.0 * math.pi
        C1 = 6.28125
        C2 = TWO_PI - C1
        with ExitStack() as e1:
            RC = min(S, 2048)
            pi_ = sb(e1, "r_pi", [64, RC], I32)
            pf = sb(e1, "r_pf", [64, RC])
            for tab in range(2):
                kf = sb(e1, "r_kf%d" % tab, [64, RC])
                ki = sb(e1, "r_ki%d" % tab, [64, RC], I32)
                ang = sb(e1, "r_ang%d" % tab, [64, RC])
                tm = sb(e1, "r_tm%d" % tab, [64, RC])
                res = sb(e1, "r_res%d" % tab, [64, RC])
                for c0 in range(0, S, RC):
                    if tab == 0:
                        pass
                    src = bass.AP(pos.tensor, c0, [[0, 64], [1, RC]])
                    sch.dma("sp", pi_[:], src, w=[pi_])
                    DVE(lambda: nc.vector.tensor_copy(out=pf[:], in_=pi_[:]), r=[pi_], w=[pf])
                    fcol = cst[0:64, 1280 + tab:1281 + tab]
                    DVE(lambda: nc.vector.tensor_scalar(out=ang[:], in0=pf[:], scalar1=fcol, scalar2=None,
                                                        op0=ALU.mult), r=[pf, cst], w=[ang])
                    DVE(lambda: nc.vector.tensor_scalar(out=kf[:], in0=ang[:], scalar1=1.0 / TWO_PI, scalar2=None,
                                                        op0=ALU.mult), r=[ang], w=[kf])
                    DVE(lambda: nc.vector.tensor_copy(out=ki[:], in_=kf[:]), r=[kf], w=[ki])
                    DVE(lambda: nc.vector.tensor_copy(out=kf[:], in_=ki[:]), r=[ki], w=[kf])
                    DVE(lambda: nc.vector.scalar_tensor_tensor(out=tm[:], in0=kf[:], scalar=-C1, in1=ang[:],
                                                               op0=ALU.mult, op1=ALU.add), r=[kf, ang], w=[tm])
                    DVE(lambda: nc.vector.scalar_tensor_tensor(out=ang[:], in0=kf[:], scalar=-C2, in1=tm[:],
                                                               op0=ALU.mult, op1=ALU.add), r=[kf, tm], w=[ang])
                    if tab == 0:
                        DVE(lambda: nc.vector.tensor_scalar(out=ang[:], in0=ang[:], scalar1=math.pi / 2, scalar2=None,
                                                            op0=ALU.add), r=[ang], w=[ang])
                    for _ in range(2):
                        DVE(lambda: nc.vector.tensor_scalar(out=tm[:], in0=ang[:], scalar1=math.pi, scalar2=-TWO_PI,
                                                            op0=ALU.is_gt, op1=ALU.mult), r=[ang], w=[tm])
                        DVE(lambda: nc.vector.tensor_tensor(out=ang[:], in0=ang[:], in1=tm[:], op=ALU.add),
                            r=[ang, tm], w=[ang])
                        DVE(lambda: nc.vector.tensor_scalar(out=tm[:], in0=ang[:], scalar1=-math.pi, scalar2=TWO_PI,
                                                            op0=ALU.is_lt, op1=ALU.mult), r=[ang], w=[tm])
                        DVE(lambda: nc.vector.tensor_tensor(out=ang[:], in0=ang[:], in1=tm[:], op=ALU.add),
                            r=[ang, tm], w=[ang])
                    DVE(lambda: nc.vector.tensor_scalar(out=ang[:], in0=ang[:], scalar1=3.14159, scalar2=-3.14159,
                                                        op0=ALU.min, op1=ALU.max), r=[ang], w=[ang])
                    ACT(lambda: nc.scalar.activation(out=res[:], in_=ang[:], func=AF.Sin), r=[ang], w=[res])
                    dst = sc_cc if tab == 0 else sc_ss
                    sch.dma("sp", dst[:, c0:c0 + RC], res[:], r=[res], w=[dst])
            sch.barrier()

        def load_w(q, dst_tile, dst_ap, src_ap):
            sch.dma(q, dst_ap, src_ap, w=[dst_tile])

        if upto >= 1:
            with ExitStack() as e1:
                Wd = sb(e1, "Wd", [128, 8, 640], BF16)
                Wuq = sb(e1, "Wuq", [128, 2, 8, 256], BF16)
                Wkv = sb(e1, "Wkv", [128, 2, 2, 8, 128], BF16)
                wdv = w_dqkv.rearrange("(c p) n -> p c n", p=128)
                load_w("pool", Wd, Wd[:, :, 0:576], wdv)
                load_w("pool", Wd, Wd[:, :, 576:608], wdv[:, :, 544:576])
                load_w("pool", Wd, Wd[:, :, 608:640], wdv[:, :, 512:544])
                for kc in range(2):
                    wv = w_uq[kc * 128:(kc + 1) * 128, :].rearrange("p (h d) -> p h d", h=8)
                    load_w("pool", Wuq, Wuq[:, kc, :, 0:192], wv)
                    load_w("pool", Wuq, Wuq[:, kc, :, 192:224], wv[:, :, 160:192])
                    load_w("pool", Wuq, Wuq[:, kc, :, 224:256], wv[:, :, 128:160])
                    wk = w_ukv[kc * 128:(kc + 1) * 128, :].rearrange("p (h kv d) -> p kv h d", h=8, kv=2)
                    for kv in range(2):
                        load_w("pool", Wkv, Wkv[:, kc, kv, :, :], wk[:, kv, :, :])
                xtok = [sb(e1, "xtok%d" % i, [128, 4, D]) for i in range(2)]
                xT32 = [sb(e1, "xT32_%d" % i, [128, 8, TT]) for i in range(2)]
                xTb = sb(e1, "xTb", [128, 8, TT], BF16)
                sq = sb(e1, "sq", [128, 4, TT], BF16)
                lnt = sb(e1, "lnt", [128, 2, TT])
                rstd = sb(e1, "rstd", [128, 2, TT])
                cl = sb(e1, "cl", [128, 4, TT], BF16)
                cct = [sb(e1, "cct%d" % i, [64, TT]) for i in range(2)]
                sst = [sb(e1, "sst%d" % i, [64, TT]) for i in range(2)]
                t1 = sb(e1, "t1", [64, TT])
                t2 = sb(e1, "t2", [64, TT])
                krb = sb(e1, "krb", [64, TT], BF16)
                Qn = sb(e1, "Qn", [128, 8, TT], BF16)
                Qr = sb(e1, "Qr", [64, 8, TT], BF16)
                Kn = sb(e1, "Kn", [128, 8, TT], BF16)
                Vt = sb(e1, "Vt", [128, 4, D], BF16)

                def load_x(it):
                    t0 = it * TT
                    sch.dma("sp", xtok[it % 2][:], x[t0:t0 + TT, :].rearrange("(n p) d -> p n d", p=128),
                            w=[xtok[it % 2]])
                    sch.dma("sp", cct[it % 2][:], sc_cc[:, t0:t0 + TT], r=[sc_cc], w=[cct[it % 2]])
                    sch.dma("sp", sst[it % 2][:], sc_ss[:, t0:t0 + TT], r=[sc_ss], w=[sst[it % 2]])

                import os as _os
                _kstop = int(_os.environ.get("KSTOP", "99"))
                load_x(0)
                for it in range(NT if _kstop > 0 else 0):
                    t0 = it * TT
                    if it + 1 < NT:
                        load_x(it + 1)
                    xt = xtok[it % 2]
                    x32 = xT32[it % 2]
                    CCt = cct[it % 2]
                    SSt = sst[it % 2]
                    for c in range(8):
                        bk = nextps()
                        for n in range(4):
                            PE(lambda: nc.tensor.transpose(out=bk[:, n * 128:(n + 1) * 128],
                                                           in_=xt[:, n, c * 128:(c + 1) * 128], identity=ident_f),
                               r=[xt, cst], w=[bk])
                        ACT(lambda: nc.scalar.copy(out=x32[:, c, :], in_=bk[:]), r=[bk], w=[x32])
                        DVE(lambda: nc.vector.tensor_copy(out=xTb[:, c, :], in_=x32[:, c, :]), r=[x32], w=[xTb])
                    sch.dma("sp", sc_xT[:, t0:t0 + TT].rearrange("(c p) t -> p c t", p=128), x32[:], r=[x32], w=[sc_xT])
                    if _kstop <= 1:
                        continue
                    dps = []
                    for oc in range(4):
                        bk = nextps()
                        for kc in range(8):
                            PE(lambda: nc.tensor.matmul(bk[:], lhsT=Wd[:, kc, oc * 128:(oc + 1) * 128], rhs=xTb[:, kc, :],
                                                        start=(kc == 0), stop=(kc == 7)), r=[Wd, xTb], w=[bk])
                        ACT(lambda: nc.scalar.activation(out=sq[:, oc, :], in_=bk[:], func=AF.Square), r=[bk], w=[sq])
                        dps.append(bk)
                    for g in range(2):
                        bs = nextps()
                        for j in range(2):
                            PE(lambda: nc.tensor.matmul(bs[:], lhsT=ones_b[:], rhs=sq[:, 2 * g + j, :],
                                                        start=(j == 0), stop=(j == 1)), r=[ones_b, sq], w=[bs])
                        ACT(lambda: nc.scalar.activation(out=lnt[:, g, :], in_=bs[:], func=AF.Ln, scale=1.0 / 256.0,
                                                         bias=RMS_EPS), r=[bs], w=[lnt])
                        ACT(lambda: nc.scalar.activation(out=rstd[:, g, :], in_=lnt[:, g, :], func=AF.Exp, scale=-0.5),
                            r=[lnt], w=[rstd])
                        for j in range(2):
                            col = (c_qn if g == 0 else c_kvn) + j
                            DVE(lambda: nc.vector.scalar_tensor_tensor(out=cl[:, 2 * g + j, :], in0=dps[2 * g + j][:],
                                                                       scalar=pv[:, col:col + 1], in1=rstd[:, g, :],
                                                                       op0=ALU.mult, op1=ALU.mult),
                                r=[dps[2 * g + j], pv, rstd], w=[cl])

                    def rope(b1, b2, dst_ap, dst_t):
                        DVE(lambda: nc.vector.tensor_tensor(out=t1[:], in0=b1[0:64, :], in1=CCt[:], op=ALU.mult),
                            r=[b1, CCt], w=[t1])
                        DVE(lambda: nc.vector.tensor_tensor(out=t2[:], in0=b2[0:64, :], in1=SSt[:], op=ALU.mult),
                            r=[b2, SSt], w=[t2])
                        DVE(lambda: nc.vector.tensor_tensor(out=dst_ap, in0=t1[:], in1=t2[:], op=ALU.add),
                            r=[t1, t2], w=[dst_t])

                    if _kstop <= 2:
                        continue
                    b1 = nextps()
                    b2 = nextps()
                    for bk, c0 in ((b1, 512), (b2, 576)):
                        for kc in range(8):
                            PE(lambda: nc.tensor.matmul(bk[0:64, :], lhsT=Wd[:, kc, c0:c0 + 64], rhs=xTb[:, kc, :],
                                                        start=(kc == 0), stop=(kc == 7)), r=[Wd, xTb], w=[bk])
                    rope(b1, b2, krb[:], krb)
                    sch.dma("sp", sc_kr[:, t0:t0 + TT], krb[:], r=[krb], w=[sc_kr])
                    if _kstop <= 3:
                        continue
                    for h in range(8):
                        bn = nextps()
                        b1 = nextps()
                        b2 = nextps()
                        for kc in range(2):
                            PE(lambda: nc.tensor.matmul(bn[:], lhsT=Wuq[:, kc, h, 0:128], rhs=cl[:, kc, :],
                                                        start=(kc == 0), stop=(kc == 1)), r=[Wuq, cl], w=[bn])
                        for bk, c0 in ((b1, 128), (b2, 192)):
                            for kc in range(2):
                                PE(lambda: nc.tensor.matmul(bk[0:64, :], lhsT=Wuq[:, kc, h, c0:c0 + 64], rhs=cl[:, kc, :],
                                                            start=(kc == 0), stop=(kc == 1)), r=[Wuq, cl], w=[bk])
                        ACT(lambda: nc.scalar.copy(out=Qn[:, h, :], in_=bn[:]), r=[bn], w=[Qn])
                        rope(b1, b2, Qr[:, h, :], Qr)
                    sch.dma("sp", sc_qn[:, :, t0:t0 + TT].rearrange("h p t -> p h t"), Qn[:], r=[Qn], w=[sc_qn])
                    sch.dma("sp", sc_qr[:, :, t0:t0 + TT].rearrange("h p t -> p h t"), Qr[:], r=[Qr], w=[sc_qr])
                    if _kstop <= 4:
                        continue
                    for h in range(8):
                        bk = nextps()
                        for kc in range(2):
                            PE(lambda: nc.tensor.matmul(bk[:], lhsT=Wkv[:, kc, 0, h, :], rhs=cl[:, 2 + kc, :],
                                                        start=(kc == 0), stop=(kc == 1)), r=[Wkv, cl], w=[bk])
                        ACT(lambda: nc.scalar.copy(out=Kn[:, h, :], in_=bk[:]), r=[bk], w=[Kn])
                    sch.dma("sp", sc_kn[:, :, t0:t0 + TT].rearrange("h p t -> p h t"), Kn[:], r=[Kn], w=[sc_kn])
                    for n in range(4):
                        for hf in range(2):
                            bk = nextps()
                            for kc in range(2):
                                PE(lambda: nc.tensor.matmul(bk[:], lhsT=cl[:, 2 + kc, n * 128:(n + 1) * 128],
                                                            rhs=Wkv[:, kc, 1, 4 * hf:4 * hf + 4, :],
                                                            start=(kc == 0), stop=(kc == 1)), r=[Wkv, cl], w=[bk])
                            DVE(lambda: nc.vector.tensor_copy(out=Vt[:, n, hf * 512:(hf + 1) * 512], in_=bk[:]),
                                r=[bk], w=[Vt])
                    sch.dma("sp", sc_v[t0:t0 + TT, :].rearrange("(n p) d -> p n d", p=128), Vt[:], r=[Vt], w=[sc_v])
                sch.barrier()

        if upto >= 2:
            SCALE = 192.0 ** -0.5
            with ExitStack() as e1:
                Kr = sb(e1, "Kr", [64, S], BF16)
                Knh = [sb(e1, "Knh%d" % i, [128, S], BF16) for i in range(2)]
                Vh = [sb(e1, "Vh%d" % i, [128, NB, 128], BF16) for i in range(2)]
                Qnb = [sb(e1, "Qnb%d" % i, [128, TT], BF16) for i in range(2)]
                Qrb = [sb(e1, "Qrb%d" % i, [64, TT], BF16) for i in range(2)]
                NPT = 4
                pt = [sb(e1, "pt%d" % i, [128, TT], BF16) for i in range(NPT)]
                pacc = [[sb(e1, "pacc", [128, TT]) for _ in range(2)] for _ in range(2)]
                dsum = [sb(e1, "dsum", [128, TT]) for _ in range(2)]
                ones_f = sb(e1, "ones_f", [128, 128])
                DVE(lambda: nc.vector.memset(ones_f[:], 1.0), w=[ones_f])
                rcp = [sb(e1, "rcp%d" % i, [128, TT]) for i in range(2)]
                ob = [sb(e1, "ob%d" % i, [128, TT], BF16) for i in range(2)]
                sbank = banks[0:3]
                accs = [(banks[3], banks[4]), (banks[5], banks[6])]
                sch.dma("sp", Kr[:], sc_kr[:, :], r=[sc_kr], w=[Kr])

                def load_head(h):
                    sch.dma("sp", Knh[h % 2][:], sc_kn[h, :, :], r=[sc_kn], w=[Knh[h % 2]])
                    sch.dma("sp", Vh[h % 2][:], sc_v[:, h * 128:(h + 1) * 128].rearrange("(n p) d -> p n d", p=128),
                            r=[sc_v], w=[Vh[h % 2]])

                def load_q(h, qb, i):
                    sch.dma("sp", Qnb[i % 2][:], sc_qn[h, :, qb * TT:(qb + 1) * TT], r=[sc_qn], w=[Qnb[i % 2]])
                    sch.dma("sp", Qrb[i % 2][:], sc_qr[h, :, qb * TT:(qb + 1) * TT], r=[sc_qr], w=[Qrb[i % 2]])

                load_head(0)
                load_q(0, 0, 0)
                blk = 0
                ti = 0
                for h in range(8):
                    if h + 1 < 8:
                        load_head(h + 1)
                    K_ = Knh[h % 2]
                    V_ = Vh[h % 2]
                    for qb in range(NT):
                        nh, nq = (h, qb + 1) if qb + 1 < NT else (h + 1, 0)
                        if nh < 8:
                            load_q(nh, nq, blk + 1)
                        Qn_ = Qnb[blk % 2]
                        Qr_ = Qrb[blk % 2]
                        acc_o, acc_d = accs[blk % 2]
                        nk = 4 * qb + 4
                        pa = pacc[blk % 2]
                        DVE(lambda: nc.vector.memset(pa[0][:], 0.0), w=[pa[0]])
                        POOL(lambda: nc.gpsimd.memset(pa[1][:], 0.0), w=[pa[1]])

                        def qk(kb, sp_):
                            j = kb - 4 * qb
                            c0 = max(j, 0) * 128
                            ks = slice(kb * 128, (kb + 1) * 128)
                            PE(lambda: nc.tensor.matmul(sp_[:, c0:TT], lhsT=K_[:, ks], rhs=Qn_[:, c0:TT],
                                                        start=True, stop=False), r=[K_, Qn_], w=[sp_], ms=False)
                            PE(lambda: nc.tensor.matmul(sp_[:, c0:TT], lhsT=Kr[:, ks], rhs=Qr_[:, c0:TT],
                                                        start=False, stop=(j < 0)), r=[Kr, Qr_], w=[sp_], ms=(j < 0))
                            if j >= 0:
                                PE(lambda: nc.tensor.matmul(sp_[:, c0:c0 + 128], lhsT=ident_b[:], rhs=maskb[:],
                                                            start=False, stop=True), r=[ident_b, maskb], w=[sp_])
                            return c0

                        c0s = {}
                        c0s[0] = qk(0, sbank[ti % 3])
                        for kb in range(nk):
                            sp_ = sbank[(ti + kb) % 3]
                            p_ = pt[(ti + kb) % NPT]
                            c0 = c0s[kb]
                            ACT(lambda: nc.scalar.activation(out=p_[:, c0:TT], in_=sp_[:, c0:TT], func=AF.Exp,
                                                             scale=SCALE), r=[sp_], w=[p_])
                            if kb + 1 < nk:
                                c0s[kb + 1] = qk(kb + 1, sbank[(ti + kb + 1) % 3])
                            PE(lambda: nc.tensor.matmul(acc_o[:, c0:TT], lhsT=V_[:, kb, :], rhs=p_[:, c0:TT],
                                                        start=(kb == 0), stop=(kb == nk - 1)), r=[V_, p_], w=[acc_o])
                            if kb % 2 == 0:
                                DVE(lambda: nc.vector.tensor_tensor(out=pa[0][:, c0:TT], in0=pa[0][:, c0:TT], in1=p_[:, c0:TT],
                                                                    op=ALU.add), r=[pa[0], p_], w=[pa[0]])
                            else:
                                POOL(lambda: nc.gpsimd.tensor_tensor(out=pa[1][:, c0:TT], in0=pa[1][:, c0:TT], in1=p_[:, c0:TT],
                                                                     op=ALU.add), r=[pa[1], p_], w=[pa[1]])
                        ti += nk
                        rc = rcp[blk % 2]
                        o_ = ob[blk % 2]
                        ds_ = dsum[blk % 2]
                        DVE(lambda: nc.vector.tensor_tensor(out=ds_[:], in0=pa[0][:], in1=pa[1][:], op=ALU.add),
                            r=[pa[0], pa[1]], w=[ds_])
                        PE(lambda: nc.tensor.matmul(acc_d[:], lhsT=ones_f[:], rhs=ds_[:], start=True, stop=True),
                           r=[ones_f, ds_], w=[acc_d])
                        DVE(lambda: nc.vector.reciprocal(out=rc[:], in_=acc_d[:]), r=[acc_d], w=[rc])
                        DVE(lambda: nc.vector.tensor_tensor(out=o_[:], in0=acc_o[:], in1=rc[:], op=ALU.mult),
                            r=[acc_o, rc], w=[o_])
                        sch.dma("sp", sc_o[h * 128:(h + 1) * 128, qb * TT:(qb + 1) * TT], o_[:], r=[o_], w=[sc_o])
                        blk += 1
                sch.barrier()


        def ln_tile(y, scr, scr_t, st, tmp, gcol, bcol, nb=8):
            mean, msq, lnv = st
            for c in range(8):
                DVE(lambda: nc.vector.tensor_copy(out=scr[:, c, :], in_=y[:, c, :]), r=[y], w=[scr_t])
                ACT(lambda: nc.scalar.activation(out=scr[:, 8 + c, :], in_=y[:, c, :], func=AF.Square), r=[y], w=[scr_t])
            s1 = nextps(nb)
            s2 = nextps(nb)
            for c in range(8):
                PE(lambda: nc.tensor.matmul(s1[:], lhsT=ones_b[:], rhs=scr[:, c, :], start=(c == 0), stop=(c == 7)),
                   r=[ones_b, scr_t], w=[s1], ms=(c == 7))
            for c in range(8):
                PE(lambda: nc.tensor.matmul(s2[:], lhsT=ones_b[:], rhs=scr[:, 8 + c, :], start=(c == 0), stop=(c == 7)),
                   r=[ones_b, scr_t], w=[s2], ms=(c == 7))
            ACT(lambda: nc.scalar.activation(out=mean[:], in_=s1[:], func=AF.Copy, scale=1.0 / D), r=[s1], w=[mean])
            DVE(lambda: nc.vector.tensor_tensor(out=msq[:], in0=mean[:], in1=mean[:], op=ALU.mult), r=[mean], w=[msq])
            DVE(lambda: nc.vector.scalar_tensor_tensor(out=msq[:], in0=s2[:], scalar=1.0 / D, in1=msq[:],
                                                       op0=ALU.mult, op1=ALU.subtract), r=[s2, msq], w=[msq])
            ACT(lambda: nc.scalar.activation(out=lnv[:], in_=msq[:], func=AF.Ln, bias=LN_EPS), r=[msq], w=[lnv])
            ACT(lambda: nc.scalar.activation(out=lnv[:], in_=lnv[:], func=AF.Exp, scale=-0.5), r=[lnv], w=[lnv])
            for c in range(8):
                DVE(lambda: nc.vector.tensor_tensor(out=tmp[:], in0=y[:, c, :], in1=mean[:], op=ALU.subtract),
                    r=[y, mean], w=[tmp])
                DVE(lambda: nc.vector.tensor_tensor(out=tmp[:], in0=tmp[:], in1=lnv[:], op=ALU.mult),
                    r=[tmp, lnv], w=[tmp])
                ACT(lambda: nc.scalar.activation(out=y[:, c, :], in_=tmp[:], func=AF.Identity,
                                                 scale=pv[:, gcol + c:gcol + c + 1], bias=pv[:, bcol + c:bcol + c + 1]),
                    r=[tmp, pv], w=[y])

        def fm(ap2d, t0):
            return ap2d[:, t0:t0 + TT].rearrange("(c p) t -> p c t", p=128)

        def hm(ap3d, t0):
            return ap3d[:, :, t0:t0 + TT].rearrange("h p t -> p h t")

        def wload(dst, src2d, nkc):
            v = src2d.rearrange("(c p) n -> p c n", p=128)
            for kc in range(nkc):
                sch.dma("pool", dst[:, kc, :], v[:, kc, :], w=[dst])

        def stage_proj_ln(w_o_ap, src_o, src_x, dst, gcol, bcol):
            with ExitStack() as e1:
                Wo = sb(e1, "Wo", [128, 8, D], BF16)
                wload(Wo, w_o_ap, 8)
                oT = [sb(e1, "oT%d" % i, [128, 8, TT], BF16) for i in range(2)]
                yy = [sb(e1, "yy%d" % i, [128, 8, TT]) for i in range(2)]
                scr = sb(e1, "scr", [128, 16, TT], BF16)
                st = [sb(e1, "st%d" % i, [128, TT]) for i in range(3)]
                tmp = sb(e1, "tmp", [128, TT])

                def ld(it):
                    sch.dma("sp", oT[it % 2][:], fm(src_o.t, it * TT), r=[src_o], w=[oT[it % 2]])
                    sch.dma("sp", yy[it % 2][:], fm(src_x.t, it * TT), r=[src_x], w=[yy[it % 2]])

                ld(0)
                for it in range(NT):
                    if it + 1 < NT:
                        ld(it + 1)
                    o_ = oT[it % 2]
                    y = yy[it % 2]
                    for oc in range(8):
                        bk = nextps()
                        for kc in range(8):
                            PE(lambda: nc.tensor.matmul(bk[:], lhsT=Wo[:, kc, oc * 128:(oc + 1) * 128], rhs=o_[:, kc, :],
                                                        start=(kc == 0), stop=(kc == 7)), r=[Wo, o_], w=[bk], ms=(kc == 7))
                        DVE(lambda: nc.vector.scalar_tensor_tensor(out=y[:, oc, :], in0=y[:, oc, :], scalar=ALPHA, in1=bk[:],
                                                                   op0=ALU.mult, op1=ALU.add), r=[y, bk], w=[y])
                    ln_tile(y, scr, scr, st, tmp, gcol, bcol)
                    sch.dma("sp", fm(dst.t, it * TT), y[:], r=[y], w=[dst])
                sch.barrier()

        def stage_ffn(li, src, dst, gcol, bcol):
            with ExitStack() as e1:
                Win = sb(e1, "Win", [128, 8, 2 * DFF], BF16)
                Wdn = sb(e1, "Wdn", [128, 22, D], BF16)
                wload(Win, ffn_w_in[li], 8)
                wload(Wdn, ffn_w_dn[li], 22)
                y = sb(e1, "y", [128, 8, TT])
                xb = sb(e1, "xb", [128, 8, TT], BF16)
                hh = sb(e1, "hh", [128, 22, TT], BF16)
                sg = [sb(e1, "sg%d" % i, [128, TT]) for i in range(2)]
                st = [sb(e1, "st%d" % i, [128, TT]) for i in range(3)]
                tmp = sb(e1, "tmp", [128, TT])
                for it in range(NT):
                    sch.dma("sp", y[:], fm(src.t, it * TT), r=[src], w=[y])
                    for c in range(8):
                        DVE(lambda: nc.vector.tensor_copy(out=xb[:, c, :], in_=y[:, c, :]), r=[y], w=[xb])
                    for j in range(22):
                        bg = nextps()
                        bu = nextps()
                        for kc in range(8):
                            PE(lambda: nc.tensor.matmul(bg[:], lhsT=Win[:, kc, j * 128:(j + 1) * 128], rhs=xb[:, kc, :],
                                                        start=(kc == 0), stop=(kc == 7)), r=[Win, xb], w=[bg], ms=(kc == 7))
                        for kc in range(8):
                            PE(lambda: nc.tensor.matmul(bu[:], lhsT=Win[:, kc, DFF + j * 128:DFF + (j + 1) * 128],
                                                        rhs=xb[:, kc, :], start=(kc == 0), stop=(kc == 7)),
                               r=[Win, xb], w=[bu], ms=(kc == 7))
                        s_ = sg[j % 2]
                        ACT(lambda: nc.scalar.activation(out=s_[:], in_=bg[:], func=AF.Silu), r=[bg], w=[s_])
                        DVE(lambda: nc.vector.tensor_tensor(out=hh[:, j, :], in0=s_[:], in1=bu[:], op=ALU.mult),
                            r=[s_, bu], w=[hh])
                    for oc in range(8):
                        bk = nextps()
                        for j in range(22):
                            PE(lambda: nc.tensor.matmul(bk[:], lhsT=Wdn[:, j, oc * 128:(oc + 1) * 128], rhs=hh[:, j, :],
                                                        start=(j == 0), stop=(j == 21)), r=[Wdn, hh], w=[bk], ms=(j == 21))
                        DVE(lambda: nc.vector.scalar_tensor_tensor(out=y[:, oc, :], in0=y[:, oc, :], scalar=ALPHA, in1=bk[:],
                                                                   op0=ALU.mult, op1=ALU.add), r=[y, bk], w=[y])
                    ln_tile(y, hh, hh, st, tmp, gcol, bcol)
                    sch.dma("sp", fm(dst.t, it * TT), y[:], r=[y], w=[dst])
                sch.barrier()

        def stage_ple(li, src, dst, final):
            with ExitStack() as e1:
                Wg = sb(e1, "Wg", [128, 8, D], BF16)
                Wp = sb(e1, "Wp", [128, 2, D], BF16)
                wload(Wg, ple_w_gate[li], 8)
                wload(Wp, ple_w_proj[li], 2)
                if not final:
                    Wh = sb(e1, "Wh", [128, 8, 4096], BF16)
                    wload(Wh, hg_w_in, 8)
                y = sb(e1, "y", [128, 8, TT])
                xb = sb(e1, "xb", [128, 8, TT], BF16)
                ptok = sb(e1, "ptok", [128, 4, 256])
                pT = sb(e1, "pT", [128, 2, TT], BF16)
                sg = [sb(e1, "sg%d" % i, [128, TT]) for i in range(2)]
                tmp = sb(e1, "tmp", [128, TT])
                if final:
                    otok = sb(e1, "otok", [128, 4, D])
                else:
                    FS = [dict((nm, sb(e1, "f_" + nm, [128, TT])) for nm in
                               ("sq", "ft", "kk", "lf", "G", "d1", "d2", "e0", "e1", "e2", "e3")) for _ in range(2)]
                    HO = [dict((nm, sb(e1, "ho_" + nm, [128, TT], BF16)) for nm in ("qa", "qs", "ka", "kh", "gt"))
                          for _ in range(2)]
                    HD = sb(e1, "HD", [128, 8, 4])
                    HV = sb(e1, "HV", [128, 4, D], BF16)
                for it in range(NT):
                    t0 = it * TT
                    sch.dma("sp", y[:], fm(src.t, t0), r=[src], w=[y])
                    sch.dma("sp", ptok[:], p_in[li, t0:t0 + TT, :].rearrange("(n p) d -> p n d", p=128), w=[ptok])
                    for c in range(8):
                        DVE(lambda: nc.vector.tensor_copy(out=xb[:, c, :], in_=y[:, c, :]), r=[y], w=[xb])
                    for c2 in range(2):
                        bk = nextps()
                        for n in range(4):
                            PE(lambda: nc.tensor.transpose(out=bk[:, n * 128:(n + 1) * 128],
                                                           in_=ptok[:, n, c2 * 128:(c2 + 1) * 128], identity=ident_f),
                               r=[ptok, cst], w=[bk], ms=(n == 3))
                        ACT(lambda: nc.scalar.copy(out=pT[:, c2, :], in_=bk[:]), r=[bk], w=[pT])
                    for oc in range(8):
                        bg = nextps()
                        bp = nextps()
                        for kc in range(8):
                            PE(lambda: nc.tensor.matmul(bg[:], lhsT=Wg[:, kc, oc * 128:(oc + 1) * 128], rhs=xb[:, kc, :],
                                                        start=(kc == 0), stop=(kc == 7)), r=[Wg, xb], w=[bg], ms=(kc == 7))
                        for kc in range(2):
                            PE(lambda: nc.tensor.matmul(bp[:], lhsT=Wp[:, kc, oc * 128:(oc + 1) * 128], rhs=pT[:, kc, :],
                                                        start=(kc == 0), stop=(kc == 1)), r=[Wp, pT], w=[bp], ms=(kc == 1))
                        s_ = sg[oc % 2]
                        ACT(lambda: nc.scalar.activation(out=s_[:], in_=bg[:], func=AF.Sigmoid), r=[bg], w=[s_])
                        DVE(lambda: nc.vector.tensor_tensor(out=tmp[:], in0=s_[:], in1=bp[:], op=ALU.mult),
                            r=[s_, bp], w=[tmp])
                        DVE(lambda: nc.vector.tensor_tensor(out=y[:, oc, :], in0=y[:, oc, :], in1=tmp[:], op=ALU.add),
                            r=[y, tmp], w=[y])
                    if final:
                        for n in range(4):
                            for hf in range(2):
                                bk = nextps()
                                for c in range(4):
                                    PE(lambda: nc.tensor.transpose(out=bk[:, c * 128:(c + 1) * 128],
                                                                   in_=y[:, hf * 4 + c, n * 128:(n + 1) * 128],
                                                                   identity=ident_f), r=[y, cst], w=[bk], ms=(c == 3))
                                ACT(lambda: nc.scalar.copy(out=otok[:, n, hf * 512:(hf + 1) * 512], in_=bk[:]),
                                    r=[bk], w=[otok])
                        sch.dma("sp", out[t0:t0 + TT, :].rearrange("(n p) d -> p n d", p=128), otok[:], r=[otok], w=[dst])
                        continue
                    sch.dma("sp", fm(dst.t, t0), y[:], r=[y], w=[dst])
                    for c in range(8):
                        DVE(lambda: nc.vector.tensor_copy(out=xb[:, c, :], in_=y[:, c, :]), r=[y], w=[xb])

                    def proj(col0):
                        bk = nextps()
                        for kc in range(8):
                            PE(lambda: nc.tensor.matmul(bk[:], lhsT=Wh[:, kc, col0:col0 + 128], rhs=xb[:, kc, :],
                                                        start=(kc == 0), stop=(kc == 7)), r=[Wh, xb], w=[bk], ms=(kc == 7))
                        return bk

                    def bc(tl, pos_):
                        base = tl.t[:, pos_:pos_ + 1]
                        return bass.AP(base.tensor, base.offset, [[TT, 128], [128, 4], [0, 128]])

                    def v3(tl):
                        return tl[:].rearrange("p (c t) -> p c t", c=4)

                    for h in range(8):
                        F = FS[h % 2]
                        O = HO[h % 2]
                        bq = proj(h * 128)
                        ACT(lambda: nc.scalar.activation(out=F["sq"][:], in_=bq[:], func=AF.Silu), r=[bq], w=[F["sq"]])
                        bf = proj(1024 + h * 128)
                        ACT(lambda: nc.scalar.activation(out=F["ft"][:], in_=bf[:], func=AF.Sigmoid), r=[bf], w=[F["ft"]])
                        bgt = proj(3072 + h * 128)
                        ACT(lambda: nc.scalar.activation(out=O["gt"][:], in_=bgt[:], func=AF.Silu), r=[bgt], w=[O["gt"]])
                        DVE(lambda: nc.vector.tensor_scalar(out=F["ft"][:], in0=F["ft"][:], scalar1=lbv[:, 8 + h:9 + h],
                                                            scalar2=lbv[:, h:h + 1], op0=ALU.mult, op1=ALU.add),
                            r=[F["ft"], lbv], w=[F["ft"]])
                        ACT(lambda: nc.scalar.activation(out=F["lf"][:], in_=F["ft"][:], func=AF.Ln), r=[F["ft"]], w=[F["lf"]])
                        DVE(lambda: nc.vector.tensor_scalar(out=F["kk"][:], in0=F["ft"][:], scalar1=-1.0, scalar2=1.0,
                                                            op0=ALU.mult, op1=ALU.add), r=[F["ft"]], w=[F["kk"]])
                        DVE(lambda: nc.vector.tensor_tensor_scan(out=F["G"][:], data0=cst[:, 768:1280], data1=F["lf"][:],
                                                                 initial=0.0, op0=ALU.mult, op1=ALU.add),
                            r=[cst, F["lf"]], w=[F["G"]])
                        ACT(lambda: nc.scalar.activation(out=F["e0"][:], in_=F["G"][:], func=AF.Exp), r=[F["G"]], w=[F["e0"]])
                        DVE(lambda: nc.vector.tensor_tensor(out=v3(F["d1"]), in0=v3(F["G"]), in1=bc(F["G"], 63), op=ALU.subtract),
                            r=[F["G"]], w=[F["d1"]])
                        DVE(lambda: nc.vector.tensor_tensor(out=v3(F["d2"]), in0=bc(F["G"], 127), in1=v3(F["G"]), op=ALU.subtract),
                            r=[F["G"]], w=[F["d2"]])
                        ACT(lambda: nc.scalar.activation(out=F["e1"][:], in_=F["d1"][:], func=AF.Exp), r=[F["d1"]], w=[F["e1"]])
                        ACT(lambda: nc.scalar.activation(out=F["e2"][:], in_=F["d1"][:], func=AF.Exp, scale=-1.0),
                            r=[F["d1"]], w=[F["e2"]])
                        ACT(lambda: nc.scalar.activation(out=F["e3"][:], in_=F["d2"][:], func=AF.Exp), r=[F["d2"]], w=[F["e3"]])
                        DVE(lambda: nc.vector.tensor_tensor(out=O["qs"][:], in0=F["sq"][:], in1=F["e0"][:], op=ALU.mult),
                            r=[F["sq"], F["e0"]], w=[O["qs"]])
                        DVE(lambda: nc.vector.tensor_copy(out=HD[:, h, :], in_=F["e0"][:, 127:TT:128]), r=[F["e0"]], w=[HD])
                        POOL(lambda: nc.gpsimd.tensor_tensor(out=O["qa"][:], in0=F["sq"][:], in1=F["e1"][:], op=ALU.mult),
                             r=[F["sq"], F["e1"]], w=[O["qa"]])
                        DVE(lambda: nc.vector.tensor_tensor(out=O["ka"][:], in0=F["kk"][:], in1=F["e2"][:], op=ALU.mult),
                            r=[F["kk"], F["e2"]], w=[O["ka"]])
                        POOL(lambda: nc.gpsimd.tensor_tensor(out=O["kh"][:], in0=F["kk"][:], in1=F["e3"][:], op=ALU.mult),
                             r=[F["kk"], F["e3"]], w=[O["kh"]])
                        for nm, dstt in (("qa", sc_hqa), ("qs", sc_hqs), ("ka", sc_hka), ("kh", sc_hkh), ("gt", sc_hg)):
                            sch.dma("sp", dstt[h, :, t0:t0 + TT], O[nm][:], r=[O[nm]], w=[dstt])
                    for n in range(4):
                        for hf in range(2):
                            bk = nextps()
                            for kc in range(8):
                                PE(lambda: nc.tensor.matmul(bk[:], lhsT=xb[:, kc, n * 128:(n + 1) * 128],
                                                            rhs=Wh[:, kc, 2048 + hf * 512:2048 + (hf + 1) * 512],
                                                            start=(kc == 0), stop=(kc == 7)), r=[Wh, xb], w=[bk], ms=(kc == 7))
                            DVE(lambda: nc.vector.tensor_copy(out=HV[:, n, hf * 512:(hf + 1) * 512], in_=bk[:]),
                                r=[bk], w=[HV])
                    sch.dma("sp", sc_hd[:, :, it * 4:(it + 1) * 4], HD[:], r=[HD], w=[sc_hd])
                    sch.dma("sp", sc_hv[t0:t0 + TT, :].rearrange("(n p) d -> p n d", p=128), HV[:], r=[HV], w=[sc_hv])
                sch.barrier()

        def stage_hgrn(src_x, dst, gcol, bcol):
            with ExitStack() as e1:
                Wo = sb(e1, "Who", [128, 8, D], BF16)
                wload(Wo, hg_w_o, 8)
                for kc in range(8):
                    DVE(lambda: nc.vector.tensor_scalar(out=Wo[:, kc, :], in0=Wo[:, kc, :], scalar1=pv[:, c_on + kc:c_on + kc + 1],
                                                        scalar2=None, op0=ALU.mult), r=[Wo, pv], w=[Wo])
                um = sb(e1, "um", [128, 128])
                DVE(lambda: nc.vector.tensor_scalar(out=um[:], in0=cst[:, 128:256], scalar1=-1.0, scalar2=None,
                                                    op0=ALU.is_gt), r=[cst], w=[um])
                um_bc = bass.AP(um.t[:].tensor, um.t[:].offset, [[128, 128], [0, 4], [1, 128]])
                NBUF = 2
                HQA = [sb(e1, "HQA", [128, 8, TT], BF16) for _ in range(NBUF)]
                HQS = [sb(e1, "HQS", [128, 8, TT], BF16) for _ in range(NBUF)]
                HKA = [sb(e1, "HKA", [128, 8, TT], BF16) for _ in range(NBUF)]
                HKH = [sb(e1, "HKH", [128, 8, TT], BF16) for _ in range(NBUF)]
                HGt = [sb(e1, "HGt", [128, 8, TT], BF16) for _ in range(NBUF)]
                HD = [sb(e1, "HD", [128, 8, 4]) for _ in range(NBUF)]
                HV = [sb(e1, "HV", [128, 4, D], BF16) for _ in range(NBUF)]
                yy = [sb(e1, "y", [128, 8, TT]) for _ in range(1)]
                OG = sb(e1, "OG", [128, 8, TT], BF16)
                S32 = [sb(e1, "S32", [128, 4, 128]) for _ in range(2)]
                Sb = [sb(e1, "Sb", [128, 4, 128], BF16) for _ in range(2)]
                khT = [sb(e1, "khT", [128, 4, 128], BF16) for _ in range(8)]
                ATb = [sb(e1, "ATb", [128, 4, 128], BF16) for _ in range(8)]
                osq = [sb(e1, "osq", [128, TT], BF16) for _ in range(2)]
                on = [sb(e1, "on", [128, TT]) for _ in range(2)]
                lnr = [sb(e1, "lnr", [128, TT]) for _ in range(2)]
                scr = sb(e1, "scr", [128, 16, TT], BF16)
                st = [sb(e1, "st", [128, TT]) for i in range(3)]
                tmp = sb(e1, "tmp", [128, TT])
                for g in range(2):
                    DVE(lambda: nc.vector.memset(S32[g][:], 0.0), w=[S32[g]])
                    DVE(lambda: nc.vector.memset(Sb[g][:], 0.0), w=[Sb[g]])

                def ld(it):
                    t0 = it * TT
                    i = it % NBUF
                    sch.dma("sp", HKH[i][:], hm(sc_hkh.t, t0), r=[sc_hkh], w=[HKH[i]])
                    sch.dma("sp", HKA[i][:], hm(sc_hka.t, t0), r=[sc_hka], w=[HKA[i]])
                    sch.dma("sp", HQA[i][:], hm(sc_hqa.t, t0), r=[sc_hqa], w=[HQA[i]])
                    sch.dma("sp", HQS[i][:], hm(sc_hqs.t, t0), r=[sc_hqs], w=[HQS[i]])
                    sch.dma("sp", HV[i][:], sc_hv[t0:t0 + TT, :].rearrange("(n p) d -> p n d", p=128), r=[sc_hv], w=[HV[i]])
                    sch.dma("sp", HD[i][:], sc_hd[:, :, it * 4:(it + 1) * 4], r=[sc_hd], w=[HD[i]])
                    sch.dma("sp", HGt[i][:], hm(sc_hg.t, t0), r=[sc_hg], w=[HGt[i]])

                ld(0)
                kk = 0
                for it in range(NT):
                    t0 = it * TT
                    if it + 1 < NT:
                        ld(it + 1)
                    i = it % NBUF
                    qa, qs, ka, kh, gt, hd, hv, y = HQA[i], HQS[i], HKA[i], HKH[i], HGt[i], HD[i], HV[i], yy[0]
                    sch.dma("sp", y[:], fm(src_x.t, t0), r=[src_x], w=[y])
                    for h in range(8):
                        bt = nextps(4)
                        for c in range(4):
                            cs = slice(c * 128, (c + 1) * 128)
                            PE(lambda: nc.tensor.matmul(bt[:, cs], lhsT=kh[:, h, cs], rhs=ident_b[:], start=True, stop=True),
                               r=[kh, ident_b], w=[bt], ms=(c == 3))
                        ACT(lambda: nc.scalar.copy(out=khT[h][:].rearrange("p c d -> p (c d)"), in_=bt[:]), r=[bt], w=[khT[h]])
                        ba = nextps(4)
                        for c in range(4):
                            cs = slice(c * 128, (c + 1) * 128)
                            PE(lambda: nc.tensor.matmul(ba[:, cs], lhsT=ka[:, h, cs], rhs=qa[:, h, cs], start=True, stop=True),
                               r=[ka, qa], w=[ba], ms=(c == 3))
                        DVE(lambda: nc.vector.tensor_tensor(out=ATb[h][:], in0=ba[:].rearrange("p (c t) -> p c t", c=4),
                                                            in1=um_bc, op=ALU.mult), r=[ba, um], w=[ATb[h]])
                    for c in range(4):
                        cs = slice(c * 128, (c + 1) * 128)
                        for g in range(2):
                            ob_ = banks[4 + (kk % 2)]
                            bd = banks[6]
                            j2 = kk % 2
                            kk += 1
                            for j in range(4):
                                h = 4 * g + j
                                js = slice(j * 128, (j + 1) * 128)
                                hs = slice(h * 128, (h + 1) * 128)
                                PE(lambda: nc.tensor.matmul(ob_[:, js], lhsT=Sb[g][:, j, :], rhs=qs[:, h, cs], start=True, stop=False),
                                   r=[Sb[g], qs], w=[ob_], ms=False)
                                PE(lambda: nc.tensor.matmul(ob_[:, js], lhsT=hv[:, c, hs], rhs=ATb[h][:, c, :], start=False, stop=True),
                                   r=[hv, ATb[h]], w=[ob_], ms=(j == 3))
                            for j in range(4):
                                h = 4 * g + j
                                js = slice(j * 128, (j + 1) * 128)
                                hs = slice(h * 128, (h + 1) * 128)
                                PE(lambda: nc.tensor.matmul(bd[:, js], lhsT=khT[h][:, c, :], rhs=hv[:, c, hs], start=True, stop=True),
                                   r=[khT[h], hv], w=[bd], ms=(j == 3))
                            dec = bass.AP(hd.t[:].tensor, hd.t[:, 4 * g, c:c + 1].offset, [[32, 128], [4, 4], [0, 128]])
                            DVE(lambda: nc.vector.tensor_tensor(out=S32[g][:], in0=S32[g][:], in1=dec, op=ALU.mult),
                                r=[S32[g], hd], w=[S32[g]])
                            DVE(lambda: nc.vector.tensor_tensor(out=S32[g][:], in0=S32[g][:],
                                                                in1=bd[:].rearrange("p (j e) -> p j e", j=4), op=ALU.add),
                                r=[S32[g], bd], w=[S32[g]])
                            ACT(lambda: nc.scalar.copy(out=Sb[g][:], in_=S32[g][:]), r=[S32[g]], w=[Sb[g]])
                            ACT(lambda: nc.scalar.activation(out=osq[j2][:], in_=ob_[:], func=AF.Square), r=[ob_], w=[osq[j2]])
                            bs = banks[7]
                            PE(lambda: nc.tensor.matmul(bs[:], lhsT=ones_b[:], rhs=osq[j2][:], start=True, stop=True),
                               r=[ones_b, osq[j2]], w=[bs])
                            ACT(lambda: nc.scalar.activation(out=lnr[j2][:], in_=bs[:], func=AF.Ln, scale=1.0 / 128.0, bias=RMS_EPS),
                                r=[bs], w=[lnr[j2]])
                            ACT(lambda: nc.scalar.activation(out=lnr[j2][:], in_=lnr[j2][:], func=AF.Exp, scale=-0.5),
                                r=[lnr[j2]], w=[lnr[j2]])
                            DVE(lambda: nc.vector.tensor_tensor(out=on[j2][:], in0=ob_[:], in1=lnr[j2][:], op=ALU.mult),
                                r=[ob_, lnr[j2]], w=[on[j2]])
                            POOL(lambda: nc.gpsimd.tensor_tensor(out=OG[:, 4 * g:4 * g + 4, cs],
                                                                 in0=on[j2][:].rearrange("p (j t) -> p j t", j=4),
                                                                 in1=gt[:, 4 * g:4 * g + 4, cs], op=ALU.mult),
                                 r=[on[j2], gt], w=[OG])
                    if dbg_og is not None:
                        sch.dma("sp", fm(dbg_og.t, t0), OG[:], r=[OG], w=[dbg_og])
                    for oc in range(8):
                        bk = nextps(4)
                        for kc in range(8):
                            PE(lambda: nc.tensor.matmul(bk[:], lhsT=Wo[:, kc, oc * 128:(oc + 1) * 128], rhs=OG[:, kc, :],
                                                        start=(kc == 0), stop=(kc == 7)), r=[Wo, OG], w=[bk], ms=(kc == 7))
                        DVE(lambda: nc.vector.scalar_tensor_tensor(out=y[:, oc, :], in0=y[:, oc, :], scalar=ALPHA, in1=bk[:],
                                                                   op0=ALU.mult, op1=ALU.add), r=[y, bk], w=[y])
                    ln_tile(y, scr, scr, st, tmp, gcol, bcol, nb=4)
                    sch.dma("sp", fm(dst.t, t0), y[:], r=[y], w=[dst])
                sch.barrier()

        out_t = T(out)
        if upto >= 3:
            stage_proj_ln(mla_w_o, sc_o, sc_xT, sc_x1, c_lng[0][0], c_lnb[0][0])
        if upto >= 4:
            stage_ffn(0, sc_x1, sc_xT, c_lng[0][1], c_lnb[0][1])
        if upto >= 5:
            stage_ple(0, sc_xT, sc_x1, False)
        if upto >= 6:
            stage_hgrn(sc_x1, sc_xT, c_lng[1][0], c_lnb[1][0])
        if upto >= 7:
            stage_ffn(1, sc_xT, sc_x1, c_lng[1][1], c_lnb[1][1])
        if upto >= 8:
            stage_ple(1, sc_x1, out_t, True)

        sch.barrier()
    return nc


def make_consts():
    c = np.zeros((128, 1408), np.float32)
    c[:, 0:128] = np.eye(128, dtype=np.float32)
    k = np.arange(128)[:, None]
    q = np.arange(128)[None, :]
    c[:, 128:256] = np.where(k <= q, 0.0, -30000.0)
    s = np.arange(64)[:, None]
    t = np.arange(64)[None, :]
    c[0:64, 256:768] = np.tile((s <= t).astype(np.float32), (1, 8))
    c[:, 768:1280] = (np.arange(512) % 128 != 0).astype(np.float32)[None, :]
    inv = (10000.0 ** (-np.arange(0, 64, 2, dtype=np.float32) / 64.0)).astype(np.float32)
    c[0:64, 1280] = np.concatenate([inv, inv])
    c[0:64, 1281] = np.concatenate([-inv, inv])
    return c


_W_NAMES = ["mla_w_dqkv", "mla_w_uq", "mla_w_ukv", "mla_w_o", "hgrn_w_in", "hgrn_w_o"]
_W_FULL = ["ffn_w_in", "ffn_w_down", "ple_w_proj", "ple_w_gate"]


def make_pvec(inputs):
    def col(v):
        return np.asarray(v, dtype=np.float32).reshape(-1, 128).T
    cols = [col(inputs["mla_q_norm"][0]), col(inputs["mla_kv_norm"][0])]
    for nm in ("ln_mix_g", "ln_ffn_g"):
        pass
    for i in range(2):
        cols += [col(inputs["ln_mix_g"][i]), col(inputs["ln_ffn_g"][i])]
    for i in range(2):
        cols += [col(inputs["ln_mix_b"][i]), col(inputs["ln_ffn_b"][i])]
    cols += [col(inputs["hgrn_out_norm"][0]), col(inputs["hgrn_lb_logits"][0]), col(inputs["hgrn_lb_logits"][1])]
    pv = np.concatenate(cols, axis=1)
    out = np.zeros((128, 96), np.float32)
    out[:, :pv.shape[1]] = pv
    return out


def make_in_maps(inputs, n_cores, S):
    consts = make_consts()
    shared = {"consts": consts, "pvec": make_pvec(inputs)}
    for k in _W_NAMES:
        shared[k] = np.ascontiguousarray(np.asarray(inputs[k], dtype=np.float32)[0])
    for k in _W_FULL:
        shared[k] = np.ascontiguousarray(np.asarray(inputs[k], dtype=np.float32))
    maps = []
    xs = np.asarray(inputs["x"])
    ps = np.asarray(inputs["p"])
    po = np.asarray(inputs["positions"])
    for b in range(n_cores):
        m = dict(shared)
        m["x"] = np.ascontiguousarray(xs[b, :S])
        m["p"] = np.ascontiguousarray(ps[:, b, :S])
        m["pos"] = np.ascontiguousarray(po[b:b + 1, :S]).astype(np.int32)
        maps.append(m)
    return maps


def kernel(**inputs):
    n = 8
    S = 8192
    nc = build(S)
    maps = make_in_maps(inputs, n, S)
    res = run_bass_kernel_spmd(nc, maps, core_ids=list(range(n)))
    return np.stack([np.asarray(r["out"]) for r in res.results], axis=0).astype(np.float32)
```

```python
import math
from contextlib import ExitStack

import numpy as np
import concourse.bass as bass
import concourse.mybir as mybir
from concourse.bass_utils import run_bass_kernel_spmd

F32 = mybir.dt.float32
BF16 = mybir.dt.bfloat16
I32 = mybir.dt.int32
AF = mybir.ActivationFunctionType
ALU = mybir.AluOpType

D = 1024
DFF = 2816
ALPHA = 4.0 ** 0.25
LN_EPS = 1e-5
RMS_EPS = 1e-6
TT = 512


class Buf:
    __slots__ = ("w", "r")

    def __init__(self):
        self.w = None
        self.r = {}


class T:
    def __init__(self, t, excl=False):
        self.t = t
        self.b = Buf()
        self.excl = excl

    def __getitem__(self, k):
        return self.t[k]


class Sched:
    def __init__(self, nc, es):
        self.nc = nc
        self.eng = {"pe": nc.tensor, "act": nc.scalar, "dve": nc.vector, "pool": nc.gpsimd, "sp": nc.sync}
        self.sems = {}
        self.cnt = {}
        self.seen = {k: {} for k in self.eng}
        for k in self.eng:
            self.sems[k] = es.enter_context(nc.semaphore("s_" + k))
            self.cnt[k] = 0
        self.ring = {}
        for q, n in (("sp", 8), ("pool", 8), ("act", 4)):
            lst = []
            for i in range(n):
                key = (q, i)
                self.sems[key] = es.enter_context(nc.semaphore("d_%s%d" % (q, i)))
                self.cnt[key] = 0
                lst.append(key)
            self.ring[q] = [lst, 0]
        self.nops = 0

    def _need(self, e, reads, writes):
        need = {}

        def add(ev, raw):
            if ev is None:
                return
            key, val = ev
            if key == e and e == "pe":
                return
            if need.get(key, 0) < val:
                need[key] = val

        for b in reads:
            add(b.b.w, True)
            if b.excl:
                for k, v in b.b.r.items():
                    if k != e:
                        add((k, v), False)
        for b in writes:
            add(b.b.w, False)
            for k, v in b.b.r.items():
                add((k, v), False)
        seen = self.seen[e]
        for key, val in need.items():
            assert val <= self.cnt[key], ("wait on a milestone not yet emitted", e, key, val, self.cnt[key])
            if seen.get(key, 0) < val:
                self.eng[e].wait_ge(self.sems[key], val)
                seen[key] = val

    def _record(self, ev, reads, writes):
        key, val = ev
        for b in reads:
            b.b.r[key] = val
        for b in writes:
            b.b.w = ev
            b.b.r = {}

    def op(self, e, fn, r=(), w=(), ms=True):
        self._need(e, r, w)
        inst = fn()
        val = self.cnt[e] + 1
        if ms:
            inst.then_inc(self.sems[e], 1)
            self.cnt[e] = val
        self._record((e, val), r, w)
        self.nops += 1
        return inst

    def dma(self, q, out, in_, r=(), w=()):
        self._need(q, r, w)
        lst, i = self.ring[q]
        key = lst[i % len(lst)]
        self.ring[q][1] = i + 1
        prior = self.cnt[key]
        if prior > 0 and self.seen[q].get(key, 0) < prior:
            self.eng[q].wait_ge(self.sems[key], prior)
            self.seen[q][key] = prior
        inst = self.eng[q].dma_start(out=out, in_=in_)
        inst.then_inc(self.sems[key], 16)
        self.cnt[key] = prior + 16
        self._record((key, prior + 16), r, w)
        self.nops += 1
        return inst

    def barrier(self):
        for e in self.eng:
            for key, val in self.cnt.items():
                if key == e or val == 0:
                    continue
                if self.seen[e].get(key, 0) < val:
                    self.eng[e].wait_ge(self.sems[key], val)
                    self.seen[e][key] = val


def build(S=8192, upto=99, debug=False):
    nc = bass.Bass("TRN2", target_bir_lowering=False)
    NT = S // TT
    NB = S // 128
    dbg_kind = "ExternalOutput" if debug else "Internal"

    def dram(name, shape, dt, kind=None):
        return nc.dram_tensor(name, list(shape), dt, kind=kind or dbg_kind).ap()

    def din(name, shape, dt=F32):
        return dram(name, shape, dt, "ExternalInput")

    x = din("x", [S, D])
    p_in = din("p", [2, S, 256])
    pos = din("pos", [1, S], I32)
    consts = din("consts", [128, 1408])
    pvec = din("pvec", [128, 96])
    w_dqkv = din("mla_w_dqkv", [D, 576])
    w_uq = din("mla_w_uq", [256, 1536])
    w_ukv = din("mla_w_ukv", [256, 2048])
    mla_w_o = din("mla_w_o", [D, D])
    hg_w_in = din("hgrn_w_in", [D, 4096])
    hg_w_o = din("hgrn_w_o", [D, D])
    ffn_w_in = din("ffn_w_in", [2, D, 2 * DFF])
    ffn_w_dn = din("ffn_w_down", [2, DFF, D])
    ple_w_proj = din("ple_w_proj", [2, 256, D])
    ple_w_gate = din("ple_w_gate", [2, D, D])
    out = dram("out", [S, D], F32, "ExternalOutput")

    sc_cc = T(dram("sc_cc", [64, S], F32))
    sc_ss = T(dram("sc_ss", [64, S], F32))
    sc_xT = T(dram("sc_xT", [D, S], F32))
    sc_x1 = T(dram("sc_x1", [D, S], F32))
    sc_qn = T(dram("sc_qn", [8, 128, S], BF16))
    sc_qr = T(dram("sc_qr", [8, 64, S], BF16))
    sc_kn = T(dram("sc_kn", [8, 128, S], BF16))
    sc_kr = T(dram("sc_kr", [64, S], BF16))
    sc_v = T(dram("sc_v", [S, D], BF16))
    sc_o = T(dram("sc_o", [D, S], BF16))
    sc_hqa = T(dram("sc_hqa", [8, 128, S], BF16))
    sc_hqs = T(dram("sc_hqs", [8, 128, S], BF16))
    sc_hka = T(dram("sc_hka", [8, 128, S], BF16))
    sc_hkh = T(dram("sc_hkh", [8, 128, S], BF16))
    sc_hg = T(dram("sc_hg", [8, 128, S], BF16))
    sc_hv = T(dram("sc_hv", [S, D], BF16))
    sc_hd = T(dram("sc_hd", [128, 8, S // 128], F32))
    dbg_og = T(dram("dbg_og", [D, S], BF16)) if debug else None

    es = ExitStack()
    with es:
        sch = Sched(nc, es)
        PE = lambda fn, r=(), w=(), ms=True: sch.op("pe", fn, r, w, ms)
        ACT = lambda fn, r=(), w=(): sch.op("act", fn, r, w)
        DVE = lambda fn, r=(), w=(): sch.op("dve", fn, r, w)
        POOL = lambda fn, r=(), w=(): sch.op("pool", fn, r, w)

        uniq = [0]

        def sb(es_, name, shape, dt=F32):
            uniq[0] += 1
            return T(es_.enter_context(nc.sbuf_tensor("%s_%d" % (name, uniq[0]), list(shape), dt)))

        banks = [T(es.enter_context(nc.psum_tensor("ps%d" % i, [128, 512], F32)), excl=True) for i in range(8)]
        bank_i = [0]

        def nextps(n=8):
            b = banks[bank_i[0] % n]
            bank_i[0] += 1
            return b

        cst = sb(es, "cst", [128, 1408])
        sch.dma("sp", cst[:], consts[:, :], w=[cst])
        ident_f = cst[:, 0:128]
        ident_b = sb(es, "ident_b", [128, 128], BF16)
        maskb = sb(es, "maskb", [128, 128], BF16)
        amask = sb(es, "amask", [64, 512], BF16)
        ones_b = sb(es, "ones_b", [128, 128], BF16)
        DVE(lambda: nc.vector.tensor_copy(out=ident_b[:], in_=cst[:, 0:128]), r=[cst], w=[ident_b])
        DVE(lambda: nc.vector.tensor_copy(out=maskb[:], in_=cst[:, 128:256]), r=[cst], w=[maskb])
        DVE(lambda: nc.vector.tensor_copy(out=amask[:], in_=cst[0:64, 256:768]), r=[cst], w=[amask])
        DVE(lambda: nc.vector.memset(ones_b[:], 1.0), w=[ones_b])
        rmask = cst[:, 768:1280]
        pv = sb(es, "pv", [128, 96])
        sch.dma("sp", pv[:], pvec[:, :], w=[pv])
        c_qn, c_kvn = 0, 2
        c_lng = [[4, 12], [20, 28]]
        c_lnb = [[36, 44], [52, 60]]
        c_on, c_l0, c_l1 = 68, 76, 84
        lbv = sb(es, "lbv", [128, 24])
        DVE(lambda: nc.vector.tensor_tensor(out=lbv[:, 16:24], in0=pv[:, c_l1:c_l1 + 8], in1=pv[:, c_l0:c_l0 + 8],
                                            op=ALU.subtract), r=[pv], w=[lbv])
        ACT(lambda: nc.scalar.activation(out=lbv[:, 0:8], in_=lbv[:, 16:24], func=AF.Sigmoid), r=[lbv], w=[lbv])
        DVE(lambda: nc.vector.tensor_scalar(out=lbv[:, 8:16], in0=lbv[:, 0:8], scalar1=-1.0, scalar2=1.0,
                                            op0=ALU.mult, op1=ALU.add), r=[lbv], w=[lbv])

        TWO_PI = 2.0 * math.pi
        C1 = 6.28125
        C2 = TWO_PI - C1
        with ExitStack() as e1:
            RC = min(S, 2048)
            pi_ = sb(e1, "r_pi", [64, RC], I32)
            pf = sb(e1, "r_pf", [64, RC])
            for tab in range(2):
                kf = sb(e1, "r_kf%d" % tab, [64, RC])
                ki = sb(e1, "r_ki%d" % tab, [64, RC], I32)
                ang = sb(e1, "r_ang%d" % tab, [64, RC])
                tm = sb(e1, "r_tm%d" % tab, [64, RC])
                res = sb(e1, "r_res%d" % tab, [64, RC])
                for c0 in range(0, S, RC):
                    if tab == 0:
                        pass
                    src = bass.AP(pos.tensor, c0, [[0, 64], [1, RC]])
                    sch.dma("sp", pi_[:], src, w=[pi_])
                    DVE(lambda: nc.vector.tensor_copy(out=pf[:], in_=pi_[:]), r=[pi_], w=[pf])
                    fcol = cst[0:64, 1280 + tab:1281 + tab]
                    DVE(lambda: nc.vector.tensor_scalar(out=ang[:], in0=pf[:], scalar1=fcol, scalar2=None,
                                                        op0=ALU.mult), r=[pf, cst], w=[ang])
                    DVE(lambda: nc.vector.tensor_scalar(out=kf[:], in0=ang[:], scalar1=1.0 / TWO_PI, scalar2=None,
                                                        op0=ALU.mult), r=[ang], w=[kf])
                    DVE(lambda: nc.vector.tensor_copy(out=ki[:], in_=kf[:]), r=[kf], w=[ki])
                    DVE(lambda: nc.vector.tensor_copy(out=kf[:], in_=ki[:]), r=[ki], w=[kf])
                    DVE(lambda: nc.vector.scalar_tensor_tensor(out=tm[:], in0=kf[:], scalar=-C1, in1=ang[:],
                                                               op0=ALU.mult, op1=ALU.add), r=[kf, ang], w=[tm])
                    DVE(lambda: nc.vector.scalar_tensor_tensor(out=ang[:], in0=kf[:], scalar=-C2, in1=tm[:],
                                                               op0=ALU.mult, op1=ALU.add), r=[kf, tm], w=[ang])
                    if tab == 0:
                        DVE(lambda: nc.vector.tensor_scalar(out=ang[:], in0=ang[:], scalar1=math.pi / 2, scalar2=None,
                                                            op0=ALU.add), r=[ang], w=[ang])
                    for _ in range(2):
                        DVE(lambda: nc.vector.tensor_scalar(out=tm[:], in0=ang[:], scalar1=math.pi, scalar2=-TWO_PI,
                                                            op0=ALU.is_gt, op1=ALU.mult), r=[ang], w=[tm])
                        DVE(lambda: nc.vector.tensor_tensor(out=ang[:], in0=ang[:], in1=tm[:], op=ALU.add),
                            r=[ang, tm], w=[ang])
                        DVE(lambda: nc.vector.tensor_scalar(out=tm[:], in0=ang[:], scalar1=-math.pi, scalar2=TWO_PI,
                                                            op0=ALU.is_lt, op1=ALU.mult), r=[ang], w=[tm])
                        DVE(lambda: nc.vector.tensor_tensor(out=ang[:], in0=ang[:], in1=tm[:], op=ALU.add),
                            r=[ang, tm], w=[ang])
                    DVE(lambda: nc.vector.tensor_scalar(out=ang[:], in0=ang[:], scalar1=3.14159, scalar2=-3.14159,
                                                        op0=ALU.min, op1=ALU.max), r=[ang], w=[ang])
                    ACT(lambda: nc.scalar.activation(out=res[:], in_=ang[:], func=AF.Sin), r=[ang], w=[res])
                    dst = sc_cc if tab == 0 else sc_ss
                    sch.dma("sp", dst[:, c0:c0 + RC], res[:], r=[res], w=[dst])
            sch.barrier()

        def load_w(q, dst_tile, dst_ap, src_ap):
            sch.dma(q, dst_ap, src_ap, w=[dst_tile])

        if upto >= 1:
            with ExitStack() as e1:
                Wd = sb(e1, "Wd", [128, 8, 640], BF16)
                Wuq = sb(e1, "Wuq", [128, 2, 8, 256], BF16)
                Wkv = sb(e1, "Wkv", [128, 2, 2, 8, 128], BF16)
                wdv = w_dqkv.rearrange("(c p) n -> p c n", p=128)
                load_w("pool", Wd, Wd[:, :, 0:576], wdv)
                load_w("pool", Wd, Wd[:, :, 576:608], wdv[:, :, 544:576])
                load_w("pool", Wd, Wd[:, :, 608:640], wdv[:, :, 512:544])
                for kc in range(2):
                    wv = w_uq[kc * 128:(kc + 1) * 128, :].rearrange("p (h d) -> p h d", h=8)
                    load_w("pool", Wuq, Wuq[:, kc, :, 0:192], wv)
                    load_w("pool", Wuq, Wuq[:, kc, :, 192:224], wv[:, :, 160:192])
                    load_w("pool", Wuq, Wuq[:, kc, :, 224:256], wv[:, :, 128:160])
                    wk = w_ukv[kc * 128:(kc + 1) * 128, :].rearrange("p (h kv d) -> p kv h d", h=8, kv=2)
                    for kv in range(2):
                        load_w("pool", Wkv, Wkv[:, kc, kv, :, :], wk[:, kv, :, :])
                xtok = [sb(e1, "xtok%d" % i, [128, 4, D]) for i in range(2)]
                xT32 = [sb(e1, "xT32_%d" % i, [128, 8, TT]) for i in range(2)]
                xTb = sb(e1, "xTb", [128, 8, TT], BF16)
                sq = sb(e1, "sq", [128, 4, TT], BF16)
                lnt = sb(e1, "lnt", [128, 2, TT])
                rstd = sb(e1, "rstd", [128, 2, TT])
                cl = sb(e1, "cl", [128, 4, TT], BF16)
                cct = [sb(e1, "cct%d" % i, [64, TT]) for i in range(2)]
                sst = [sb(e1, "sst%d" % i, [64, TT]) for i in range(2)]
                t1 = sb(e1, "t1", [64, TT])
                t2 = sb(e1, "t2", [64, TT])
                krb = sb(e1, "krb", [64, TT], BF16)
                Qn = sb(e1, "Qn", [128, 8, TT], BF16)
                Qr = sb(e1, "Qr", [64, 8, TT], BF16)
                Kn = sb(e1, "Kn", [128, 8, TT], BF16)
                Vt = sb(e1, "Vt", [128, 4, D], BF16)

                def load_x(it):
                    t0 = it * TT
                    sch.dma("sp", xtok[it % 2][:], x[t0:t0 + TT, :].rearrange("(n p) d -> p n d", p=128),
                            w=[xtok[it % 2]])
                    sch.dma("sp", cct[it % 2][:], sc_cc[:, t0:t0 + TT], r=[sc_cc], w=[cct[it % 2]])
                    sch.dma("sp", sst[it % 2][:], sc_ss[:, t0:t0 + TT], r=[sc_ss], w=[sst[it % 2]])

                import os as _os
                _kstop = int(_os.environ.get("KSTOP", "99"))
                load_x(0)
                for it in range(NT if _kstop > 0 else 0):
                    t0 = it * TT
                    if it + 1 < NT:
                        load_x(it + 1)
                    xt = xtok[it % 2]
                    x32 = xT32[it % 2]
                    CCt = cct[it % 2]
                    SSt = sst[it % 2]
                    for c in range(8):
                        bk = nextps()
                        for n in range(4):
                            PE(lambda: nc.tensor.transpose(out=bk[:, n * 128:(n + 1) * 128],
                                                           in_=xt[:, n, c * 128:(c + 1) * 128], identity=ident_f),
                               r=[xt, cst], w=[bk])
                        ACT(lambda: nc.scalar.copy(out=x32[:, c, :], in_=bk[:]), r=[bk], w=[x32])
                        DVE(lambda: nc.vector.tensor_copy(out=xTb[:, c, :], in_=x32[:, c, :]), r=[x32], w=[xTb])
                    sch.dma("sp", sc_xT[:, t0:t0 + TT].rearrange("(c p) t -> p c t", p=128), x32[:], r=[x32], w=[sc_xT])
                    if _kstop <= 1:
                        continue
                    dps = []
                    for oc in range(4):
                        bk = nextps()
                        for kc in range(8):
                            PE(lambda: nc.tensor.matmul(bk[:], lhsT=Wd[:, kc, oc * 128:(oc + 1) * 128], rhs=xTb[:, kc, :],
                                                        start=(kc == 0), stop=(kc == 7)), r=[Wd, xTb], w=[bk])
                        ACT(lambda: nc.scalar.activation(out=sq[:, oc, :], in_=bk[:], func=AF.Square), r=[bk], w=[sq])
                        dps.append(bk)
                    for g in range(2):
                        bs = nextps()
                        for j in range(2):
                            PE(lambda: nc.tensor.matmul(bs[:], lhsT=ones_b[:], rhs=sq[:, 2 * g + j, :],
                                                        start=(j == 0), stop=(j == 1)), r=[ones_b, sq], w=[bs])
                        ACT(lambda: nc.scalar.activation(out=lnt[:, g, :], in_=bs[:], func=AF.Ln, scale=1.0 / 256.0,
                                                         bias=RMS_EPS), r=[bs], w=[lnt])
                        ACT(lambda: nc.scalar.activation(out=rstd[:, g, :], in_=lnt[:, g, :], func=AF.Exp, scale=-0.5),
                            r=[lnt], w=[rstd])
                        for j in range(2):
                            col = (c_qn if g == 0 else c_kvn) + j
                            DVE(lambda: nc.vector.scalar_tensor_tensor(out=cl[:, 2 * g + j, :], in0=dps[2 * g + j][:],
                                                                       scalar=pv[:, col:col + 1], in1=rstd[:, g, :],
                                                                       op0=ALU.mult, op1=ALU.mult),
                                r=[dps[2 * g + j], pv, rstd], w=[cl])

                    def rope(b1, b2, dst_ap, dst_t):
                        DVE(lambda: nc.vector.tensor_tensor(out=t1[:], in0=b1[0:64, :], in1=CCt[:], op=ALU.mult),
                            r=[b1, CCt], w=[t1])
                        DVE(lambda: nc.vector.tensor_tensor(out=t2[:], in0=b2[0:64, :], in1=SSt[:], op=ALU.mult),
                            r=[b2, SSt], w=[t2])
                        DVE(lambda: nc.vector.tensor_tensor(out=dst_ap, in0=t1[:], in1=t2[:], op=ALU.add),
                            r=[t1, t2], w=[dst_t])

                    if _kstop <= 2:
                        continue
                    b1 = nextps()
                    b2 = nextps()
                    for bk, c0 in ((b1, 512), (b2, 576)):
                        for kc in range(8):
                            PE(lambda: nc.tensor.matmul(bk[0:64, :], lhsT=Wd[:, kc, c0:c0 + 64], rhs=xTb[:, kc, :],
                                                        start=(kc == 0), stop=(kc == 7)), r=[Wd, xTb], w=[bk])
                    rope(b1, b2, krb[:], krb)
                    sch.dma("sp", sc_kr[:, t0:t0 + TT], krb[:], r=[krb], w=[sc_kr])
                    if _kstop <= 3:
                        continue
                    for h in range(8):
                        bn = nextps()
                        b1 = nextps()
                        b2 = nextps()
                        for kc in range(2):
                            PE(lambda: nc.tensor.matmul(bn[:], lhsT=Wuq[:, kc, h, 0:128], rhs=cl[:, kc, :],
                                                        start=(kc == 0), stop=(kc == 1)), r=[Wuq, cl], w=[bn])
                        for bk, c0 in ((b1, 128), (b2, 192)):
                            for kc in range(2):
                                PE(lambda: nc.tensor.matmul(bk[0:64, :], lhsT=Wuq[:, kc, h, c0:c0 + 64], rhs=cl[:, kc, :],
                                                            start=(kc == 0), stop=(kc == 1)), r=[Wuq, cl], w=[bk])
                        ACT(lambda: nc.scalar.copy(out=Qn[:, h, :], in_=bn[:]), r=[bn], w=[Qn])
                        rope(b1, b2, Qr[:, h, :], Qr)
                    sch.dma("sp", sc_qn[:, :, t0:t0 + TT].rearrange("h p t -> p h t"), Qn[:], r=[Qn], w=[sc_qn])
                    sch.dma("sp", sc_qr[:, :, t0:t0 + TT].rearrange("h p t -> p h t"), Qr[:], r=[Qr], w=[sc_qr])
                    if _kstop <= 4:
                        continue
                    for h in range(8):
                        bk = nextps()
                        for kc in range(2):
                            PE(lambda: nc.tensor.matmul(bk[:], lhsT=Wkv[:, kc, 0, h, :], rhs=cl[:, 2 + kc, :],
                                                        start=(kc == 0), stop=(kc == 1)), r=[Wkv, cl], w=[bk])
                        ACT(lambda: nc.scalar.copy(out=Kn[:, h, :], in_=bk[:]), r=[bk], w=[Kn])
                    sch.dma("sp", sc_kn[:, :, t0:t0 + TT].rearrange("h p t -> p h t"), Kn[:], r=[Kn], w=[sc_kn])
                    for n in range(4):
                        for hf in range(2):
                            bk = nextps()
                            for kc in range(2):
                                PE(lambda: nc.tensor.matmul(bk[:], lhsT=cl[:, 2 + kc, n * 128:(n + 1) * 128],
                                                            rhs=Wkv[:, kc, 1, 4 * hf:4 * hf + 4, :],
                                                            start=(kc == 0), stop=(kc == 1)), r=[Wkv, cl], w=[bk])
                            DVE(lambda: nc.vector.tensor_copy(out=Vt[:, n, hf * 512:(hf + 1) * 512], in_=bk[:]),
                                r=[bk], w=[Vt])
                    sch.dma("sp", sc_v[t0:t0 + TT, :].rearrange("(n p) d -> p n d", p=128), Vt[:], r=[Vt], w=[sc_v])
                sch.barrier()

        if upto >= 2:
            SCALE = 192.0 ** -0.5
            import os as _os2
            ATT_MODE = int(_os2.environ.get("ATT_MODE", "0"))
            with ExitStack() as e1:
                Kr = sb(e1, "Kr", [128, S], BF16)
                Knh = [sb(e1, "Knh%d" % i, [128, S], BF16) for i in range(2)]
                Vh = [sb(e1, "Vh%d" % i, [128, NB, 128], BF16) for i in range(2)]
                Qnb = [sb(e1, "Qnb%d" % i, [128, TT], BF16) for i in range(2)]
                Qrb = [sb(e1, "Qrb%d" % i, [128, TT], BF16) for i in range(2)]
                NPT = 4
                pt = [sb(e1, "pt%d" % i, [128, TT], BF16) for i in range(NPT)]
                pacc = [[sb(e1, "pacc", [128, TT]) for _ in range(2)] for _ in range(2)]
                dsum = [sb(e1, "dsum", [128, TT]) for _ in range(2)]
                ones_f = sb(e1, "ones_f", [128, 128])
                DVE(lambda: nc.vector.memset(ones_f[:], 1.0), w=[ones_f])
                rcp = [sb(e1, "rcp%d" % i, [128, TT]) for i in range(2)]
                ob = [sb(e1, "ob%d" % i, [128, TT], BF16) for i in range(2)]
                NSB = 4
                LOOK = 2
                sbank = banks[0:NSB]
                accs = [(banks[4], banks[5]), (banks[6], banks[7])]
                POOL(lambda: nc.gpsimd.memset(Kr[64:128, :], 0.0), w=[Kr])
                for i_ in range(2):
                    POOL(lambda: nc.gpsimd.memset(Qrb[i_][64:128, :], 0.0), w=[Qrb[i_]])
                sch.dma("sp", Kr[0:64, :], sc_kr[:, :], r=[sc_kr], w=[Kr])

                def load_head(h):
                    sch.dma("sp", Knh[h % 2][:], sc_kn[h, :, :], r=[sc_kn], w=[Knh[h % 2]])
                    sch.dma("sp", Vh[h % 2][:], sc_v[:, h * 128:(h + 1) * 128].rearrange("(n p) d -> p n d", p=128),
                            r=[sc_v], w=[Vh[h % 2]])

                def load_q(h, qb, i):
                    sch.dma("sp", Qnb[i % 2][:], sc_qn[h, :, qb * TT:(qb + 1) * TT], r=[sc_qn], w=[Qnb[i % 2]])
                    sch.dma("sp", Qrb[i % 2][0:64, :], sc_qr[h, :, qb * TT:(qb + 1) * TT], r=[sc_qr], w=[Qrb[i % 2]])

                load_head(0)
                load_q(0, 0, 0)
                blk = 0
                ti = 0
                for h in range(8):
                    if h + 1 < 8:
                        load_head(h + 1)
                    K_ = Knh[h % 2]
                    V_ = Vh[h % 2]
                    for qb in range(NT):
                        nh, nq = (h, qb + 1) if qb + 1 < NT else (h + 1, 0)
                        if nh < 8:
                            load_q(nh, nq, blk + 1)
                        Qn_ = Qnb[blk % 2]
                        Qr_ = Qrb[blk % 2]
                        acc_o, acc_d = accs[blk % 2]
                        nk = 4 * qb + 4
                        pa = pacc[blk % 2]
                        DVE(lambda: nc.vector.memset(pa[0][:], 0.0), w=[pa[0]])
                        POOL(lambda: nc.gpsimd.memset(pa[1][:], 0.0), w=[pa[1]])

                        def qk(kb, sp_):
                            j = kb - 4 * qb
                            c0 = max(j, 0) * 128
                            ks = slice(kb * 128, (kb + 1) * 128)
                            PE(lambda: nc.tensor.matmul(sp_[:, c0:TT], lhsT=K_[:, ks], rhs=Qn_[:, c0:TT],
                                                        start=True, stop=False), r=[K_, Qn_], w=[sp_], ms=False)
                            PE(lambda: nc.tensor.matmul(sp_[:, c0:TT], lhsT=Kr[:, ks], rhs=Qr_[:, c0:TT],
                                                        start=False, stop=(j < 0)), r=[Kr, Qr_], w=[sp_], ms=(j < 0))
                            if j >= 0:
                                PE(lambda: nc.tensor.matmul(sp_[:, c0:c0 + 128], lhsT=ident_b[:], rhs=maskb[:],
                                                            start=False, stop=True), r=[ident_b, maskb], w=[sp_])
                            return c0

                        c0s = {}
                        for k0 in range(min(LOOK, nk)):
                            c0s[k0] = qk(k0, sbank[(ti + k0) % NSB])
                        for kb in range(nk):
                            sp_ = sbank[(ti + kb) % NSB]
                            p_ = pt[(ti + kb) % NPT]
                            c0 = c0s[kb]
                            ACT(lambda: nc.scalar.activation(out=p_[:, c0:TT], in_=sp_[:, c0:TT], func=AF.Exp,
                                                             scale=SCALE), r=[sp_], w=[p_])
                            if kb + LOOK < nk:
                                c0s[kb + LOOK] = qk(kb + LOOK, sbank[(ti + kb + LOOK) % NSB])
                            PE(lambda: nc.tensor.matmul(acc_o[:, c0:TT], lhsT=V_[:, kb, :], rhs=p_[:, c0:TT],
                                                        start=(kb == 0), stop=(kb == nk - 1)), r=[V_, p_], w=[acc_o])
                            if ATT_MODE == 0:
                                PE(lambda: nc.tensor.matmul(acc_d[:, c0:TT], lhsT=ones_b[:], rhs=p_[:, c0:TT],
                                                            start=(kb == 0), stop=(kb == nk - 1)), r=[ones_b, p_], w=[acc_d])
                            elif kb % 2 == 0 or ATT_MODE == 2:
                                DVE(lambda: nc.vector.tensor_tensor(out=pa[0][:, c0:TT], in0=pa[0][:, c0:TT], in1=p_[:, c0:TT],
                                                                    op=ALU.add), r=[pa[0], p_], w=[pa[0]])
                            else:
                                POOL(lambda: nc.gpsimd.tensor_tensor(out=pa[1][:, c0:TT], in0=pa[1][:, c0:TT], in1=p_[:, c0:TT],
                                                                     op=ALU.add), r=[pa[1], p_], w=[pa[1]])
                        ti += nk
                        rc = rcp[blk % 2]
                        o_ = ob[blk % 2]
                        ds_ = dsum[blk % 2]
                        if ATT_MODE != 0:
                            DVE(lambda: nc.vector.tensor_tensor(out=ds_[:], in0=pa[0][:], in1=pa[1][:], op=ALU.add),
                                r=[pa[0], pa[1]], w=[ds_])
                            PE(lambda: nc.tensor.matmul(acc_d[:], lhsT=ones_f[:], rhs=ds_[:], start=True, stop=True),
                               r=[ones_f, ds_], w=[acc_d])
                        DVE(lambda: nc.vector.reciprocal(out=rc[:], in_=acc_d[:]), r=[acc_d], w=[rc])
                        DVE(lambda: nc.vector.tensor_tensor(out=o_[:], in0=acc_o[:], in1=rc[:], op=ALU.mult),
                            r=[acc_o, rc], w=[o_])
                        sch.dma("sp", sc_o[h * 128:(h + 1) * 128, qb * TT:(qb + 1) * TT], o_[:], r=[o_], w=[sc_o])
                        blk += 1
                sch.barrier()


        def ln_tile(y, scr, scr_t, st, tmp, gcol, bcol, nb=8):
            mean, msq, lnv = st
            for c in range(8):
                DVE(lambda: nc.vector.tensor_copy(out=scr[:, c, :], in_=y[:, c, :]), r=[y], w=[scr_t])
                ACT(lambda: nc.scalar.activation(out=scr[:, 8 + c, :], in_=y[:, c, :], func=AF.Square), r=[y], w=[scr_t])
            s1 = nextps(nb)
            s2 = nextps(nb)
            for c in range(8):
                PE(lambda: nc.tensor.matmul(s1[:], lhsT=ones_b[:], rhs=scr[:, c, :], start=(c == 0), stop=(c == 7)),
                   r=[ones_b, scr_t], w=[s1], ms=(c == 7))
            for c in range(8):
                PE(lambda: nc.tensor.matmul(s2[:], lhsT=ones_b[:], rhs=scr[:, 8 + c, :], start=(c == 0), stop=(c == 7)),
                   r=[ones_b, scr_t], w=[s2], ms=(c == 7))
            ACT(lambda: nc.scalar.activation(out=mean[:], in_=s1[:], func=AF.Copy, scale=1.0 / D), r=[s1], w=[mean])
            DVE(lambda: nc.vector.tensor_tensor(out=msq[:], in0=mean[:], in1=mean[:], op=ALU.mult), r=[mean], w=[msq])
            DVE(lambda: nc.vector.scalar_tensor_tensor(out=msq[:], in0=s2[:], scalar=1.0 / D, in1=msq[:],
                                                       op0=ALU.mult, op1=ALU.subtract), r=[s2, msq], w=[msq])
            ACT(lambda: nc.scalar.activation(out=lnv[:], in_=msq[:], func=AF.Ln, bias=LN_EPS), r=[msq], w=[lnv])
            ACT(lambda: nc.scalar.activation(out=lnv[:], in_=lnv[:], func=AF.Exp, scale=-0.5), r=[lnv], w=[lnv])
            for c in range(8):
                DVE(lambda: nc.vector.tensor_tensor(out=tmp[:], in0=y[:, c, :], in1=mean[:], op=ALU.subtract),
                    r=[y, mean], w=[tmp])
                DVE(lambda: nc.vector.tensor_tensor(out=tmp[:], in0=tmp[:], in1=lnv[:], op=ALU.mult),
                    r=[tmp, lnv], w=[tmp])
                ACT(lambda: nc.scalar.activation(out=y[:, c, :], in_=tmp[:], func=AF.Identity,
                                                 scale=pv[:, gcol + c:gcol + c + 1], bias=pv[:, bcol + c:bcol + c + 1]),
                    r=[tmp, pv], w=[y])

        def fm(ap2d, t0):
            return ap2d[:, t0:t0 + TT].rearrange("(c p) t -> p c t", p=128)

        def hm(ap3d, t0):
            return ap3d[:, :, t0:t0 + TT].rearrange("h p t -> p h t")

        def wload(dst, src2d, nkc):
            v = src2d.rearrange("(c p) n -> p c n", p=128)
            for kc in range(nkc):
                sch.dma("pool", dst[:, kc, :], v[:, kc, :], w=[dst])

        def stage_proj_ln(w_o_ap, src_o, src_x, dst, gcol, bcol):
            with ExitStack() as e1:
                Wo = sb(e1, "Wo", [128, 8, D], BF16)
                wload(Wo, w_o_ap, 8)
                oT = [sb(e1, "oT%d" % i, [128, 8, TT], BF16) for i in range(2)]
                yy = [sb(e1, "yy%d" % i, [128, 8, TT]) for i in range(2)]
                scr = sb(e1, "scr", [128, 16, TT], BF16)
                st = [sb(e1, "st%d" % i, [128, TT]) for i in range(3)]
                tmp = sb(e1, "tmp", [128, TT])

                def ld(it):
                    sch.dma("sp", oT[it % 2][:], fm(src_o.t, it * TT), r=[src_o], w=[oT[it % 2]])
                    sch.dma("sp", yy[it % 2][:], fm(src_x.t, it * TT), r=[src_x], w=[yy[it % 2]])

                ld(0)
                for it in range(NT):
                    if it + 1 < NT:
                        ld(it + 1)
                    o_ = oT[it % 2]
                    y = yy[it % 2]
                    for oc in range(8):
                        bk = nextps()
                        for kc in range(8):
                            PE(lambda: nc.tensor.matmul(bk[:], lhsT=Wo[:, kc, oc * 128:(oc + 1) * 128], rhs=o_[:, kc, :],
                                                        start=(kc == 0), stop=(kc == 7)), r=[Wo, o_], w=[bk], ms=(kc == 7))
                        DVE(lambda: nc.vector.scalar_tensor_tensor(out=y[:, oc, :], in0=y[:, oc, :], scalar=ALPHA, in1=bk[:],
                                                                   op0=ALU.mult, op1=ALU.add), r=[y, bk], w=[y])
                    ln_tile(y, scr, scr, st, tmp, gcol, bcol)
                    sch.dma("sp", fm(dst.t, it * TT), y[:], r=[y], w=[dst])
                sch.barrier()

        def stage_ffn(li, src, dst, gcol, bcol):
            with ExitStack() as e1:
                Win = sb(e1, "Win", [128, 8, 2 * DFF], BF16)
                Wdn = sb(e1, "Wdn", [128, 22, D], BF16)
                wload(Win, ffn_w_in[li], 8)
                wload(Wdn, ffn_w_dn[li], 22)
                y = sb(e1, "y", [128, 8, TT])
                xb = sb(e1, "xb", [128, 8, TT], BF16)
                hh = sb(e1, "hh", [128, 22, TT], BF16)
                sg = [sb(e1, "sg%d" % i, [128, TT]) for i in range(2)]
                st = [sb(e1, "st%d" % i, [128, TT]) for i in range(3)]
                tmp = sb(e1, "tmp", [128, TT])
                for it in range(NT):
                    sch.dma("sp", y[:], fm(src.t, it * TT), r=[src], w=[y])
                    for c in range(8):
                        DVE(lambda: nc.vector.tensor_copy(out=xb[:, c, :], in_=y[:, c, :]), r=[y], w=[xb])
                    for j in range(22):
                        bg = nextps()
                        bu = nextps()
                        for kc in range(8):
                            PE(lambda: nc.tensor.matmul(bg[:], lhsT=Win[:, kc, j * 128:(j + 1) * 128], rhs=xb[:, kc, :],
                                                        start=(kc == 0), stop=(kc == 7)), r=[Win, xb], w=[bg], ms=(kc == 7))
                        for kc in range(8):
                            PE(lambda: nc.tensor.matmul(bu[:], lhsT=Win[:, kc, DFF + j * 128:DFF + (j + 1) * 128],
                                                        rhs=xb[:, kc, :], start=(kc == 0), stop=(kc == 7)),
                               r=[Win, xb], w=[bu], ms=(kc == 7))
                        s_ = sg[j % 2]
                        ACT(lambda: nc.scalar.activation(out=s_[:], in_=bg[:], func=AF.Silu), r=[bg], w=[s_])
                        DVE(lambda: nc.vector.tensor_tensor(out=hh[:, j, :], in0=s_[:], in1=bu[:], op=ALU.mult),
                            r=[s_, bu], w=[hh])
                    for oc in range(8):
                        bk = nextps()
                        for j in range(22):
                            PE(lambda: nc.tensor.matmul(bk[:], lhsT=Wdn[:, j, oc * 128:(oc + 1) * 128], rhs=hh[:, j, :],
                                                        start=(j == 0), stop=(j == 21)), r=[Wdn, hh], w=[bk], ms=(j == 21))
                        DVE(lambda: nc.vector.scalar_tensor_tensor(out=y[:, oc, :], in0=y[:, oc, :], scalar=ALPHA, in1=bk[:],
                                                                   op0=ALU.mult, op1=ALU.add), r=[y, bk], w=[y])
                    ln_tile(y, hh, hh, st, tmp, gcol, bcol)
                    sch.dma("sp", fm(dst.t, it * TT), y[:], r=[y], w=[dst])
                sch.barrier()

        def stage_ple(li, src, dst, final):
            with ExitStack() as e1:
                Wg = sb(e1, "Wg", [128, 8, D], BF16)
                Wp = sb(e1, "Wp", [128, 2, D], BF16)
                wload(Wg, ple_w_gate[li], 8)
                wload(Wp, ple_w_proj[li], 2)
                if not final:
                    Wh = sb(e1, "Wh", [128, 8, 4096], BF16)
                    wload(Wh, hg_w_in, 8)
                y = sb(e1, "y", [128, 8, TT])
                xb = sb(e1, "xb", [128, 8, TT], BF16)
                ptok = sb(e1, "ptok", [128, 4, 256])
                pT = sb(e1, "pT", [128, 2, TT], BF16)
                sg = [sb(e1, "sg%d" % i, [128, TT]) for i in range(2)]
                tmp = sb(e1, "tmp", [128, TT])
                if final:
                    otok = sb(e1, "otok", [128, 4, D])
                else:
                    FS = [dict((nm, sb(e1, "f_" + nm, [128, TT])) for nm in
                               ("sq", "ft", "kk", "lf", "G", "d1", "d2", "e0", "e1", "e2", "e3")) for _ in range(2)]
                    HO = [dict((nm, sb(e1, "ho_" + nm, [128, TT], BF16)) for nm in ("qa", "qs", "ka", "kh", "gt"))
                          for _ in range(2)]
                    HD = sb(e1, "HD", [128, 8, 4])
                    HV = sb(e1, "HV", [128, 4, D], BF16)
                for it in range(NT):
                    t0 = it * TT
                    sch.dma("sp", y[:], fm(src.t, t0), r=[src], w=[y])
                    sch.dma("sp", ptok[:], p_in[li, t0:t0 + TT, :].rearrange("(n p) d -> p n d", p=128), w=[ptok])
                    for c in range(8):
                        DVE(lambda: nc.vector.tensor_copy(out=xb[:, c, :], in_=y[:, c, :]), r=[y], w=[xb])
                    for c2 in range(2):
                        bk = nextps()
                        for n in range(4):
                            PE(lambda: nc.tensor.transpose(out=bk[:, n * 128:(n + 1) * 128],
                                                           in_=ptok[:, n, c2 * 128:(c2 + 1) * 128], identity=ident_f),
                               r=[ptok, cst], w=[bk], ms=(n == 3))
                        ACT(lambda: nc.scalar.copy(out=pT[:, c2, :], in_=bk[:]), r=[bk], w=[pT])
                    for oc in range(8):
                        bg = nextps()
                        bp = nextps()
                        for kc in range(8):
                            PE(lambda: nc.tensor.matmul(bg[:], lhsT=Wg[:, kc, oc * 128:(oc + 1) * 128], rhs=xb[:, kc, :],
                                                        start=(kc == 0), stop=(kc == 7)), r=[Wg, xb], w=[bg], ms=(kc == 7))
                        for kc in range(2):
                            PE(lambda: nc.tensor.matmul(bp[:], lhsT=Wp[:, kc, oc * 128:(oc + 1) * 128], rhs=pT[:, kc, :],
                                                        start=(kc == 0), stop=(kc == 1)), r=[Wp, pT], w=[bp], ms=(kc == 1))
                        s_ = sg[oc % 2]
                        ACT(lambda: nc.scalar.activation(out=s_[:], in_=bg[:], func=AF.Sigmoid), r=[bg], w=[s_])
                        DVE(lambda: nc.vector.tensor_tensor(out=tmp[:], in0=s_[:], in1=bp[:], op=ALU.mult),
                            r=[s_, bp], w=[tmp])
                        DVE(lambda: nc.vector.tensor_tensor(out=y[:, oc, :], in0=y[:, oc, :], in1=tmp[:], op=ALU.add),
                            r=[y, tmp], w=[y])
                    if final:
                        for n in range(4):
                            for hf in range(2):
                                bk = nextps()
                                for c in range(4):
                                    PE(lambda: nc.tensor.transpose(out=bk[:, c * 128:(c + 1) * 128],
                                                                   in_=y[:, hf * 4 + c, n * 128:(n + 1) * 128],
                                                                   identity=ident_f), r=[y, cst], w=[bk], ms=(c == 3))
                                ACT(lambda: nc.scalar.copy(out=otok[:, n, hf * 512:(hf + 1) * 512], in_=bk[:]),
                                    r=[bk], w=[otok])
                        sch.dma("sp", out[t0:t0 + TT, :].rearrange("(n p) d -> p n d", p=128), otok[:], r=[otok], w=[dst])
                        continue
                    sch.dma("sp", fm(dst.t, t0), y[:], r=[y], w=[dst])
                    for c in range(8):
                        DVE(lambda: nc.vector.tensor_copy(out=xb[:, c, :], in_=y[:, c, :]), r=[y], w=[xb])

                    def proj(col0):
                        bk = nextps()
                        for kc in range(8):
                            PE(lambda: nc.tensor.matmul(bk[:], lhsT=Wh[:, kc, col0:col0 + 128], rhs=xb[:, kc, :],
                                                        start=(kc == 0), stop=(kc == 7)), r=[Wh, xb], w=[bk], ms=(kc == 7))
                        return bk

                    def bc(tl, pos_):
                        base = tl.t[:, pos_:pos_ + 1]
                        return bass.AP(base.tensor, base.offset, [[TT, 128], [128, 4], [0, 128]])

                    def v3(tl):
                        return tl[:].rearrange("p (c t) -> p c t", c=4)

                    for h in range(8):
                        F = FS[h % 2]
                        O = HO[h % 2]
                        bq = proj(h * 128)
                        ACT(lambda: nc.scalar.activation(out=F["sq"][:], in_=bq[:], func=AF.Silu), r=[bq], w=[F["sq"]])
                        bf = proj(1024 + h * 128)
                        ACT(lambda: nc.scalar.activation(out=F["ft"][:], in_=bf[:], func=AF.Sigmoid), r=[bf], w=[F["ft"]])
                        bgt = proj(3072 + h * 128)
                        ACT(lambda: nc.scalar.activation(out=O["gt"][:], in_=bgt[:], func=AF.Silu), r=[bgt], w=[O["gt"]])
                        DVE(lambda: nc.vector.tensor_scalar(out=F["ft"][:], in0=F["ft"][:], scalar1=lbv[:, 8 + h:9 + h],
                                                            scalar2=lbv[:, h:h + 1], op0=ALU.mult, op1=ALU.add),
                            r=[F["ft"], lbv], w=[F["ft"]])
                        ACT(lambda: nc.scalar.activation(out=F["lf"][:], in_=F["ft"][:], func=AF.Ln), r=[F["ft"]], w=[F["lf"]])
                        DVE(lambda: nc.vector.tensor_scalar(out=F["kk"][:], in0=F["ft"][:], scalar1=-1.0, scalar2=1.0,
                                                            op0=ALU.mult, op1=ALU.add), r=[F["ft"]], w=[F["kk"]])
                        DVE(lambda: nc.vector.tensor_tensor_scan(out=F["G"][:], data0=cst[:, 768:1280], data1=F["lf"][:],
                                                                 initial=0.0, op0=ALU.mult, op1=ALU.add),
                            r=[cst, F["lf"]], w=[F["G"]])
                        ACT(lambda: nc.scalar.activation(out=F["e0"][:], in_=F["G"][:], func=AF.Exp), r=[F["G"]], w=[F["e0"]])
                        DVE(lambda: nc.vector.tensor_tensor(out=v3(F["d1"]), in0=v3(F["G"]), in1=bc(F["G"], 63), op=ALU.subtract),
                            r=[F["G"]], w=[F["d1"]])
                        DVE(lambda: nc.vector.tensor_tensor(out=v3(F["d2"]), in0=bc(F["G"], 127), in1=v3(F["G"]), op=ALU.subtract),
                            r=[F["G"]], w=[F["d2"]])
                        ACT(lambda: nc.scalar.activation(out=F["e1"][:], in_=F["d1"][:], func=AF.Exp), r=[F["d1"]], w=[F["e1"]])
                        ACT(lambda: nc.scalar.activation(out=F["e2"][:], in_=F["d1"][:], func=AF.Exp, scale=-1.0),
                            r=[F["d1"]], w=[F["e2"]])
                        ACT(lambda: nc.scalar.activation(out=F["e3"][:], in_=F["d2"][:], func=AF.Exp), r=[F["d2"]], w=[F["e3"]])
                        DVE(lambda: nc.vector.tensor_tensor(out=O["qs"][:], in0=F["sq"][:], in1=F["e0"][:], op=ALU.mult),
                            r=[F["sq"], F["e0"]], w=[O["qs"]])
                        DVE(lambda: nc.vector.tensor_copy(out=HD[:, h, :], in_=F["e0"][:, 127:TT:128]), r=[F["e0"]], w=[HD])
                        POOL(lambda: nc.gpsimd.tensor_tensor(out=O["qa"][:], in0=F["sq"][:], in1=F["e1"][:], op=ALU.mult),
                             r=[F["sq"], F["e1"]], w=[O["qa"]])
                        DVE(lambda: nc.vector.tensor_tensor(out=O["ka"][:], in0=F["kk"][:], in1=F["e2"][:], op=ALU.mult),
                            r=[F["kk"], F["e2"]], w=[O["ka"]])
                        POOL(lambda: nc.gpsimd.tensor_tensor(out=O["kh"][:], in0=F["kk"][:], in1=F["e3"][:], op=ALU.mult),
                             r=[F["kk"], F["e3"]], w=[O["kh"]])
                        for nm, dstt in (("qa", sc_hqa), ("qs", sc_hqs), ("ka", sc_hka), ("kh", sc_hkh), ("gt", sc_hg)):
                            sch.dma("sp", dstt[h, :, t0:t0 + TT], O[nm][:], r=[O[nm]], w=[dstt])
                    for n in range(4):
                        for hf in range(2):
                            bk = nextps()
                            for kc in range(8):
                                PE(lambda: nc.tensor.matmul(bk[:], lhsT=xb[:, kc, n * 128:(n + 1) * 128],
                                                            rhs=Wh[:, kc, 2048 + hf * 512:2048 + (hf + 1) * 512],
                                                            start=(kc == 0), stop=(kc == 7)), r=[Wh, xb], w=[bk], ms=(kc == 7))
                            DVE(lambda: nc.vector.tensor_copy(out=HV[:, n, hf * 512:(hf + 1) * 512], in_=bk[:]),
                                r=[bk], w=[HV])
                    sch.dma("sp", sc_hd[:, :, it * 4:(it + 1) * 4], HD[:], r=[HD], w=[sc_hd])
                    sch.dma("sp", sc_hv[t0:t0 + TT, :].rearrange("(n p) d -> p n d", p=128), HV[:], r=[HV], w=[sc_hv])
                sch.barrier()

        def stage_hgrn(src_x, dst, gcol, bcol):
            with ExitStack() as e1:
                Wo = sb(e1, "Who", [128, 8, D], BF16)
                wload(Wo, hg_w_o, 8)
                for kc in range(8):
                    DVE(lambda: nc.vector.tensor_scalar(out=Wo[:, kc, :], in0=Wo[:, kc, :], scalar1=pv[:, c_on + kc:c_on + kc + 1],
                                                        scalar2=None, op0=ALU.mult), r=[Wo, pv], w=[Wo])
                um = sb(e1, "um", [128, 128])
                DVE(lambda: nc.vector.tensor_scalar(out=um[:], in0=cst[:, 128:256], scalar1=-1.0, scalar2=None,
                                                    op0=ALU.is_gt), r=[cst], w=[um])
                um_bc = bass.AP(um.t[:].tensor, um.t[:].offset, [[128, 128], [0, 4], [1, 128]])
                NBUF = 2
                HQA = [sb(e1, "HQA", [128, 8, TT], BF16) for _ in range(NBUF)]
                HQS = [sb(e1, "HQS", [128, 8, TT], BF16) for _ in range(NBUF)]
                HKA = [sb(e1, "HKA", [128, 8, TT], BF16) for _ in range(NBUF)]
                HKH = [sb(e1, "HKH", [128, 8, TT], BF16) for _ in range(NBUF)]
                HGt = [sb(e1, "HGt", [128, 8, TT], BF16) for _ in range(NBUF)]
                HD = [sb(e1, "HD", [128, 8, 4]) for _ in range(NBUF)]
                HV = [sb(e1, "HV", [128, 4, D], BF16) for _ in range(NBUF)]
                yy = [sb(e1, "y", [128, 8, TT]) for _ in range(1)]
                OG = sb(e1, "OG", [128, 8, TT], BF16)
                S32 = [sb(e1, "S32", [128, 4, 128]) for _ in range(2)]
                Sb = [sb(e1, "Sb", [128, 4, 128], BF16) for _ in range(2)]
                khT = [sb(e1, "khT", [128, 4, 128], BF16) for _ in range(8)]
                ATb = [sb(e1, "ATb", [128, 4, 128], BF16) for _ in range(8)]
                osq = [sb(e1, "osq", [128, TT], BF16) for _ in range(2)]
                on = [sb(e1, "on", [128, TT]) for _ in range(2)]
                lnr = [sb(e1, "lnr", [128, TT]) for _ in range(2)]
                scr = sb(e1, "scr", [128, 16, TT], BF16)
                st = [sb(e1, "st", [128, TT]) for i in range(3)]
                tmp = sb(e1, "tmp", [128, TT])
                for g in range(2):
                    DVE(lambda: nc.vector.memset(S32[g][:], 0.0), w=[S32[g]])
                    DVE(lambda: nc.vector.memset(Sb[g][:], 0.0), w=[Sb[g]])

                def ld(it):
                    t0 = it * TT
                    i = it % NBUF
                    sch.dma("sp", HKH[i][:], hm(sc_hkh.t, t0), r=[sc_hkh], w=[HKH[i]])
                    sch.dma("sp", HKA[i][:], hm(sc_hka.t, t0), r=[sc_hka], w=[HKA[i]])
                    sch.dma("sp", HQA[i][:], hm(sc_hqa.t, t0), r=[sc_hqa], w=[HQA[i]])
                    sch.dma("sp", HQS[i][:], hm(sc_hqs.t, t0), r=[sc_hqs], w=[HQS[i]])
                    sch.dma("sp", HV[i][:], sc_hv[t0:t0 + TT, :].rearrange("(n p) d -> p n d", p=128), r=[sc_hv], w=[HV[i]])
                    sch.dma("sp", HD[i][:], sc_hd[:, :, it * 4:(it + 1) * 4], r=[sc_hd], w=[HD[i]])
                    sch.dma("sp", HGt[i][:], hm(sc_hg.t, t0), r=[sc_hg], w=[HGt[i]])

                ld(0)
                kk = 0
                for it in range(NT):
                    t0 = it * TT
                    if it + 1 < NT:
                        ld(it + 1)
                    i = it % NBUF
                    qa, qs, ka, kh, gt, hd, hv, y = HQA[i], HQS[i], HKA[i], HKH[i], HGt[i], HD[i], HV[i], yy[0]
                    sch.dma("sp", y[:], fm(src_x.t, t0), r=[src_x], w=[y])
                    for h in range(8):
                        bt = nextps(4)
                        for c in range(4):
                            cs = slice(c * 128, (c + 1) * 128)
                            PE(lambda: nc.tensor.matmul(bt[:, cs], lhsT=kh[:, h, cs], rhs=ident_b[:], start=True, stop=True),
                               r=[kh, ident_b], w=[bt], ms=(c == 3))
                        ACT(lambda: nc.scalar.copy(out=khT[h][:].rearrange("p c d -> p (c d)"), in_=bt[:]), r=[bt], w=[khT[h]])
                        ba = nextps(4)
                        for c in range(4):
                            cs = slice(c * 128, (c + 1) * 128)
                            PE(lambda: nc.tensor.matmul(ba[:, cs], lhsT=ka[:, h, cs], rhs=qa[:, h, cs], start=True, stop=True),
                               r=[ka, qa], w=[ba], ms=(c == 3))
                        DVE(lambda: nc.vector.tensor_tensor(out=ATb[h][:], in0=ba[:].rearrange("p (c t) -> p c t", c=4),
                                                            in1=um_bc, op=ALU.mult), r=[ba, um], w=[ATb[h]])
                    for c in range(4):
                        cs = slice(c * 128, (c + 1) * 128)
                        for g in range(2):
                            ob_ = banks[4 + (kk % 2)]
                            bd = banks[6]
                            j2 = kk % 2
                            kk += 1
                            for j in range(4):
                                h = 4 * g + j
                                js = slice(j * 128, (j + 1) * 128)
                                hs = slice(h * 128, (h + 1) * 128)
                                PE(lambda: nc.tensor.matmul(ob_[:, js], lhsT=Sb[g][:, j, :], rhs=qs[:, h, cs], start=True, stop=False),
                                   r=[Sb[g], qs], w=[ob_], ms=False)
                                PE(lambda: nc.tensor.matmul(ob_[:, js], lhsT=hv[:, c, hs], rhs=ATb[h][:, c, :], start=False, stop=True),
                                   r=[hv, ATb[h]], w=[ob_], ms=(j == 3))
                            for j in range(4):
                                h = 4 * g + j
                                js = slice(j * 128, (j + 1) * 128)
                                hs = slice(h * 128, (h + 1) * 128)
                                PE(lambda: nc.tensor.matmul(bd[:, js], lhsT=khT[h][:, c, :], rhs=hv[:, c, hs], start=True, stop=True),
                                   r=[khT[h], hv], w=[bd], ms=(j == 3))
                            dec = bass.AP(hd.t[:].tensor, hd.t[:, 4 * g, c:c + 1].offset, [[32, 128], [4, 4], [0, 128]])
                            DVE(lambda: nc.vector.tensor_tensor(out=S32[g][:], in0=S32[g][:], in1=dec, op=ALU.mult),
                                r=[S32[g], hd], w=[S32[g]])
                            DVE(lambda: nc.vector.tensor_tensor(out=S32[g][:], in0=S32[g][:],
                                                                in1=bd[:].rearrange("p (j e) -> p j e", j=4), op=ALU.add),
                                r=[S32[g], bd], w=[S32[g]])
                            ACT(lambda: nc.scalar.copy(out=Sb[g][:], in_=S32[g][:]), r=[S32[g]], w=[Sb[g]])
                            ACT(lambda: nc.scalar.activation(out=osq[j2][:], in_=ob_[:], func=AF.Square), r=[ob_], w=[osq[j2]])
                            bs = banks[7]
                            PE(lambda: nc.tensor.matmul(bs[:], lhsT=ones_b[:], rhs=osq[j2][:], start=True, stop=True),
                               r=[ones_b, osq[j2]], w=[bs])
                            ACT(lambda: nc.scalar.activation(out=lnr[j2][:], in_=bs[:], func=AF.Ln, scale=1.0 / 128.0, bias=RMS_EPS),
                                r=[bs], w=[lnr[j2]])
                            ACT(lambda: nc.scalar.activation(out=lnr[j2][:], in_=lnr[j2][:], func=AF.Exp, scale=-0.5),
                                r=[lnr[j2]], w=[lnr[j2]])
                            DVE(lambda: nc.vector.tensor_tensor(out=on[j2][:], in0=ob_[:], in1=lnr[j2][:], op=ALU.mult),
                                r=[ob_, lnr[j2]], w=[on[j2]])
                            POOL(lambda: nc.gpsimd.tensor_tensor(out=OG[:, 4 * g:4 * g + 4, cs],
                                                                 in0=on[j2][:].rearrange("p (j t) -> p j t", j=4),
                                                                 in1=gt[:, 4 * g:4 * g + 4, cs], op=ALU.mult),
                                 r=[on[j2], gt], w=[OG])
                    if dbg_og is not None:
                        sch.dma("sp", fm(dbg_og.t, t0), OG[:], r=[OG], w=[dbg_og])
                    for oc in range(8):
                        bk = nextps(4)
                        for kc in range(8):
                            PE(lambda: nc.tensor.matmul(bk[:], lhsT=Wo[:, kc, oc * 128:(oc + 1) * 128], rhs=OG[:, kc, :],
                                                        start=(kc == 0), stop=(kc == 7)), r=[Wo, OG], w=[bk], ms=(kc == 7))
                        DVE(lambda: nc.vector.scalar_tensor_tensor(out=y[:, oc, :], in0=y[:, oc, :], scalar=ALPHA, in1=bk[:],
                                                                   op0=ALU.mult, op1=ALU.add), r=[y, bk], w=[y])
                    ln_tile(y, scr, scr, st, tmp, gcol, bcol, nb=4)
                    sch.dma("sp", fm(dst.t, t0), y[:], r=[y], w=[dst])
                sch.barrier()

        out_t = T(out)
        if upto >= 3:
            stage_proj_ln(mla_w_o, sc_o, sc_xT, sc_x1, c_lng[0][0], c_lnb[0][0])
        if upto >= 4:
            stage_ffn(0, sc_x1, sc_xT, c_lng[0][1], c_lnb[0][1])
        if upto >= 5:
            stage_ple(0, sc_xT, sc_x1, False)
        if upto >= 6:
            stage_hgrn(sc_x1, sc_xT, c_lng[1][0], c_lnb[1][0])
        if upto >= 7:
            stage_ffn(1, sc_xT, sc_x1, c_lng[1][1], c_lnb[1][1])
        if upto >= 8:
            stage_ple(1, sc_x1, out_t, True)

        sch.barrier()
    return nc


def make_consts():
    c = np.zeros((128, 1408), np.float32)
    c[:, 0:128] = np.eye(128, dtype=np.float32)
    k = np.arange(128)[:, None]
    q = np.arange(128)[None, :]
    c[:, 128:256] = np.where(k <= q, 0.0, -30000.0)
    s = np.arange(64)[:, None]
    t = np.arange(64)[None, :]
    c[0:64, 256:768] = np.tile((s <= t).astype(np.float32), (1, 8))
    c[:, 768:1280] = (np.arange(512) % 128 != 0).astype(np.float32)[None, :]
    inv = (10000.0 ** (-np.arange(0, 64, 2, dtype=np.float32) / 64.0)).astype(np.float32)
    c[0:64, 1280] = np.concatenate([inv, inv])
    c[0:64, 1281] = np.concatenate([-inv, inv])
    return c


_W_NAMES = ["mla_w_dqkv", "mla_w_uq", "mla_w_ukv", "mla_w_o", "hgrn_w_in", "hgrn_w_o"]
_W_FULL = ["ffn_w_in", "ffn_w_down", "ple_w_proj", "ple_w_gate"]


def make_pvec(inputs):
    def col(v):
        return np.asarray(v, dtype=np.float32).reshape(-1, 128).T
    cols = [col(inputs["mla_q_norm"][0]), col(inputs["mla_kv_norm"][0])]
    for nm in ("ln_mix_g", "ln_ffn_g"):
        pass
    for i in range(2):
        cols += [col(inputs["ln_mix_g"][i]), col(inputs["ln_ffn_g"][i])]
    for i in range(2):
        cols += [col(inputs["ln_mix_b"][i]), col(inputs["ln_ffn_b"][i])]
    cols += [col(inputs["hgrn_out_norm"][0]), col(inputs["hgrn_lb_logits"][0]), col(inputs["hgrn_lb_logits"][1])]
    pv = np.concatenate(cols, axis=1)
    out = np.zeros((128, 96), np.float32)
    out[:, :pv.shape[1]] = pv
    return out


def make_in_maps(inputs, n_cores, S):
    consts = make_consts()
    shared = {"consts": consts, "pvec": make_pvec(inputs)}
    for k in _W_NAMES:
        shared[k] = np.ascontiguousarray(np.asarray(inputs[k], dtype=np.float32)[0])
    for k in _W_FULL:
        shared[k] = np.ascontiguousarray(np.asarray(inputs[k], dtype=np.float32))
    maps = []
    xs = np.asarray(inputs["x"])
    ps = np.asarray(inputs["p"])
    po = np.asarray(inputs["positions"])
    for b in range(n_cores):
        m = dict(shared)
        m["x"] = np.ascontiguousarray(xs[b, :S])
        m["p"] = np.ascontiguousarray(ps[:, b, :S])
        m["pos"] = np.ascontiguousarray(po[b:b + 1, :S]).astype(np.int32)
        maps.append(m)
    return maps


def kernel(**inputs):
    n = 8
    S = 8192
    nc = build(S)
    maps = make_in_maps(inputs, n, S)
    res = run_bass_kernel_spmd(nc, maps, core_ids=list(range(n)))
    return np.stack([np.asarray(r["out"]) for r in res.results], axis=0).astype(np.float32)
```

```python
import math
from contextlib import ExitStack

import numpy as np
import concourse.bass as bass
import concourse.mybir as mybir
from concourse.bass_utils import run_bass_kernel_spmd

F32 = mybir.dt.float32
BF16 = mybir.dt.bfloat16
I32 = mybir.dt.int32
AF = mybir.ActivationFunctionType
ALU = mybir.AluOpType

D = 1024
DFF = 2816
ALPHA = 4.0 ** 0.25
LN_EPS = 1e-5
RMS_EPS = 1e-6
TT = 512


class Buf:
    __slots__ = ("w", "r")

    def __init__(self):
        self.w = None
        self.r = {}


class T:
    def __init__(self, t, excl=False):
        self.t = t
        self.b = Buf()
        self.excl = excl

    def __getitem__(self, k):
        return self.t[k]


class Sched:
    def __init__(self, nc, es):
        self.nc = nc
        self.eng = {"pe": nc.tensor, "act": nc.scalar, "dve": nc.vector, "pool": nc.gpsimd, "sp": nc.sync}
        self.sems = {}
        self.cnt = {}
        self.seen = {k: {} for k in self.eng}
        for k in self.eng:
            self.sems[k] = es.enter_context(nc.semaphore("s_" + k))
            self.cnt[k] = 0
        self.ring = {}
        for q, n in (("sp", 8), ("pool", 8), ("act", 4)):
            lst = []
            for i in range(n):
                key = (q, i)
                self.sems[key] = es.enter_context(nc.semaphore("d_%s%d" % (q, i)))
                self.cnt[key] = 0
                lst.append(key)
            self.ring[q] = [lst, 0]
        self.nops = 0

    def _need(self, e, reads, writes):
        need = {}

        def add(ev, raw):
            if ev is None:
                return
            key, val = ev
            if key == e and e == "pe":
                return
            if need.get(key, 0) < val:
                need[key] = val

        for b in reads:
            add(b.b.w, True)
            if b.excl:
                for k, v in b.b.r.items():
                    if k != e:
                        add((k, v), False)
        for b in writes:
            add(b.b.w, False)
            for k, v in b.b.r.items():
                add((k, v), False)
        seen = self.seen[e]
        for key, val in need.items():
            assert val <= self.cnt[key], ("wait on a milestone not yet emitted", e, key, val, self.cnt[key])
            if seen.get(key, 0) < val:
                self.eng[e].wait_ge(self.sems[key], val)
                seen[key] = val

    def _record(self, ev, reads, writes):
        key, val = ev
        for b in reads:
            b.b.r[key] = val
        for b in writes:
            b.b.w = ev
            b.b.r = {}

    def op(self, e, fn, r=(), w=(), ms=True):
        self._need(e, r, w)
        inst = fn()
        val = self.cnt[e] + 1
        if ms:
            inst.then_inc(self.sems[e], 1)
            self.cnt[e] = val
        self._record((e, val), r, w)
        self.nops += 1
        return inst

    def dma(self, q, out, in_, r=(), w=()):
        self._need(q, r, w)
        lst, i = self.ring[q]
        key = lst[i % len(lst)]
        self.ring[q][1] = i + 1
        prior = self.cnt[key]
        if prior > 0 and self.seen[q].get(key, 0) < prior:
            self.eng[q].wait_ge(self.sems[key], prior)
            self.seen[q][key] = prior
        inst = self.eng[q].dma_start(out=out, in_=in_)
        inst.then_inc(self.sems[key], 16)
        self.cnt[key] = prior + 16
        self._record((key, prior + 16), r, w)
        self.nops += 1
        return inst

    def barrier(self):
        for e in self.eng:
            for key, val in self.cnt.items():
                if key == e or val == 0:
                    continue
                if self.seen[e].get(key, 0) < val:
                    self.eng[e].wait_ge(self.sems[key], val)
                    self.seen[e][key] = val


def build(S=8192, upto=99, debug=False):
    nc = bass.Bass("TRN2", target_bir_lowering=False)
    NT = S // TT
    NB = S // 128
    dbg_kind = "ExternalOutput" if debug else "Internal"

    def dram(name, shape, dt, kind=None):
        return nc.dram_tensor(name, list(shape), dt, kind=kind or dbg_kind).ap()

    def din(name, shape, dt=F32):
        return dram(name, shape, dt, "ExternalInput")

    x = din("x", [S, D])
    p_in = din("p", [2, S, 256])
    pos = din("pos", [1, S], I32)
    consts = din("consts", [128, 1408])
    pvec = din("pvec", [128, 96])
    w_dqkv = din("mla_w_dqkv", [D, 576])
    w_uq = din("mla_w_uq", [256, 1536])
    w_ukv = din("mla_w_ukv", [256, 2048])
    mla_w_o = din("mla_w_o", [D, D])
    hg_w_in = din("hgrn_w_in", [D, 4096])
    hg_w_o = din("hgrn_w_o", [D, D])
    ffn_w_in = din("ffn_w_in", [2, D, 2 * DFF])
    ffn_w_dn = din("ffn_w_down", [2, DFF, D])
    ple_w_proj = din("ple_w_proj", [2, 256, D])
    ple_w_gate = din("ple_w_gate", [2, D, D])
    out = dram("out", [S, D], F32, "ExternalOutput")

    sc_cc = T(dram("sc_cc", [64, S], F32))
    sc_ss = T(dram("sc_ss", [64, S], F32))
    sc_xT = T(dram("sc_xT", [D, S], F32))
    sc_x1 = T(dram("sc_x1", [D, S], F32))
    sc_qn = T(dram("sc_qn", [8, 128, S], BF16))
    sc_qr = T(dram("sc_qr", [8, 64, S], BF16))
    sc_kn = T(dram("sc_kn", [8, 128, S], BF16))
    sc_kr = T(dram("sc_kr", [64, S], BF16))
    sc_v = T(dram("sc_v", [S, D], BF16))
    sc_o = T(dram("sc_o", [D, S], BF16))
    sc_hqa = T(dram("sc_hqa", [8, 128, S], BF16))
    sc_hqs = T(dram("sc_hqs", [8, 128, S], BF16))
    sc_hka = T(dram("sc_hka", [8, 128, S], BF16))
    sc_hkh = T(dram("sc_hkh", [8, 128, S], BF16))
    sc_hg = T(dram("sc_hg", [8, 128, S], BF16))
    sc_hv = T(dram("sc_hv", [S, D], BF16))
    sc_hd = T(dram("sc_hd", [128, 8, S // 128], F32))
    dbg_og = T(dram("dbg_og", [D, S], BF16)) if debug else None

    es = ExitStack()
    with es:
        sch = Sched(nc, es)
        PE = lambda fn, r=(), w=(), ms=True: sch.op("pe", fn, r, w, ms)
        ACT = lambda fn, r=(), w=(): sch.op("act", fn, r, w)
        DVE = lambda fn, r=(), w=(): sch.op("dve", fn, r, w)
        POOL = lambda fn, r=(), w=(): sch.op("pool", fn, r, w)

        uniq = [0]

        def sb(es_, name, shape, dt=F32):
            uniq[0] += 1
            return T(es_.enter_context(nc.sbuf_tensor("%s_%d" % (name, uniq[0]), list(shape), dt)))

        banks = [T(es.enter_context(nc.psum_tensor("ps%d" % i, [128, 512], F32)), excl=True) for i in range(8)]
        bank_i = [0]

        def nextps(n=8):
            b = banks[bank_i[0] % n]
            bank_i[0] += 1
            return b

        cst = sb(es, "cst", [128, 1408])
        sch.dma("sp", cst[:], consts[:, :], w=[cst])
        ident_f = cst[:, 0:128]
        ident_b = sb(es, "ident_b", [128, 128], BF16)
        maskb = sb(es, "maskb", [128, 128], BF16)
        amask = sb(es, "amask", [64, 512], BF16)
        ones_b = sb(es, "ones_b", [128, 128], BF16)
        DVE(lambda: nc.vector.tensor_copy(out=ident_b[:], in_=cst[:, 0:128]), r=[cst], w=[ident_b])
        DVE(lambda: nc.vector.tensor_copy(out=maskb[:], in_=cst[:, 128:256]), r=[cst], w=[maskb])
        DVE(lambda: nc.vector.tensor_copy(out=amask[:], in_=cst[0:64, 256:768]), r=[cst], w=[amask])
        DVE(lambda: nc.vector.memset(ones_b[:], 1.0), w=[ones_b])
        rmask = cst[:, 768:1280]
        pv = sb(es, "pv", [128, 96])
        sch.dma("sp", pv[:], pvec[:, :], w=[pv])
        c_qn, c_kvn = 0, 2
        c_lng = [[4, 12], [20, 28]]
        c_lnb = [[36, 44], [52, 60]]
        c_on, c_l0, c_l1 = 68, 76, 84
        lbv = sb(es, "lbv", [128, 24])
        DVE(lambda: nc.vector.tensor_tensor(out=lbv[:, 16:24], in0=pv[:, c_l1:c_l1 + 8], in1=pv[:, c_l0:c_l0 + 8],
                                            op=ALU.subtract), r=[pv], w=[lbv])
        ACT(lambda: nc.scalar.activation(out=lbv[:, 0:8], in_=lbv[:, 16:24], func=AF.Sigmoid), r=[lbv], w=[lbv])
        DVE(lambda: nc.vector.tensor_scalar(out=lbv[:, 8:16], in0=lbv[:, 0:8], scalar1=-1.0, scalar2=1.0,
                                            op0=ALU.mult, op1=ALU.add), r=[lbv], w=[lbv])

        TWO_PI = 2.0 * math.pi
        C1 = 6.28125
        C2 = TWO_PI - C1
        with ExitStack() as e1:
            RC = min(S, 2048)
            pi_ = sb(e1, "r_pi", [64, RC], I32)
            pf = sb(e1, "r_pf", [64, RC])
            for tab in range(2):
                kf = sb(e1, "r_kf%d" % tab, [64, RC])
                ki = sb(e1, "r_ki%d" % tab, [64, RC], I32)
                ang = sb(e1, "r_ang%d" % tab, [64, RC])
                tm = sb(e1, "r_tm%d" % tab, [64, RC])
                res = sb(e1, "r_res%d" % tab, [64, RC])
                for c0 in range(0, S, RC):
                    if tab == 0:
                        pass
                    src = bass.AP(pos.tensor, c0, [[0, 64], [1, RC]])
                    sch.dma("sp", pi_[:], src, w=[pi_])
                    DVE(lambda: nc.vector.tensor_copy(out=pf[:], in_=pi_[:]), r=[pi_], w=[pf])
                    fcol = cst[0:64, 1280 + tab:1281 + tab]
                    DVE(lambda: nc.vector.tensor_scalar(out=ang[:], in0=pf[:], scalar1=fcol, scalar2=None,
                                                        op0=ALU.mult), r=[pf, cst], w=[ang])
                    DVE(lambda: nc.vector.tensor_scalar(out=kf[:], in0=ang[:], scalar1=1.0 / TWO_PI, scalar2=None,
                                                        op0=ALU.mult), r=[ang], w=[kf])
                    DVE(lambda: nc.vector.tensor_copy(out=ki[:], in_=kf[:]), r=[kf], w=[ki])
                    DVE(lambda: nc.vector.tensor_copy(out=kf[:], in_=ki[:]), r=[ki], w=[kf])
                    DVE(lambda: nc.vector.scalar_tensor_tensor(out=tm[:], in0=kf[:], scalar=-C1, in1=ang[:],
                                                               op0=ALU.mult, op1=ALU.add), r=[kf, ang], w=[tm])
                    DVE(lambda: nc.vector.scalar_tensor_tensor(out=ang[:], in0=kf[:], scalar=-C2, in1=tm[:],
                                                               op0=ALU.mult, op1=ALU.add), r=[kf, tm], w=[ang])
                    if tab == 0:
                        DVE(lambda: nc.vector.tensor_scalar(out=ang[:], in0=ang[:], scalar1=math.pi / 2, scalar2=None,
                                                            op0=ALU.add), r=[ang], w=[ang])
                    for _ in range(2):
                        DVE(lambda: nc.vector.tensor_scalar(out=tm[:], in0=ang[:], scalar1=math.pi, scalar2=-TWO_PI,
                                                            op0=ALU.is_gt, op1=ALU.mult), r=[ang], w=[tm])
                        DVE(lambda: nc.vector.tensor_tensor(out=ang[:], in0=ang[:], in1=tm[:], op=ALU.add),
                            r=[ang, tm], w=[ang])
                        DVE(lambda: nc.vector.tensor_scalar(out=tm[:], in0=ang[:], scalar1=-math.pi, scalar2=TWO_PI,
                                                            op0=ALU.is_lt, op1=ALU.mult), r=[ang], w=[tm])
                        DVE(lambda: nc.vector.tensor_tensor(out=ang[:], in0=ang[:], in1=tm[:], op=ALU.add),
                            r=[ang, tm], w=[ang])
                    DVE(lambda: nc.vector.tensor_scalar(out=ang[:], in0=ang[:], scalar1=3.14159, scalar2=-3.14159,
                                                        op0=ALU.min, op1=ALU.max), r=[ang], w=[ang])
                    ACT(lambda: nc.scalar.activation(out=res[:], in_=ang[:], func=AF.Sin), r=[ang], w=[res])
                    dst = sc_cc if tab == 0 else sc_ss
                    sch.dma("sp", dst[:, c0:c0 + RC], res[:], r=[res], w=[dst])
            sch.barrier()

        def load_w(q, dst_tile, dst_ap, src_ap):
            sch.dma(q, dst_ap, src_ap, w=[dst_tile])

        if upto >= 1:
            with ExitStack() as e1:
                Wd = sb(e1, "Wd", [128, 8, 640], BF16)
                Wuq = sb(e1, "Wuq", [128, 2, 8, 256], BF16)
                Wkv = sb(e1, "Wkv", [128, 2, 2, 8, 128], BF16)
                wdv = w_dqkv.rearrange("(c p) n -> p c n", p=128)
                load_w("pool", Wd, Wd[:, :, 0:576], wdv)
                load_w("pool", Wd, Wd[:, :, 576:608], wdv[:, :, 544:576])
                load_w("pool", Wd, Wd[:, :, 608:640], wdv[:, :, 512:544])
                for kc in range(2):
                    wv = w_uq[kc * 128:(kc + 1) * 128, :].rearrange("p (h d) -> p h d", h=8)
                    load_w("pool", Wuq, Wuq[:, kc, :, 0:192], wv)
                    load_w("pool", Wuq, Wuq[:, kc, :, 192:224], wv[:, :, 160:192])
                    load_w("pool", Wuq, Wuq[:, kc, :, 224:256], wv[:, :, 128:160])
                    wk = w_ukv[kc * 128:(kc + 1) * 128, :].rearrange("p (h kv d) -> p kv h d", h=8, kv=2)
                    for kv in range(2):
                        load_w("pool", Wkv, Wkv[:, kc, kv, :, :], wk[:, kv, :, :])
                xtok = [sb(e1, "xtok%d" % i, [128, 4, D]) for i in range(2)]
                xT32 = [sb(e1, "xT32_%d" % i, [128, 8, TT]) for i in range(2)]
                xTb = sb(e1, "xTb", [128, 8, TT], BF16)
                sq = sb(e1, "sq", [128, 4, TT], BF16)
                lnt = sb(e1, "lnt", [128, 2, TT])
                rstd = sb(e1, "rstd", [128, 2, TT])
                cl = sb(e1, "cl", [128, 4, TT], BF16)
                cct = [sb(e1, "cct%d" % i, [64, TT]) for i in range(2)]
                sst = [sb(e1, "sst%d" % i, [64, TT]) for i in range(2)]
                t1 = sb(e1, "t1", [64, TT])
                t2 = sb(e1, "t2", [64, TT])
                krb = sb(e1, "krb", [64, TT], BF16)
                Qn = sb(e1, "Qn", [128, 8, TT], BF16)
                Qr = sb(e1, "Qr", [64, 8, TT], BF16)
                Kn = sb(e1, "Kn", [128, 8, TT], BF16)
                Vt = sb(e1, "Vt", [128, 4, D], BF16)

                def load_x(it):
                    t0 = it * TT
                    sch.dma("sp", xtok[it % 2][:], x[t0:t0 + TT, :].rearrange("(n p) d -> p n d", p=128),
                            w=[xtok[it % 2]])
                    sch.dma("sp", cct[it % 2][:], sc_cc[:, t0:t0 + TT], r=[sc_cc], w=[cct[it % 2]])
                    sch.dma("sp", sst[it % 2][:], sc_ss[:, t0:t0 + TT], r=[sc_ss], w=[sst[it % 2]])

                import os as _os
                _kstop = int(_os.environ.get("KSTOP", "99"))
                load_x(0)
                for it in range(NT if _kstop > 0 else 0):
                    t0 = it * TT
                    if it + 1 < NT:
                        load_x(it + 1)
                    xt = xtok[it % 2]
                    x32 = xT32[it % 2]
                    CCt = cct[it % 2]
                    SSt = sst[it % 2]
                    for c in range(8):
                        bk = nextps()
                        for n in range(4):
                            PE(lambda: nc.tensor.transpose(out=bk[:, n * 128:(n + 1) * 128],
                                                           in_=xt[:, n, c * 128:(c + 1) * 128], identity=ident_f),
                               r=[xt, cst], w=[bk])
                        ACT(lambda: nc.scalar.copy(out=x32[:, c, :], in_=bk[:]), r=[bk], w=[x32])
                        DVE(lambda: nc.vector.tensor_copy(out=xTb[:, c, :], in_=x32[:, c, :]), r=[x32], w=[xTb])
                    sch.dma("sp", sc_xT[:, t0:t0 + TT].rearrange("(c p) t -> p c t", p=128), x32[:], r=[x32], w=[sc_xT])
                    if _kstop <= 1:
                        continue
                    dps = []
                    for oc in range(4):
                        bk = nextps()
                        for kc in range(8):
                            PE(lambda: nc.tensor.matmul(bk[:], lhsT=Wd[:, kc, oc * 128:(oc + 1) * 128], rhs=xTb[:, kc, :],
                                                        start=(kc == 0), stop=(kc == 7)), r=[Wd, xTb], w=[bk])
                        ACT(lambda: nc.scalar.activation(out=sq[:, oc, :], in_=bk[:], func=AF.Square), r=[bk], w=[sq])
                        dps.append(bk)
                    for g in range(2):
                        bs = nextps()
                        for j in range(2):
                            PE(lambda: nc.tensor.matmul(bs[:], lhsT=ones_b[:], rhs=sq[:, 2 * g + j, :],
                                                        start=(j == 0), stop=(j == 1)), r=[ones_b, sq], w=[bs])
                        ACT(lambda: nc.scalar.activation(out=lnt[:, g, :], in_=bs[:], func=AF.Ln, scale=1.0 / 256.0,
                                                         bias=RMS_EPS), r=[bs], w=[lnt])
                        ACT(lambda: nc.scalar.activation(out=rstd[:, g, :], in_=lnt[:, g, :], func=AF.Exp, scale=-0.5),
                            r=[lnt], w=[rstd])
                        for j in range(2):
                            col = (c_qn if g == 0 else c_kvn) + j
                            DVE(lambda: nc.vector.scalar_tensor_tensor(out=cl[:, 2 * g + j, :], in0=dps[2 * g + j][:],
                                                                       scalar=pv[:, col:col + 1], in1=rstd[:, g, :],
                                                                       op0=ALU.mult, op1=ALU.mult),
                                r=[dps[2 * g + j], pv, rstd], w=[cl])

                    def rope(b1, b2, dst_ap, dst_t):
                        DVE(lambda: nc.vector.tensor_tensor(out=t1[:], in0=b1[0:64, :], in1=CCt[:], op=ALU.mult),
                            r=[b1, CCt], w=[t1])
                        DVE(lambda: nc.vector.tensor_tensor(out=t2[:], in0=b2[0:64, :], in1=SSt[:], op=ALU.mult),
                            r=[b2, SSt], w=[t2])
                        DVE(lambda: nc.vector.tensor_tensor(out=dst_ap, in0=t1[:], in1=t2[:], op=ALU.add),
                            r=[t1, t2], w=[dst_t])

                    if _kstop <= 2:
                        continue
                    b1 = nextps()
                    b2 = nextps()
                    for bk, c0 in ((b1, 512), (b2, 576)):
                        for kc in range(8):
                            PE(lambda: nc.tensor.matmul(bk[0:64, :], lhsT=Wd[:, kc, c0:c0 + 64], rhs=xTb[:, kc, :],
                                                        start=(kc == 0), stop=(kc == 7)), r=[Wd, xTb], w=[bk])
                    rope(b1, b2, krb[:], krb)
                    sch.dma("sp", sc_kr[:, t0:t0 + TT], krb[:], r=[krb], w=[sc_kr])
                    if _kstop <= 3:
                        continue
                    for h in range(8):
                        bn = nextps()
                        b1 = nextps()
                        b2 = nextps()
                        for kc in range(2):
                            PE(lambda: nc.tensor.matmul(bn[:], lhsT=Wuq[:, kc, h, 0:128], rhs=cl[:, kc, :],
                                                        start=(kc == 0), stop=(kc == 1)), r=[Wuq, cl], w=[bn])
                        for bk, c0 in ((b1, 128), (b2, 192)):
                            for kc in range(2):
                                PE(lambda: nc.tensor.matmul(bk[0:64, :], lhsT=Wuq[:, kc, h, c0:c0 + 64], rhs=cl[:, kc, :],
                                                            start=(kc == 0), stop=(kc == 1)), r=[Wuq, cl], w=[bk])
                        ACT(lambda: nc.scalar.copy(out=Qn[:, h, :], in_=bn[:]), r=[bn], w=[Qn])
                        rope(b1, b2, Qr[:, h, :], Qr)
                    sch.dma("sp", sc_qn[:, :, t0:t0 + TT].rearrange("h p t -> p h t"), Qn[:], r=[Qn], w=[sc_qn])
                    sch.dma("sp", sc_qr[:, :, t0:t0 + TT].rearrange("h p t -> p h t"), Qr[:], r=[Qr], w=[sc_qr])
                    if _kstop <= 4:
                        continue
                    for h in range(8):
                        bk = nextps()
                        for kc in range(2):
                            PE(lambda: nc.tensor.matmul(bk[:], lhsT=Wkv[:, kc, 0, h, :], rhs=cl[:, 2 + kc, :],
                                                        start=(kc == 0), stop=(kc == 1)), r=[Wkv, cl], w=[bk])
                        ACT(lambda: nc.scalar.copy(out=Kn[:, h, :], in_=bk[:]), r=[bk], w=[Kn])
                    sch.dma("sp", sc_kn[:, :, t0:t0 + TT].rearrange("h p t -> p h t"), Kn[:], r=[Kn], w=[sc_kn])
                    for n in range(4):
                        for hf in range(2):
                            bk = nextps()
                            for kc in range(2):
                                PE(lambda: nc.tensor.matmul(bk[:], lhsT=cl[:, 2 + kc, n * 128:(n + 1) * 128],
                                                            rhs=Wkv[:, kc, 1, 4 * hf:4 * hf + 4, :],
                                                            start=(kc == 0), stop=(kc == 1)), r=[Wkv, cl], w=[bk])
                            DVE(lambda: nc.vector.tensor_copy(out=Vt[:, n, hf * 512:(hf + 1) * 512], in_=bk[:]),
                                r=[bk], w=[Vt])
                    sch.dma("sp", sc_v[t0:t0 + TT, :].rearrange("(n p) d -> p n d", p=128), Vt[:], r=[Vt], w=[sc_v])
                sch.barrier()

        if upto >= 2:
            SCALE = 192.0 ** -0.5
            import os as _os2
            ATT_MODE = int(_os2.environ.get("ATT_MODE", "0"))
            with ExitStack() as e1:
                Kr = sb(e1, "Kr", [128, S], BF16)
                Knh = [sb(e1, "Knh%d" % i, [128, S], BF16) for i in range(2)]
                Vh = [sb(e1, "Vh%d" % i, [128, NB, 128], BF16) for i in range(2)]
                Qnb = [sb(e1, "Qnb%d" % i, [128, TT], BF16) for i in range(2)]
                Qrb = [sb(e1, "Qrb%d" % i, [128, TT], BF16) for i in range(2)]
                NPT = 4
                pt = [sb(e1, "pt%d" % i, [128, TT], BF16) for i in range(NPT)]
                pacc = [[sb(e1, "pacc", [128, TT]) for _ in range(2)] for _ in range(2)]
                dsum = [sb(e1, "dsum", [128, TT]) for _ in range(2)]
                ones_f = sb(e1, "ones_f", [128, 128])
                DVE(lambda: nc.vector.memset(ones_f[:], 1.0), w=[ones_f])
                rcp = [sb(e1, "rcp%d" % i, [128, TT]) for i in range(2)]
                ob = [sb(e1, "ob%d" % i, [128, TT], BF16) for i in range(2)]
                NSB = 4
                LOOK = 2
                sbank = banks[0:NSB]
                accs = [(banks[4], banks[5]), (banks[6], banks[7])]
                POOL(lambda: nc.gpsimd.memset(Kr[64:128, :], 0.0), w=[Kr])
                for i_ in range(2):
                    POOL(lambda: nc.gpsimd.memset(Qrb[i_][64:128, :], 0.0), w=[Qrb[i_]])
                sch.dma("sp", Kr[0:64, :], sc_kr[:, :], r=[sc_kr], w=[Kr])

                def load_head(h):
                    sch.dma("sp", Knh[h % 2][:], sc_kn[h, :, :], r=[sc_kn], w=[Knh[h % 2]])
                    sch.dma("sp", Vh[h % 2][:], sc_v[:, h * 128:(h + 1) * 128].rearrange("(n p) d -> p n d", p=128),
                            r=[sc_v], w=[Vh[h % 2]])

                def load_q(h, qb, i):
                    sch.dma("sp", Qnb[i % 2][:], sc_qn[h, :, qb * TT:(qb + 1) * TT], r=[sc_qn], w=[Qnb[i % 2]])
                    sch.dma("sp", Qrb[i % 2][0:64, :], sc_qr[h, :, qb * TT:(qb + 1) * TT], r=[sc_qr], w=[Qrb[i % 2]])

                load_head(0)
                load_q(0, 0, 0)
                blk = 0
                ti = 0
                for h in range(8):
                    if h + 1 < 8:
                        load_head(h + 1)
                    K_ = Knh[h % 2]
                    V_ = Vh[h % 2]
                    for qb in range(NT):
                        nh, nq = (h, qb + 1) if qb + 1 < NT else (h + 1, 0)
                        if nh < 8:
                            load_q(nh, nq, blk + 1)
                        Qn_ = Qnb[blk % 2]
                        Qr_ = Qrb[blk % 2]
                        acc_o, acc_d = accs[blk % 2]
                        nk = 4 * qb + 4
                        pa = pacc[blk % 2]
                        DVE(lambda: nc.vector.memset(pa[0][:], 0.0), w=[pa[0]])
                        POOL(lambda: nc.gpsimd.memset(pa[1][:], 0.0), w=[pa[1]])

                        def qk(kb, sp_):
                            j = kb - 4 * qb
                            c0 = max(j, 0) * 128
                            ks = slice(kb * 128, (kb + 1) * 128)
                            PE(lambda: nc.tensor.matmul(sp_[:, c0:TT], lhsT=K_[:, ks], rhs=Qn_[:, c0:TT],
                                                        start=True, stop=False), r=[K_, Qn_], w=[sp_], ms=False)
                            PE(lambda: nc.tensor.matmul(sp_[:, c0:TT], lhsT=Kr[:, ks], rhs=Qr_[:, c0:TT],
                                                        start=False, stop=(j < 0)), r=[Kr, Qr_], w=[sp_], ms=(j < 0))
                            if j >= 0:
                                PE(lambda: nc.tensor.matmul(sp_[:, c0:c0 + 128], lhsT=ident_b[:], rhs=maskb[:],
                                                            start=False, stop=True), r=[ident_b, maskb], w=[sp_])
                            return c0

                        c0s = {}
                        for k0 in range(min(LOOK, nk)):
                            c0s[k0] = qk(k0, sbank[(ti + k0) % NSB])
                        for kb in range(nk):
                            sp_ = sbank[(ti + kb) % NSB]
                            p_ = pt[(ti + kb) % NPT]
                            c0 = c0s[kb]
                            ACT(lambda: nc.scalar.activation(out=p_[:, c0:TT], in_=sp_[:, c0:TT], func=AF.Exp,
                                                             scale=SCALE), r=[sp_], w=[p_])
                            if kb + LOOK < nk:
                                c0s[kb + LOOK] = qk(kb + LOOK, sbank[(ti + kb + LOOK) % NSB])
                            PE(lambda: nc.tensor.matmul(acc_o[:, c0:TT], lhsT=V_[:, kb, :], rhs=p_[:, c0:TT],
                                                        start=(kb == 0), stop=(kb == nk - 1)), r=[V_, p_], w=[acc_o])
                            if ATT_MODE == 0:
                                PE(lambda: nc.tensor.matmul(acc_d[:, c0:TT], lhsT=ones_b[:], rhs=p_[:, c0:TT],
                                                            start=(kb == 0), stop=(kb == nk - 1)), r=[ones_b, p_], w=[acc_d])
                            elif kb % 2 == 0 or ATT_MODE == 2:
                                DVE(lambda: nc.vector.tensor_tensor(out=pa[0][:, c0:TT], in0=pa[0][:, c0:TT], in1=p_[:, c0:TT],
                                                                    op=ALU.add), r=[pa[0], p_], w=[pa[0]])
                            else:
                                POOL(lambda: nc.gpsimd.tensor_tensor(out=pa[1][:, c0:TT], in0=pa[1][:, c0:TT], in1=p_[:, c0:TT],
                                                                     op=ALU.add), r=[pa[1], p_], w=[pa[1]])
                        ti += nk
                        rc = rcp[blk % 2]
                        o_ = ob[blk % 2]
                        ds_ = dsum[blk % 2]
                        if ATT_MODE != 0:
                            DVE(lambda: nc.vector.tensor_tensor(out=ds_[:], in0=pa[0][:], in1=pa[1][:], op=ALU.add),
                                r=[pa[0], pa[1]], w=[ds_])
                            PE(lambda: nc.tensor.matmul(acc_d[:], lhsT=ones_f[:], rhs=ds_[:], start=True, stop=True),
                               r=[ones_f, ds_], w=[acc_d])
                        DVE(lambda: nc.vector.reciprocal(out=rc[:], in_=acc_d[:]), r=[acc_d], w=[rc])
                        DVE(lambda: nc.vector.tensor_tensor(out=o_[:], in0=acc_o[:], in1=rc[:], op=ALU.mult),
                            r=[acc_o, rc], w=[o_])
                        sch.dma("sp", sc_o[h * 128:(h + 1) * 128, qb * TT:(qb + 1) * TT], o_[:], r=[o_], w=[sc_o])
                        blk += 1
                sch.barrier()


        def ln_tile(y, scr, scr_t, st, tmp, gcol, bcol, nb=8):
            mean, msq, lnv = st
            for c in range(8):
                DVE(lambda: nc.vector.tensor_copy(out=scr[:, c, :], in_=y[:, c, :]), r=[y], w=[scr_t])
                ACT(lambda: nc.scalar.activation(out=scr[:, 8 + c, :], in_=y[:, c, :], func=AF.Square), r=[y], w=[scr_t])
            s1 = nextps(nb)
            s2 = nextps(nb)
            for c in range(8):
                PE(lambda: nc.tensor.matmul(s1[:], lhsT=ones_b[:], rhs=scr[:, c, :], start=(c == 0), stop=(c == 7)),
                   r=[ones_b, scr_t], w=[s1], ms=(c == 7))
            for c in range(8):
                PE(lambda: nc.tensor.matmul(s2[:], lhsT=ones_b[:], rhs=scr[:, 8 + c, :], start=(c == 0), stop=(c == 7)),
                   r=[ones_b, scr_t], w=[s2], ms=(c == 7))
            ACT(lambda: nc.scalar.activation(out=mean[:], in_=s1[:], func=AF.Copy, scale=1.0 / D), r=[s1], w=[mean])
            DVE(lambda: nc.vector.tensor_tensor(out=msq[:], in0=mean[:], in1=mean[:], op=ALU.mult), r=[mean], w=[msq])
            DVE(lambda: nc.vector.scalar_tensor_tensor(out=msq[:], in0=s2[:], scalar=1.0 / D, in1=msq[:],
                                                       op0=ALU.mult, op1=ALU.subtract), r=[s2, msq], w=[msq])
            ACT(lambda: nc.scalar.activation(out=lnv[:], in_=msq[:], func=AF.Ln, bias=LN_EPS), r=[msq], w=[lnv])
            ACT(lambda: nc.scalar.activation(out=lnv[:], in_=lnv[:], func=AF.Exp, scale=-0.5), r=[lnv], w=[lnv])
            for c in range(8):
                DVE(lambda: nc.vector.tensor_tensor(out=tmp[:], in0=y[:, c, :], in1=mean[:], op=ALU.subtract),
                    r=[y, mean], w=[tmp])
                DVE(lambda: nc.vector.tensor_tensor(out=tmp[:], in0=tmp[:], in1=lnv[:], op=ALU.mult),
                    r=[tmp, lnv], w=[tmp])
                ACT(lambda: nc.scalar.activation(out=y[:, c, :], in_=tmp[:], func=AF.Identity,
                                                 scale=pv[:, gcol + c:gcol + c + 1], bias=pv[:, bcol + c:bcol + c + 1]),
                    r=[tmp, pv], w=[y])

        def fm(ap2d, t0):
            return ap2d[:, t0:t0 + TT].rearrange("(c p) t -> p c t", p=128)

        def hm(ap3d, t0):
            return ap3d[:, :, t0:t0 + TT].rearrange("h p t -> p h t")

        def wload(dst, src2d, nkc):
            v = src2d.rearrange("(c p) n -> p c n", p=128)
            for kc in range(nkc):
                sch.dma("pool", dst[:, kc, :], v[:, kc, :], w=[dst])

        def stage_proj_ln(w_o_ap, src_o, src_x, dst, gcol, bcol):
            with ExitStack() as e1:
                Wo = sb(e1, "Wo", [128, 8, D], BF16)
                wload(Wo, w_o_ap, 8)
                oT = [sb(e1, "oT%d" % i, [128, 8, TT], BF16) for i in range(2)]
                yy = [sb(e1, "yy%d" % i, [128, 8, TT]) for i in range(2)]
                scr = sb(e1, "scr", [128, 16, TT], BF16)
                st = [sb(e1, "st%d" % i, [128, TT]) for i in range(3)]
                tmp = sb(e1, "tmp", [128, TT])

                def ld(it):
                    sch.dma("sp", oT[it % 2][:], fm(src_o.t, it * TT), r=[src_o], w=[oT[it % 2]])
                    sch.dma("sp", yy[it % 2][:], fm(src_x.t, it * TT), r=[src_x], w=[yy[it % 2]])

                ld(0)
                for it in range(NT):
                    if it + 1 < NT:
                        ld(it + 1)
                    o_ = oT[it % 2]
                    y = yy[it % 2]
                    for oc in range(8):
                        bk = nextps()
                        for kc in range(8):
                            PE(lambda: nc.tensor.matmul(bk[:], lhsT=Wo[:, kc, oc * 128:(oc + 1) * 128], rhs=o_[:, kc, :],
                                                        start=(kc == 0), stop=(kc == 7)), r=[Wo, o_], w=[bk], ms=(kc == 7))
                        DVE(lambda: nc.vector.scalar_tensor_tensor(out=y[:, oc, :], in0=y[:, oc, :], scalar=ALPHA, in1=bk[:],
                                                                   op0=ALU.mult, op1=ALU.add), r=[y, bk], w=[y])
                    ln_tile(y, scr, scr, st, tmp, gcol, bcol)
                    sch.dma("sp", fm(dst.t, it * TT), y[:], r=[y], w=[dst])
                sch.barrier()

        def stage_ffn(li, src, dst, gcol, bcol):
            with ExitStack() as e1:
                Win = sb(e1, "Win", [128, 8, 2 * DFF], BF16)
                Wdn = sb(e1, "Wdn", [128, 22, D], BF16)
                wload(Win, ffn_w_in[li], 8)
                wload(Wdn, ffn_w_dn[li], 22)
                y = sb(e1, "y", [128, 8, TT])
                xb = sb(e1, "xb", [128, 8, TT], BF16)
                hh = sb(e1, "hh", [128, 22, TT], BF16)
                sg = [sb(e1, "sg", [128, TT]) for i in range(2)]
                st = [sb(e1, "st", [128, TT]) for i in range(3)]
                tmp = sb(e1, "tmp", [128, TT])
                sc2 = [(sb(e1, "ybf", [128, TT], BF16), sb(e1, "ysq", [128, TT], BF16)) for i in range(2)]
                mean, msq, lnv = st
                s1, s2 = banks[6], banks[7]

                def load_xb(it):
                    sch.dma("pool", xb[:], fm(src.t, it * TT), r=[src], w=[xb])

                def load_y(it):
                    sch.dma("sp", y[:], fm(src.t, it * TT), r=[src], w=[y])

                def ln_gen(t0, nxt):
                    for c in range(8):
                        DVE(lambda: nc.vector.tensor_tensor(out=tmp[:], in0=y[:, c, :], in1=mean[:], op=ALU.subtract),
                            r=[y, mean], w=[tmp])
                        DVE(lambda: nc.vector.tensor_tensor(out=tmp[:], in0=tmp[:], in1=lnv[:], op=ALU.mult),
                            r=[tmp, lnv], w=[tmp])
                        ACT(lambda: nc.scalar.activation(out=y[:, c, :], in_=tmp[:], func=AF.Identity,
                                                         scale=pv[:, gcol + c:gcol + c + 1], bias=pv[:, bcol + c:bcol + c + 1]),
                            r=[tmp, pv], w=[y])
                        yield
                    sch.dma("sp", fm(dst.t, t0), y[:], r=[y], w=[dst])
                    if nxt is not None:
                        load_y(nxt)
                    yield

                def stats_mm(oc, k):
                    PE(lambda: nc.tensor.matmul(s1[:], lhsT=ones_b[:], rhs=sc2[k][0][:], start=(oc == 0), stop=(oc == 7)),
                       r=[ones_b, sc2[k][0]], w=[s1])
                    PE(lambda: nc.tensor.matmul(s2[:], lhsT=ones_b[:], rhs=sc2[k][1][:], start=(oc == 0), stop=(oc == 7)),
                       r=[ones_b, sc2[k][1]], w=[s2])

                load_xb(0)
                load_y(0)
                pending = None
                for it in range(NT):
                    for j in range(22):
                        bg = nextps(6)
                        bu = nextps(6)
                        for kc in range(8):
                            PE(lambda: nc.tensor.matmul(bg[:], lhsT=Win[:, kc, j * 128:(j + 1) * 128], rhs=xb[:, kc, :],
                                                        start=(kc == 0), stop=(kc == 7)), r=[Win, xb], w=[bg], ms=(kc == 7))
                        for kc in range(8):
                            PE(lambda: nc.tensor.matmul(bu[:], lhsT=Win[:, kc, DFF + j * 128:DFF + (j + 1) * 128],
                                                        rhs=xb[:, kc, :], start=(kc == 0), stop=(kc == 7)),
                               r=[Win, xb], w=[bu], ms=(kc == 7))
                        s_ = sg[j % 2]
                        ACT(lambda: nc.scalar.activation(out=s_[:], in_=bg[:], func=AF.Silu), r=[bg], w=[s_])
                        DVE(lambda: nc.vector.tensor_tensor(out=hh[:, j, :], in0=s_[:], in1=bu[:], op=ALU.mult),
                            r=[s_, bu], w=[hh])
                        if pending is not None and j >= 2:
                            next(pending, None)
                    if pending is not None:
                        for _ in pending:
                            pass
                        pending = None
                    if it + 1 < NT:
                        load_xb(it + 1)
                    for oc in range(8):
                        bk = nextps(6)
                        for j in range(22):
                            PE(lambda: nc.tensor.matmul(bk[:], lhsT=Wdn[:, j, oc * 128:(oc + 1) * 128], rhs=hh[:, j, :],
                                                        start=(j == 0), stop=(j == 21)), r=[Wdn, hh], w=[bk], ms=(j == 21))
                        DVE(lambda: nc.vector.scalar_tensor_tensor(out=y[:, oc, :], in0=y[:, oc, :], scalar=ALPHA, in1=bk[:],
                                                                   op0=ALU.mult, op1=ALU.add), r=[y, bk], w=[y])
                        k = oc % 2
                        DVE(lambda: nc.vector.tensor_copy(out=sc2[k][0][:], in_=y[:, oc, :]), r=[y], w=[sc2[k][0]])
                        ACT(lambda: nc.scalar.activation(out=sc2[k][1][:], in_=y[:, oc, :], func=AF.Square), r=[y], w=[sc2[k][1]])
                        if oc >= 1:
                            stats_mm(oc - 1, (oc - 1) % 2)
                    stats_mm(7, 1)
                    ACT(lambda: nc.scalar.activation(out=mean[:], in_=s1[:], func=AF.Copy, scale=1.0 / D), r=[s1], w=[mean])
                    DVE(lambda: nc.vector.tensor_tensor(out=msq[:], in0=mean[:], in1=mean[:], op=ALU.mult), r=[mean], w=[msq])
                    DVE(lambda: nc.vector.scalar_tensor_tensor(out=msq[:], in0=s2[:], scalar=1.0 / D, in1=msq[:],
                                                               op0=ALU.mult, op1=ALU.subtract), r=[s2, msq], w=[msq])
                    ACT(lambda: nc.scalar.activation(out=lnv[:], in_=msq[:], func=AF.Ln, bias=LN_EPS), r=[msq], w=[lnv])
                    ACT(lambda: nc.scalar.activation(out=lnv[:], in_=lnv[:], func=AF.Exp, scale=-0.5), r=[lnv], w=[lnv])
                    pending = ln_gen(it * TT, it + 1 if it + 1 < NT else None)
                for _ in pending:
                    pass
                sch.barrier()

        def stage_ple(li, src, dst, final):
            with ExitStack() as e1:
                Wg = sb(e1, "Wg", [128, 8, D], BF16)
                Wp = sb(e1, "Wp", [128, 2, D], BF16)
                wload(Wg, ple_w_gate[li], 8)
                wload(Wp, ple_w_proj[li], 2)
                if not final:
                    Wh = sb(e1, "Wh", [128, 8, 4096], BF16)
                    wload(Wh, hg_w_in, 8)
                y = sb(e1, "y", [128, 8, TT])
                xb = sb(e1, "xb", [128, 8, TT], BF16)
                ptok = sb(e1, "ptok", [128, 4, 256])
                pT = sb(e1, "pT", [128, 2, TT], BF16)
                sg = [sb(e1, "sg%d" % i, [128, TT]) for i in range(2)]
                tmp = sb(e1, "tmp", [128, TT])
                if final:
                    otok = sb(e1, "otok", [128, 4, D])
                else:
                    FS = [dict((nm, sb(e1, "f_" + nm, [128, TT])) for nm in
                               ("sq", "ft", "kk", "lf", "G", "d1", "d2", "e0", "e1", "e2", "e3")) for _ in range(2)]
                    HO = [dict((nm, sb(e1, "ho_" + nm, [128, TT], BF16)) for nm in ("qa", "qs", "ka", "kh", "gt"))
                          for _ in range(2)]
                    HD = sb(e1, "HD", [128, 8, 4])
                    HV = sb(e1, "HV", [128, 4, D], BF16)
                for it in range(NT):
                    t0 = it * TT
                    sch.dma("sp", y[:], fm(src.t, t0), r=[src], w=[y])
                    sch.dma("sp", ptok[:], p_in[li, t0:t0 + TT, :].rearrange("(n p) d -> p n d", p=128), w=[ptok])
                    for c in range(8):
                        DVE(lambda: nc.vector.tensor_copy(out=xb[:, c, :], in_=y[:, c, :]), r=[y], w=[xb])
                    for c2 in range(2):
                        bk = nextps()
                        for n in range(4):
                            PE(lambda: nc.tensor.transpose(out=bk[:, n * 128:(n + 1) * 128],
                                                           in_=ptok[:, n, c2 * 128:(c2 + 1) * 128], identity=ident_f),
                               r=[ptok, cst], w=[bk], ms=(n == 3))
                        ACT(lambda: nc.scalar.copy(out=pT[:, c2, :], in_=bk[:]), r=[bk], w=[pT])
                    for oc in range(8):
                        bg = nextps()
                        bp = nextps()
                        for kc in range(8):
                            PE(lambda: nc.tensor.matmul(bg[:], lhsT=Wg[:, kc, oc * 128:(oc + 1) * 128], rhs=xb[:, kc, :],
                                                        start=(kc == 0), stop=(kc == 7)), r=[Wg, xb], w=[bg], ms=(kc == 7))
                        for kc in range(2):
                            PE(lambda: nc.tensor.matmul(bp[:], lhsT=Wp[:, kc, oc * 128:(oc + 1) * 128], rhs=pT[:, kc, :],
                                                        start=(kc == 0), stop=(kc == 1)), r=[Wp, pT], w=[bp], ms=(kc == 1))
                        s_ = sg[oc % 2]
                        ACT(lambda: nc.scalar.activation(out=s_[:], in_=bg[:], func=AF.Sigmoid), r=[bg], w=[s_])
                        DVE(lambda: nc.vector.tensor_tensor(out=tmp[:], in0=s_[:], in1=bp[:], op=ALU.mult),
                            r=[s_, bp], w=[tmp])
                        DVE(lambda: nc.vector.tensor_tensor(out=y[:, oc, :], in0=y[:, oc, :], in1=tmp[:], op=ALU.add),
                            r=[y, tmp], w=[y])
                    if final:
                        for n in range(4):
                            for hf in range(2):
                                bk = nextps()
                                for c in range(4):
                                    PE(lambda: nc.tensor.transpose(out=bk[:, c * 128:(c + 1) * 128],
                                                                   in_=y[:, hf * 4 + c, n * 128:(n + 1) * 128],
                                                                   identity=ident_f), r=[y, cst], w=[bk], ms=(c == 3))
                                ACT(lambda: nc.scalar.copy(out=otok[:, n, hf * 512:(hf + 1) * 512], in_=bk[:]),
                                    r=[bk], w=[otok])
                        sch.dma("sp", out[t0:t0 + TT, :].rearrange("(n p) d -> p n d", p=128), otok[:], r=[otok], w=[dst])
                        continue
                    sch.dma("sp", fm(dst.t, t0), y[:], r=[y], w=[dst])
                    for c in range(8):
                        DVE(lambda: nc.vector.tensor_copy(out=xb[:, c, :], in_=y[:, c, :]), r=[y], w=[xb])

                    def proj(col0):
                        bk = nextps()
                        for kc in range(8):
                            PE(lambda: nc.tensor.matmul(bk[:], lhsT=Wh[:, kc, col0:col0 + 128], rhs=xb[:, kc, :],
                                                        start=(kc == 0), stop=(kc == 7)), r=[Wh, xb], w=[bk], ms=(kc == 7))
                        return bk

                    def bc(tl, pos_):
                        base = tl.t[:, pos_:pos_ + 1]
                        return bass.AP(base.tensor, base.offset, [[TT, 128], [128, 4], [0, 128]])

                    def v3(tl):
                        return tl[:].rearrange("p (c t) -> p c t", c=4)

                    for h in range(8):
                        F = FS[h % 2]
                        O = HO[h % 2]
                        bq = proj(h * 128)
                        ACT(lambda: nc.scalar.activation(out=F["sq"][:], in_=bq[:], func=AF.Silu), r=[bq], w=[F["sq"]])
                        bf = proj(1024 + h * 128)
                        ACT(lambda: nc.scalar.activation(out=F["ft"][:], in_=bf[:], func=AF.Sigmoid), r=[bf], w=[F["ft"]])
                        bgt = proj(3072 + h * 128)
                        ACT(lambda: nc.scalar.activation(out=O["gt"][:], in_=bgt[:], func=AF.Silu), r=[bgt], w=[O["gt"]])
                        DVE(lambda: nc.vector.tensor_scalar(out=F["ft"][:], in0=F["ft"][:], scalar1=lbv[:, 8 + h:9 + h],
                                                            scalar2=lbv[:, h:h + 1], op0=ALU.mult, op1=ALU.add),
                            r=[F["ft"], lbv], w=[F["ft"]])
                        ACT(lambda: nc.scalar.activation(out=F["lf"][:], in_=F["ft"][:], func=AF.Ln), r=[F["ft"]], w=[F["lf"]])
                        DVE(lambda: nc.vector.tensor_scalar(out=F["kk"][:], in0=F["ft"][:], scalar1=-1.0, scalar2=1.0,
                                                            op0=ALU.mult, op1=ALU.add), r=[F["ft"]], w=[F["kk"]])
                        DVE(lambda: nc.vector.tensor_tensor_scan(out=F["G"][:], data0=cst[:, 768:1280], data1=F["lf"][:],
                                                                 initial=0.0, op0=ALU.mult, op1=ALU.add),
                            r=[cst, F["lf"]], w=[F["G"]])
                        ACT(lambda: nc.scalar.activation(out=F["e0"][:], in_=F["G"][:], func=AF.Exp), r=[F["G"]], w=[F["e0"]])
                        DVE(lambda: nc.vector.tensor_tensor(out=v3(F["d1"]), in0=v3(F["G"]), in1=bc(F["G"], 63), op=ALU.subtract),
                            r=[F["G"]], w=[F["d1"]])
                        DVE(lambda: nc.vector.tensor_tensor(out=v3(F["d2"]), in0=bc(F["G"], 127), in1=v3(F["G"]), op=ALU.subtract),
                            r=[F["G"]], w=[F["d2"]])
                        ACT(lambda: nc.scalar.activation(out=F["e1"][:], in_=F["d1"][:], func=AF.Exp), r=[F["d1"]], w=[F["e1"]])
                        ACT(lambda: nc.scalar.activation(out=F["e2"][:], in_=F["d1"][:], func=AF.Exp, scale=-1.0),
                            r=[F["d1"]], w=[F["e2"]])
                        ACT(lambda: nc.scalar.activation(out=F["e3"][:], in_=F["d2"][:], func=AF.Exp), r=[F["d2"]], w=[F["e3"]])
                        DVE(lambda: nc.vector.tensor_tensor(out=O["qs"][:], in0=F["sq"][:], in1=F["e0"][:], op=ALU.mult),
                            r=[F["sq"], F["e0"]], w=[O["qs"]])
                        DVE(lambda: nc.vector.tensor_copy(out=HD[:, h, :], in_=F["e0"][:, 127:TT:128]), r=[F["e0"]], w=[HD])
                        POOL(lambda: nc.gpsimd.tensor_tensor(out=O["qa"][:], in0=F["sq"][:], in1=F["e1"][:], op=ALU.mult),
                             r=[F["sq"], F["e1"]], w=[O["qa"]])
                        DVE(lambda: nc.vector.tensor_tensor(out=O["ka"][:], in0=F["kk"][:], in1=F["e2"][:], op=ALU.mult),
                            r=[F["kk"], F["e2"]], w=[O["ka"]])
                        POOL(lambda: nc.gpsimd.tensor_tensor(out=O["kh"][:], in0=F["kk"][:], in1=F["e3"][:], op=ALU.mult),
                             r=[F["kk"], F["e3"]], w=[O["kh"]])
                        for nm, dstt in (("qa", sc_hqa), ("qs", sc_hqs), ("ka", sc_hka), ("kh", sc_hkh), ("gt", sc_hg)):
                            sch.dma("sp", dstt[h, :, t0:t0 + TT], O[nm][:], r=[O[nm]], w=[dstt])
                    for n in range(4):
                        for hf in range(2):
                            bk = nextps()
                            for kc in range(8):
                                PE(lambda: nc.tensor.matmul(bk[:], lhsT=xb[:, kc, n * 128:(n + 1) * 128],
                                                            rhs=Wh[:, kc, 2048 + hf * 512:2048 + (hf + 1) * 512],
                                                            start=(kc == 0), stop=(kc == 7)), r=[Wh, xb], w=[bk], ms=(kc == 7))
                            DVE(lambda: nc.vector.tensor_copy(out=HV[:, n, hf * 512:(hf + 1) * 512], in_=bk[:]),
                                r=[bk], w=[HV])
                    sch.dma("sp", sc_hd[:, :, it * 4:(it + 1) * 4], HD[:], r=[HD], w=[sc_hd])
                    sch.dma("sp", sc_hv[t0:t0 + TT, :].rearrange("(n p) d -> p n d", p=128), HV[:], r=[HV], w=[sc_hv])
                sch.barrier()

        def stage_hgrn(src_x, dst, gcol, bcol):
            with ExitStack() as e1:
                Wo = sb(e1, "Who", [128, 8, D], BF16)
                wload(Wo, hg_w_o, 8)
                for kc in range(8):
                    DVE(lambda: nc.vector.tensor_scalar(out=Wo[:, kc, :], in0=Wo[:, kc, :], scalar1=pv[:, c_on + kc:c_on + kc + 1],
                                                        scalar2=None, op0=ALU.mult), r=[Wo, pv], w=[Wo])
                um = sb(e1, "um", [128, 128])
                DVE(lambda: nc.vector.tensor_scalar(out=um[:], in0=cst[:, 128:256], scalar1=-1.0, scalar2=None,
                                                    op0=ALU.is_gt), r=[cst], w=[um])
                um_bc = bass.AP(um.t[:].tensor, um.t[:].offset, [[128, 128], [0, 4], [1, 128]])
                NBUF = 2
                HQA = [sb(e1, "HQA", [128, 8, TT], BF16) for _ in range(NBUF)]
                HQS = [sb(e1, "HQS", [128, 8, TT], BF16) for _ in range(NBUF)]
                HKA = [sb(e1, "HKA", [128, 8, TT], BF16) for _ in range(NBUF)]
                HKH = [sb(e1, "HKH", [128, 8, TT], BF16) for _ in range(NBUF)]
                HGt = [sb(e1, "HGt", [128, 8, TT], BF16) for _ in range(NBUF)]
                HD = [sb(e1, "HD", [128, 8, 4]) for _ in range(NBUF)]
                HV = [sb(e1, "HV", [128, 4, D], BF16) for _ in range(NBUF)]
                yy = [sb(e1, "y", [128, 8, TT]) for _ in range(1)]
                OG = sb(e1, "OG", [128, 8, TT], BF16)
                S32 = [sb(e1, "S32", [128, 4, 128]) for _ in range(2)]
                Sb = [sb(e1, "Sb", [128, 4, 128], BF16) for _ in range(2)]
                khT = [sb(e1, "khT", [128, 4, 128], BF16) for _ in range(8)]
                ATb = [sb(e1, "ATb", [128, 4, 128], BF16) for _ in range(8)]
                osq = [sb(e1, "osq", [128, TT], BF16) for _ in range(2)]
                on = [sb(e1, "on", [128, TT]) for _ in range(2)]
                lnr = [sb(e1, "lnr", [128, TT]) for _ in range(2)]
                scr = sb(e1, "scr", [128, 16, TT], BF16)
                st = [sb(e1, "st", [128, TT]) for i in range(3)]
                tmp = sb(e1, "tmp", [128, TT])
                for g in range(2):
                    DVE(lambda: nc.vector.memset(S32[g][:], 0.0), w=[S32[g]])
                    DVE(lambda: nc.vector.memset(Sb[g][:], 0.0), w=[Sb[g]])

                def ld(it):
                    t0 = it * TT
                    i = it % NBUF
                    sch.dma("sp", HKH[i][:], hm(sc_hkh.t, t0), r=[sc_hkh], w=[HKH[i]])
                    sch.dma("sp", HKA[i][:], hm(sc_hka.t, t0), r=[sc_hka], w=[HKA[i]])
                    sch.dma("sp", HQA[i][:], hm(sc_hqa.t, t0), r=[sc_hqa], w=[HQA[i]])
                    sch.dma("sp", HQS[i][:], hm(sc_hqs.t, t0), r=[sc_hqs], w=[HQS[i]])
                    sch.dma("sp", HV[i][:], sc_hv[t0:t0 + TT, :].rearrange("(n p) d -> p n d", p=128), r=[sc_hv], w=[HV[i]])
                    sch.dma("sp", HD[i][:], sc_hd[:, :, it * 4:(it + 1) * 4], r=[sc_hd], w=[HD[i]])
                    sch.dma("sp", HGt[i][:], hm(sc_hg.t, t0), r=[sc_hg], w=[HGt[i]])

                ld(0)
                kk = 0
                for it in range(NT):
                    t0 = it * TT
                    if it + 1 < NT:
                        ld(it + 1)
                    i = it % NBUF
                    qa, qs, ka, kh, gt, hd, hv, y = HQA[i], HQS[i], HKA[i], HKH[i], HGt[i], HD[i], HV[i], yy[0]
                    sch.dma("sp", y[:], fm(src_x.t, t0), r=[src_x], w=[y])
                    for h in range(8):
                        bt = nextps(4)
                        for c in range(4):
                            cs = slice(c * 128, (c + 1) * 128)
                            PE(lambda: nc.tensor.matmul(bt[:, cs], lhsT=kh[:, h, cs], rhs=ident_b[:], start=True, stop=True),
                               r=[kh, ident_b], w=[bt], ms=(c == 3))
                        ACT(lambda: nc.scalar.copy(out=khT[h][:].rearrange("p c d -> p (c d)"), in_=bt[:]), r=[bt], w=[khT[h]])
                        ba = nextps(4)
                        for c in range(4):
                            cs = slice(c * 128, (c + 1) * 128)
                            PE(lambda: nc.tensor.matmul(ba[:, cs], lhsT=ka[:, h, cs], rhs=qa[:, h, cs], start=True, stop=True),
                               r=[ka, qa], w=[ba], ms=(c == 3))
                        DVE(lambda: nc.vector.tensor_tensor(out=ATb[h][:], in0=ba[:].rearrange("p (c t) -> p c t", c=4),
                                                            in1=um_bc, op=ALU.mult), r=[ba, um], w=[ATb[h]])
                    for c in range(4):
                        cs = slice(c * 128, (c + 1) * 128)
                        for g in range(2):
                            ob_ = banks[4 + (kk % 2)]
                            bd = banks[6]
                            j2 = kk % 2
                            kk += 1
                            for j in range(4):
                                h = 4 * g + j
                                js = slice(j * 128, (j + 1) * 128)
                                hs = slice(h * 128, (h + 1) * 128)
                                PE(lambda: nc.tensor.matmul(ob_[:, js], lhsT=Sb[g][:, j, :], rhs=qs[:, h, cs], start=True, stop=False),
                                   r=[Sb[g], qs], w=[ob_], ms=False)
                                PE(lambda: nc.tensor.matmul(ob_[:, js], lhsT=hv[:, c, hs], rhs=ATb[h][:, c, :], start=False, stop=True),
                                   r=[hv, ATb[h]], w=[ob_], ms=(j == 3))
                            for j in range(4):
                                h = 4 * g + j
                                js = slice(j * 128, (j + 1) * 128)
                                hs = slice(h * 128, (h + 1) * 128)
                                PE(lambda: nc.tensor.matmul(bd[:, js], lhsT=khT[h][:, c, :], rhs=hv[:, c, hs], start=True, stop=True),
                                   r=[khT[h], hv], w=[bd], ms=(j == 3))
                            dec = bass.AP(hd.t[:].tensor, hd.t[:, 4 * g, c:c + 1].offset, [[32, 128], [4, 4], [0, 128]])
                            DVE(lambda: nc.vector.tensor_tensor(out=S32[g][:], in0=S32[g][:], in1=dec, op=ALU.mult),
                                r=[S32[g], hd], w=[S32[g]])
                            DVE(lambda: nc.vector.tensor_tensor(out=S32[g][:], in0=S32[g][:],
                                                                in1=bd[:].rearrange("p (j e) -> p j e", j=4), op=ALU.add),
                                r=[S32[g], bd], w=[S32[g]])
                            ACT(lambda: nc.scalar.copy(out=Sb[g][:], in_=S32[g][:]), r=[S32[g]], w=[Sb[g]])
                            ACT(lambda: nc.scalar.activation(out=osq[j2][:], in_=ob_[:], func=AF.Square), r=[ob_], w=[osq[j2]])
                            bs = banks[7]
                            PE(lambda: nc.tensor.matmul(bs[:], lhsT=ones_b[:], rhs=osq[j2][:], start=True, stop=True),
                               r=[ones_b, osq[j2]], w=[bs])
                            ACT(lambda: nc.scalar.activation(out=lnr[j2][:], in_=bs[:], func=AF.Ln, scale=1.0 / 128.0, bias=RMS_EPS),
                                r=[bs], w=[lnr[j2]])
                            ACT(lambda: nc.scalar.activation(out=lnr[j2][:], in_=lnr[j2][:], func=AF.Exp, scale=-0.5),
                                r=[lnr[j2]], w=[lnr[j2]])
                            DVE(lambda: nc.vector.tensor_tensor(out=on[j2][:], in0=ob_[:], in1=lnr[j2][:], op=ALU.mult),
                                r=[ob_, lnr[j2]], w=[on[j2]])
                            POOL(lambda: nc.gpsimd.tensor_tensor(out=OG[:, 4 * g:4 * g + 4, cs],
                                                                 in0=on[j2][:].rearrange("p (j t) -> p j t", j=4),
                                                                 in1=gt[:, 4 * g:4 * g + 4, cs], op=ALU.mult),
                                 r=[on[j2], gt], w=[OG])
                    if dbg_og is not None:
                        sch.dma("sp", fm(dbg_og.t, t0), OG[:], r=[OG], w=[dbg_og])
                    for oc in range(8):
                        bk = nextps(4)
                        for kc in range(8):
                            PE(lambda: nc.tensor.matmul(bk[:], lhsT=Wo[:, kc, oc * 128:(oc + 1) * 128], rhs=OG[:, kc, :],
                                                        start=(kc == 0), stop=(kc == 7)), r=[Wo, OG], w=[bk], ms=(kc == 7))
                        DVE(lambda: nc.vector.scalar_tensor_tensor(out=y[:, oc, :], in0=y[:, oc, :], scalar=ALPHA, in1=bk[:],
                                                                   op0=ALU.mult, op1=ALU.add), r=[y, bk], w=[y])
                    ln_tile(y, scr, scr, st, tmp, gcol, bcol, nb=4)
                    sch.dma("sp", fm(dst.t, t0), y[:], r=[y], w=[dst])
                sch.barrier()

        out_t = T(out)
        if upto >= 3:
            stage_proj_ln(mla_w_o, sc_o, sc_xT, sc_x1, c_lng[0][0], c_lnb[0][0])
        if upto >= 4:
            stage_ffn(0, sc_x1, sc_xT, c_lng[0][1], c_lnb[0][1])
        if upto >= 5:
            stage_ple(0, sc_xT, sc_x1, False)
        if upto >= 6:
            stage_hgrn(sc_x1, sc_xT, c_lng[1][0], c_lnb[1][0])
        if upto >= 7:
            stage_ffn(1, sc_xT, sc_x1, c_lng[1][1], c_lnb[1][1])
        if upto >= 8:
            stage_ple(1, sc_x1, out_t, True)

        sch.barrier()
    return nc


def make_consts():
    c = np.zeros((128, 1408), np.float32)
    c[:, 0:128] = np.eye(128, dtype=np.float32)
    k = np.arange(128)[:, None]
    q = np.arange(128)[None, :]
    c[:, 128:256] = np.where(k <= q, 0.0, -30000.0)
    s = np.arange(64)[:, None]
    t = np.arange(64)[None, :]
    c[0:64, 256:768] = np.tile((s <= t).astype(np.float32), (1, 8))
    c[:, 768:1280] = (np.arange(512) % 128 != 0).astype(np.float32)[None, :]
    inv = (10000.0 ** (-np.arange(0, 64, 2, dtype=np.float32) / 64.0)).astype(np.float32)
    c[0:64, 1280] = np.concatenate([inv, inv])
    c[0:64, 1281] = np.concatenate([-inv, inv])
    return c


_W_NAMES = ["mla_w_dqkv", "mla_w_uq", "mla_w_ukv", "mla_w_o", "hgrn_w_in", "hgrn_w_o"]
_W_FULL = ["ffn_w_in", "ffn_w_down", "ple_w_proj", "ple_w_gate"]


def make_pvec(inputs):
    def col(v):
        return np.asarray(v, dtype=np.float32).reshape(-1, 128).T
    cols = [col(inputs["mla_q_norm"][0]), col(inputs["mla_kv_norm"][0])]
    for nm in ("ln_mix_g", "ln_ffn_g"):
        pass
    for i in range(2):
        cols += [col(inputs["ln_mix_g"][i]), col(inputs["ln_ffn_g"][i])]
    for i in range(2):
        cols += [col(inputs["ln_mix_b"][i]), col(inputs["ln_ffn_b"][i])]
    cols += [col(inputs["hgrn_out_norm"][0]), col(inputs["hgrn_lb_logits"][0]), col(inputs["hgrn_lb_logits"][1])]
    pv = np.concatenate(cols, axis=1)
    out = np.zeros((128, 96), np.float32)
    out[:, :pv.shape[1]] = pv
    return out


def make_in_maps(inputs, n_cores, S):
    consts = make_consts()
    shared = {"consts": consts, "pvec": make_pvec(inputs)}
    for k in _W_NAMES:
        shared[k] = np.ascontiguousarray(np.asarray(inputs[k], dtype=np.float32)[0])
    for k in _W_FULL:
        shared[k] = np.ascontiguousarray(np.asarray(inputs[k], dtype=np.float32))
    maps = []
    xs = np.asarray(inputs["x"])
    ps = np.asarray(inputs["p"])
    po = np.asarray(inputs["positions"])
    for b in range(n_cores):
        m = dict(shared)
        m["x"] = np.ascontiguousarray(xs[b, :S])
        m["p"] = np.ascontiguousarray(ps[:, b, :S])
        m["pos"] = np.ascontiguousarray(po[b:b + 1, :S]).astype(np.int32)
        maps.append(m)
    return maps


def kernel(**inputs):
    n = 8
    S = 8192
    nc = build(S)
    maps = make_in_maps(inputs, n, S)
    res = run_bass_kernel_spmd(nc, maps, core_ids=list(range(n)))
    return np.stack([np.asarray(r["out"]) for r in res.results], axis=0).astype(np.float32)
```

```python
import math
from contextlib import ExitStack

import numpy as np
import concourse.bass as bass
import concourse.mybir as mybir
from concourse.bass_utils import run_bass_kernel_spmd

F32 = mybir.dt.float32
BF16 = mybir.dt.bfloat16
I32 = mybir.dt.int32
AF = mybir.ActivationFunctionType
ALU = mybir.AluOpType

D = 1024
DFF = 2816
ALPHA = 4.0 ** 0.25
LN_EPS = 1e-5
RMS_EPS = 1e-6
TT = 512


class Buf:
    __slots__ = ("w", "r")

    def __init__(self):
        self.w = None
        self.r = {}


class T:
    def __init__(self, t, excl=False):
        self.t = t
        self.b = Buf()
        self.excl = excl

    def __getitem__(self, k):
        return self.t[k]


class Sched:
    def __init__(self, nc, es):
        self.nc = nc
        self.eng = {"pe": nc.tensor, "act": nc.scalar, "dve": nc.vector, "pool": nc.gpsimd, "sp": nc.sync}
        self.sems = {}
        self.cnt = {}
        self.seen = {k: {} for k in self.eng}
        for k in self.eng:
            self.sems[k] = es.enter_context(nc.semaphore("s_" + k))
            self.cnt[k] = 0
        self.ring = {}
        for q, n in (("sp", 8), ("pool", 8), ("act", 4)):
            lst = []
            for i in range(n):
                key = (q, i)
                self.sems[key] = es.enter_context(nc.semaphore("d_%s%d" % (q, i)))
                self.cnt[key] = 0
                lst.append(key)
            self.ring[q] = [lst, 0]
        self.nops = 0

    def _need(self, e, reads, writes):
        need = {}

        def add(ev, raw):
            if ev is None:
                return
            key, val = ev
            if key == e and e == "pe":
                return
            if need.get(key, 0) < val:
                need[key] = val

        for b in reads:
            add(b.b.w, True)
            if b.excl:
                for k, v in b.b.r.items():
                    if k != e:
                        add((k, v), False)
        for b in writes:
            add(b.b.w, False)
            for k, v in b.b.r.items():
                add((k, v), False)
        seen = self.seen[e]
        for key, val in need.items():
            assert val <= self.cnt[key], ("wait on a milestone not yet emitted", e, key, val, self.cnt[key])
            if seen.get(key, 0) < val:
                self.eng[e].wait_ge(self.sems[key], val)
                seen[key] = val

    def _record(self, ev, reads, writes):
        key, val = ev
        for b in reads:
            b.b.r[key] = val
        for b in writes:
            b.b.w = ev
            b.b.r = {}

    def op(self, e, fn, r=(), w=(), ms=True):
        self._need(e, r, w)
        inst = fn()
        val = self.cnt[e] + 1
        if ms:
            inst.then_inc(self.sems[e], 1)
            self.cnt[e] = val
        self._record((e, val), r, w)
        self.nops += 1
        return inst

    def dma(self, q, out, in_, r=(), w=()):
        self._need(q, r, w)
        lst, i = self.ring[q]
        key = lst[i % len(lst)]
        self.ring[q][1] = i + 1
        prior = self.cnt[key]
        if prior > 0 and self.seen[q].get(key, 0) < prior:
            self.eng[q].wait_ge(self.sems[key], prior)
            self.seen[q][key] = prior
        inst = self.eng[q].dma_start(out=out, in_=in_)
        inst.then_inc(self.sems[key], 16)
        self.cnt[key] = prior + 16
        self._record((key, prior + 16), r, w)
        self.nops += 1
        return inst

    def barrier(self):
        for e in self.eng:
            for key, val in self.cnt.items():
                if key == e or val == 0:
                    continue
                if self.seen[e].get(key, 0) < val:
                    self.eng[e].wait_ge(self.sems[key], val)
                    self.seen[e][key] = val


def build(S=8192, upto=99, debug=False):
    nc = bass.Bass("TRN2", target_bir_lowering=False)
    NT = S // TT
    NB = S // 128
    dbg_kind = "ExternalOutput" if debug else "Internal"

    def dram(name, shape, dt, kind=None):
        return nc.dram_tensor(name, list(shape), dt, kind=kind or dbg_kind).ap()

    def din(name, shape, dt=F32):
        return dram(name, shape, dt, "ExternalInput")

    x = din("x", [S, D])
    p_in = din("p", [2, S, 256])
    pos = din("pos", [1, S], I32)
    consts = din("consts", [128, 1408])
    pvec = din("pvec", [128, 96])
    w_dqkv = din("mla_w_dqkv", [D, 576])
    w_uq = din("mla_w_uq", [256, 1536])
    w_ukv = din("mla_w_ukv", [256, 2048])
    mla_w_o = din("mla_w_o", [D, D])
    hg_w_in = din("hgrn_w_in", [D, 4096])
    hg_w_o = din("hgrn_w_o", [D, D])
    ffn_w_in = din("ffn_w_in", [2, D, 2 * DFF])
    ffn_w_dn = din("ffn_w_down", [2, DFF, D])
    ple_w_proj = din("ple_w_proj", [2, 256, D])
    ple_w_gate = din("ple_w_gate", [2, D, D])
    out = dram("out", [S, D], F32, "ExternalOutput")

    sc_cc = T(dram("sc_cc", [64, S], F32))
    sc_ss = T(dram("sc_ss", [64, S], F32))
    sc_xT = T(dram("sc_xT", [D, S], F32))
    sc_x1 = T(dram("sc_x1", [D, S], F32))
    sc_qn = T(dram("sc_qn", [8, 128, S], BF16))
    sc_qr = T(dram("sc_qr", [8, 64, S], BF16))
    sc_kn = T(dram("sc_kn", [8, 128, S], BF16))
    sc_kr = T(dram("sc_kr", [64, S], BF16))
    sc_v = T(dram("sc_v", [S, D], BF16))
    sc_o = T(dram("sc_o", [D, S], BF16))
    sc_hqa = T(dram("sc_hqa", [8, 128, S], BF16))
    sc_hqs = T(dram("sc_hqs", [8, 128, S], BF16))
    sc_hka = T(dram("sc_hka", [8, 128, S], BF16))
    sc_hkh = T(dram("sc_hkh", [8, 128, S], BF16))
    sc_hg = T(dram("sc_hg", [8, 128, S], BF16))
    sc_hv = T(dram("sc_hv", [S, D], BF16))
    sc_hd = T(dram("sc_hd", [128, 8, S // 128], F32))
    dbg_og = T(dram("dbg_og", [D, S], BF16)) if debug else None

    es = ExitStack()
    with es:
        sch = Sched(nc, es)
        PE = lambda fn, r=(), w=(), ms=True: sch.op("pe", fn, r, w, ms)
        ACT = lambda fn, r=(), w=(): sch.op("act", fn, r, w)
        DVE = lambda fn, r=(), w=(): sch.op("dve", fn, r, w)
        POOL = lambda fn, r=(), w=(): sch.op("pool", fn, r, w)

        uniq = [0]

        def sb(es_, name, shape, dt=F32):
            uniq[0] += 1
            return T(es_.enter_context(nc.sbuf_tensor("%s_%d" % (name, uniq[0]), list(shape), dt)))

        banks = [T(es.enter_context(nc.psum_tensor("ps%d" % i, [128, 512], F32)), excl=True) for i in range(8)]
        bank_i = [0]

        def nextps(n=8):
            b = banks[bank_i[0] % n]
            bank_i[0] += 1
            return b

        cst = sb(es, "cst", [128, 1408])
        sch.dma("sp", cst[:], consts[:, :], w=[cst])
        ident_f = cst[:, 0:128]
        ident_b = sb(es, "ident_b", [128, 128], BF16)
        maskb = sb(es, "maskb", [128, 128], BF16)
        amask = sb(es, "amask", [64, 512], BF16)
        ones_b = sb(es, "ones_b", [128, 128], BF16)
        DVE(lambda: nc.vector.tensor_copy(out=ident_b[:], in_=cst[:, 0:128]), r=[cst], w=[ident_b])
        DVE(lambda: nc.vector.tensor_copy(out=maskb[:], in_=cst[:, 128:256]), r=[cst], w=[maskb])
        DVE(lambda: nc.vector.tensor_copy(out=amask[:], in_=cst[0:64, 256:768]), r=[cst], w=[amask])
        DVE(lambda: nc.vector.memset(ones_b[:], 1.0), w=[ones_b])
        rmask = cst[:, 768:1280]
        pv = sb(es, "pv", [128, 96])
        sch.dma("sp", pv[:], pvec[:, :], w=[pv])
        c_qn, c_kvn = 0, 2
        c_lng = [[4, 12], [20, 28]]
        c_lnb = [[36, 44], [52, 60]]
        c_on, c_l0, c_l1 = 68, 76, 84
        lbv = sb(es, "lbv", [128, 24])
        DVE(lambda: nc.vector.tensor_tensor(out=lbv[:, 16:24], in0=pv[:, c_l1:c_l1 + 8], in1=pv[:, c_l0:c_l0 + 8],
                                            op=ALU.subtract), r=[pv], w=[lbv])
        ACT(lambda: nc.scalar.activation(out=lbv[:, 0:8], in_=lbv[:, 16:24], func=AF.Sigmoid), r=[lbv], w=[lbv])
        DVE(lambda: nc.vector.tensor_scalar(out=lbv[:, 8:16], in0=lbv[:, 0:8], scalar1=-1.0, scalar2=1.0,
                                            op0=ALU.mult, op1=ALU.add), r=[lbv], w=[lbv])

        TWO_PI = 2.0 * math.pi
        C1 = 6.28125
        C2 = TWO_PI - C1
        with ExitStack() as e1:
            RC = min(S, 2048)
            pi_ = sb(e1, "r_pi", [64, RC], I32)
            pf = sb(e1, "r_pf", [64, RC])
            for tab in range(2):
                kf = sb(e1, "r_kf%d" % tab, [64, RC])
                ki = sb(e1, "r_ki%d" % tab, [64, RC], I32)
                ang = sb(e1, "r_ang%d" % tab, [64, RC])
                tm = sb(e1, "r_tm%d" % tab, [64, RC])
                res = sb(e1, "r_res%d" % tab, [64, RC])
                for c0 in range(0, S, RC):
                    if tab == 0:
                        pass
                    src = bass.AP(pos.tensor, c0, [[0, 64], [1, RC]])
                    sch.dma("sp", pi_[:], src, w=[pi_])
                    DVE(lambda: nc.vector.tensor_copy(out=pf[:], in_=pi_[:]), r=[pi_], w=[pf])
                    fcol = cst[0:64, 1280 + tab:1281 + tab]
                    DVE(lambda: nc.vector.tensor_scalar(out=ang[:], in0=pf[:], scalar1=fcol, scalar2=None,
                                                        op0=ALU.mult), r=[pf, cst], w=[ang])
                    DVE(lambda: nc.vector.tensor_scalar(out=kf[:], in0=ang[:], scalar1=1.0 / TWO_PI, scalar2=None,
                                                        op0=ALU.mult), r=[ang], w=[kf])
                    DVE(lambda: nc.vector.tensor_copy(out=ki[:], in_=kf[:]), r=[kf], w=[ki])
                    DVE(lambda: nc.vector.tensor_copy(out=kf[:], in_=ki[:]), r=[ki], w=[kf])
                    DVE(lambda: nc.vector.scalar_tensor_tensor(out=tm[:], in0=kf[:], scalar=-C1, in1=ang[:],
                                                               op0=ALU.mult, op1=ALU.add), r=[kf, ang], w=[tm])
                    DVE(lambda: nc.vector.scalar_tensor_tensor(out=ang[:], in0=kf[:], scalar=-C2, in1=tm[:],
                                                               op0=ALU.mult, op1=ALU.add), r=[kf, tm], w=[ang])
                    if tab == 0:
                        DVE(lambda: nc.vector.tensor_scalar(out=ang[:], in0=ang[:], scalar1=math.pi / 2, scalar2=None,
                                                            op0=ALU.add), r=[ang], w=[ang])
                    for _ in range(2):
                        DVE(lambda: nc.vector.tensor_scalar(out=tm[:], in0=ang[:], scalar1=math.pi, scalar2=-TWO_PI,
                                                            op0=ALU.is_gt, op1=ALU.mult), r=[ang], w=[tm])
                        DVE(lambda: nc.vector.tensor_tensor(out=ang[:], in0=ang[:], in1=tm[:], op=ALU.add),
                            r=[ang, tm], w=[ang])
                        DVE(lambda: nc.vector.tensor_scalar(out=tm[:], in0=ang[:], scalar1=-math.pi, scalar2=TWO_PI,
                                                            op0=ALU.is_lt, op1=ALU.mult), r=[ang], w=[tm])
                        DVE(lambda: nc.vector.tensor_tensor(out=ang[:], in0=ang[:], in1=tm[:], op=ALU.add),
                            r=[ang, tm], w=[ang])
                    DVE(lambda: nc.vector.tensor_scalar(out=ang[:], in0=ang[:], scalar1=3.14159, scalar2=-3.14159,
                                                        op0=ALU.min, op1=ALU.max), r=[ang], w=[ang])
                    ACT(lambda: nc.scalar.activation(out=res[:], in_=ang[:], func=AF.Sin), r=[ang], w=[res])
                    dst = sc_cc if tab == 0 else sc_ss
                    sch.dma("sp", dst[:, c0:c0 + RC], res[:], r=[res], w=[dst])
            sch.barrier()

        def load_w(q, dst_tile, dst_ap, src_ap):
            sch.dma(q, dst_ap, src_ap, w=[dst_tile])

        if upto >= 1:
            with ExitStack() as e1:
                Wd = sb(e1, "Wd", [128, 8, 640], BF16)
                Wuq = sb(e1, "Wuq", [128, 2, 8, 256], BF16)
                Wkv = sb(e1, "Wkv", [128, 2, 2, 8, 128], BF16)
                wdv = w_dqkv.rearrange("(c p) n -> p c n", p=128)
                load_w("pool", Wd, Wd[:, :, 0:576], wdv)
                load_w("pool", Wd, Wd[:, :, 576:608], wdv[:, :, 544:576])
                load_w("pool", Wd, Wd[:, :, 608:640], wdv[:, :, 512:544])
                for kc in range(2):
                    wv = w_uq[kc * 128:(kc + 1) * 128, :].rearrange("p (h d) -> p h d", h=8)
                    load_w("pool", Wuq, Wuq[:, kc, :, 0:192], wv)
                    load_w("pool", Wuq, Wuq[:, kc, :, 192:224], wv[:, :, 160:192])
                    load_w("pool", Wuq, Wuq[:, kc, :, 224:256], wv[:, :, 128:160])
                    wk = w_ukv[kc * 128:(kc + 1) * 128, :].rearrange("p (h kv d) -> p kv h d", h=8, kv=2)
                    for kv in range(2):
                        load_w("pool", Wkv, Wkv[:, kc, kv, :, :], wk[:, kv, :, :])
                xtok = [sb(e1, "xtok%d" % i, [128, 4, D]) for i in range(2)]
                xT32 = [sb(e1, "xT32_%d" % i, [128, 8, TT]) for i in range(2)]
                xTb = sb(e1, "xTb", [128, 8, TT], BF16)
                sq = sb(e1, "sq", [128, 4, TT], BF16)
                lnt = sb(e1, "lnt", [128, 2, TT])
                rstd = sb(e1, "rstd", [128, 2, TT])
                cl = sb(e1, "cl", [128, 4, TT], BF16)
                cct = [sb(e1, "cct%d" % i, [64, TT]) for i in range(2)]
                sst = [sb(e1, "sst%d" % i, [64, TT]) for i in range(2)]
                t1 = sb(e1, "t1", [64, TT])
                t2 = sb(e1, "t2", [64, TT])
                krb = sb(e1, "krb", [64, TT], BF16)
                Qn = sb(e1, "Qn", [128, 8, TT], BF16)
                Qr = sb(e1, "Qr", [64, 8, TT], BF16)
                Kn = sb(e1, "Kn", [128, 8, TT], BF16)
                Vt = sb(e1, "Vt", [128, 4, D], BF16)

                def load_x(it):
                    t0 = it * TT
                    sch.dma("sp", xtok[it % 2][:], x[t0:t0 + TT, :].rearrange("(n p) d -> p n d", p=128),
                            w=[xtok[it % 2]])
                    sch.dma("sp", cct[it % 2][:], sc_cc[:, t0:t0 + TT], r=[sc_cc], w=[cct[it % 2]])
                    sch.dma("sp", sst[it % 2][:], sc_ss[:, t0:t0 + TT], r=[sc_ss], w=[sst[it % 2]])

                import os as _os
                _kstop = int(_os.environ.get("KSTOP", "99"))
                load_x(0)
                for it in range(NT if _kstop > 0 else 0):
                    t0 = it * TT
                    if it + 1 < NT:
                        load_x(it + 1)
                    xt = xtok[it % 2]
                    x32 = xT32[it % 2]
                    CCt = cct[it % 2]
                    SSt = sst[it % 2]
                    for c in range(8):
                        bk = nextps()
                        for n in range(4):
                            PE(lambda: nc.tensor.transpose(out=bk[:, n * 128:(n + 1) * 128],
                                                           in_=xt[:, n, c * 128:(c + 1) * 128], identity=ident_f),
                               r=[xt, cst], w=[bk])
                        ACT(lambda: nc.scalar.copy(out=x32[:, c, :], in_=bk[:]), r=[bk], w=[x32])
                        DVE(lambda: nc.vector.tensor_copy(out=xTb[:, c, :], in_=x32[:, c, :]), r=[x32], w=[xTb])
                    sch.dma("sp", sc_xT[:, t0:t0 + TT].rearrange("(c p) t -> p c t", p=128), x32[:], r=[x32], w=[sc_xT])
                    if _kstop <= 1:
                        continue
                    dps = []
                    for oc in range(4):
                        bk = nextps()
                        for kc in range(8):
                            PE(lambda: nc.tensor.matmul(bk[:], lhsT=Wd[:, kc, oc * 128:(oc + 1) * 128], rhs=xTb[:, kc, :],
                                                        start=(kc == 0), stop=(kc == 7)), r=[Wd, xTb], w=[bk])
                        ACT(lambda: nc.scalar.activation(out=sq[:, oc, :], in_=bk[:], func=AF.Square), r=[bk], w=[sq])
                        dps.append(bk)
                    for g in range(2):
                        bs = nextps()
                        for j in range(2):
                            PE(lambda: nc.tensor.matmul(bs[:], lhsT=ones_b[:], rhs=sq[:, 2 * g + j, :],
                                                        start=(j == 0), stop=(j == 1)), r=[ones_b, sq], w=[bs])
                        ACT(lambda: nc.scalar.activation(out=lnt[:, g, :], in_=bs[:], func=AF.Ln, scale=1.0 / 256.0,
                                                         bias=RMS_EPS), r=[bs], w=[lnt])
                        ACT(lambda: nc.scalar.activation(out=rstd[:, g, :], in_=lnt[:, g, :], func=AF.Exp, scale=-0.5),
                            r=[lnt], w=[rstd])
                        for j in range(2):
                            col = (c_qn if g == 0 else c_kvn) + j
                            DVE(lambda: nc.vector.scalar_tensor_tensor(out=cl[:, 2 * g + j, :], in0=dps[2 * g + j][:],
                                                                       scalar=pv[:, col:col + 1], in1=rstd[:, g, :],
                                                                       op0=ALU.mult, op1=ALU.mult),
                                r=[dps[2 * g + j], pv, rstd], w=[cl])

                    def rope(b1, b2, dst_ap, dst_t):
                        DVE(lambda: nc.vector.tensor_tensor(out=t1[:], in0=b1[0:64, :], in1=CCt[:], op=ALU.mult),
                            r=[b1, CCt], w=[t1])
                        DVE(lambda: nc.vector.tensor_tensor(out=t2[:], in0=b2[0:64, :], in1=SSt[:], op=ALU.mult),
                            r=[b2, SSt], w=[t2])
                        DVE(lambda: nc.vector.tensor_tensor(out=dst_ap, in0=t1[:], in1=t2[:], op=ALU.add),
                            r=[t1, t2], w=[dst_t])

                    if _kstop <= 2:
                        continue
                    b1 = nextps()
                    b2 = nextps()
                    for bk, c0 in ((b1, 512), (b2, 576)):
                        for kc in range(8):
                            PE(lambda: nc.tensor.matmul(bk[0:64, :], lhsT=Wd[:, kc, c0:c0 + 64], rhs=xTb[:, kc, :],
                                                        start=(kc == 0), stop=(kc == 7)), r=[Wd, xTb], w=[bk])
                    rope(b1, b2, krb[:], krb)
                    sch.dma("sp", sc_kr[:, t0:t0 + TT], krb[:], r=[krb], w=[sc_kr])
                    if _kstop <= 3:
                        continue
                    for h in range(8):
                        bn = nextps()
                        b1 = nextps()
                        b2 = nextps()
                        for kc in range(2):
                            PE(lambda: nc.tensor.matmul(bn[:], lhsT=Wuq[:, kc, h, 0:128], rhs=cl[:, kc, :],
                                                        start=(kc == 0), stop=(kc == 1)), r=[Wuq, cl], w=[bn])
                        for bk, c0 in ((b1, 128), (b2, 192)):
                            for kc in range(2):
                                PE(lambda: nc.tensor.matmul(bk[0:64, :], lhsT=Wuq[:, kc, h, c0:c0 + 64], rhs=cl[:, kc, :],
                                                            start=(kc == 0), stop=(kc == 1)), r=[Wuq, cl], w=[bk])
                        ACT(lambda: nc.scalar.copy(out=Qn[:, h, :], in_=bn[:]), r=[bn], w=[Qn])
                        rope(b1, b2, Qr[:, h, :], Qr)
                    sch.dma("sp", sc_qn[:, :, t0:t0 + TT].rearrange("h p t -> p h t"), Qn[:], r=[Qn], w=[sc_qn])
                    sch.dma("sp", sc_qr[:, :, t0:t0 + TT].rearrange("h p t -> p h t"), Qr[:], r=[Qr], w=[sc_qr])
                    if _kstop <= 4:
                        continue
                    for h in range(8):
                        bk = nextps()
                        for kc in range(2):
                            PE(lambda: nc.tensor.matmul(bk[:], lhsT=Wkv[:, kc, 0, h, :], rhs=cl[:, 2 + kc, :],
                                                        start=(kc == 0), stop=(kc == 1)), r=[Wkv, cl], w=[bk])
                        ACT(lambda: nc.scalar.copy(out=Kn[:, h, :], in_=bk[:]), r=[bk], w=[Kn])
                    sch.dma("sp", sc_kn[:, :, t0:t0 + TT].rearrange("h p t -> p h t"), Kn[:], r=[Kn], w=[sc_kn])
                    for n in range(4):
                        for hf in range(2):
                            bk = nextps()
                            for kc in range(2):
                                PE(lambda: nc.tensor.matmul(bk[:], lhsT=cl[:, 2 + kc, n * 128:(n + 1) * 128],
                                                            rhs=Wkv[:, kc, 1, 4 * hf:4 * hf + 4, :],
                                                            start=(kc == 0), stop=(kc == 1)), r=[Wkv, cl], w=[bk])
                            DVE(lambda: nc.vector.tensor_copy(out=Vt[:, n, hf * 512:(hf + 1) * 512], in_=bk[:]),
                                r=[bk], w=[Vt])
                    sch.dma("sp", sc_v[t0:t0 + TT, :].rearrange("(n p) d -> p n d", p=128), Vt[:], r=[Vt], w=[sc_v])
                sch.barrier()

        if upto >= 2:
            SCALE = 192.0 ** -0.5
            import os as _os2
            ATT_MODE = int(_os2.environ.get("ATT_MODE", "0"))
            with ExitStack() as e1:
                Kr = sb(e1, "Kr", [128, S], BF16)
                Knh = [sb(e1, "Knh%d" % i, [128, S], BF16) for i in range(2)]
                Vh = [sb(e1, "Vh%d" % i, [128, NB, 128], BF16) for i in range(2)]
                Qnb = [sb(e1, "Qnb%d" % i, [128, TT], BF16) for i in range(2)]
                Qrb = [sb(e1, "Qrb%d" % i, [128, TT], BF16) for i in range(2)]
                NPT = 4
                pt = [sb(e1, "pt%d" % i, [128, TT], BF16) for i in range(NPT)]
                pacc = [[sb(e1, "pacc", [128, TT]) for _ in range(2)] for _ in range(2)]
                dsum = [sb(e1, "dsum", [128, TT]) for _ in range(2)]
                ones_f = sb(e1, "ones_f", [128, 128])
                DVE(lambda: nc.vector.memset(ones_f[:], 1.0), w=[ones_f])
                rcp = [sb(e1, "rcp%d" % i, [128, TT]) for i in range(2)]
                ob = [sb(e1, "ob%d" % i, [128, TT], BF16) for i in range(2)]
                NSB = 4
                LOOK = 2
                sbank = banks[0:NSB]
                accs = [(banks[4], banks[5]), (banks[6], banks[7])]
                POOL(lambda: nc.gpsimd.memset(Kr[64:128, :], 0.0), w=[Kr])
                for i_ in range(2):
                    POOL(lambda: nc.gpsimd.memset(Qrb[i_][64:128, :], 0.0), w=[Qrb[i_]])
                sch.dma("sp", Kr[0:64, :], sc_kr[:, :], r=[sc_kr], w=[Kr])

                def load_head(h):
                    sch.dma("sp", Knh[h % 2][:], sc_kn[h, :, :], r=[sc_kn], w=[Knh[h % 2]])
                    sch.dma("sp", Vh[h % 2][:], sc_v[:, h * 128:(h + 1) * 128].rearrange("(n p) d -> p n d", p=128),
                            r=[sc_v], w=[Vh[h % 2]])

                def load_q(h, qb, i):
                    sch.dma("sp", Qnb[i % 2][:], sc_qn[h, :, qb * TT:(qb + 1) * TT], r=[sc_qn], w=[Qnb[i % 2]])
                    sch.dma("sp", Qrb[i % 2][0:64, :], sc_qr[h, :, qb * TT:(qb + 1) * TT], r=[sc_qr], w=[Qrb[i % 2]])

                load_head(0)
                load_q(0, 0, 0)
                blk = 0
                ti = 0
                for h in range(8):
                    if h + 1 < 8:
                        load_head(h + 1)
                    K_ = Knh[h % 2]
                    V_ = Vh[h % 2]
                    for qb in range(NT):
                        nh, nq = (h, qb + 1) if qb + 1 < NT else (h + 1, 0)
                        if nh < 8:
                            load_q(nh, nq, blk + 1)
                        Qn_ = Qnb[blk % 2]
                        Qr_ = Qrb[blk % 2]
                        acc_o, acc_d = accs[blk % 2]
                        nk = 4 * qb + 4
                        pa = pacc[blk % 2]
                        DVE(lambda: nc.vector.memset(pa[0][:], 0.0), w=[pa[0]])
                        POOL(lambda: nc.gpsimd.memset(pa[1][:], 0.0), w=[pa[1]])

                        def qk(kb, sp_):
                            j = kb - 4 * qb
                            c0 = max(j, 0) * 128
                            ks = slice(kb * 128, (kb + 1) * 128)
                            PE(lambda: nc.tensor.matmul(sp_[:, c0:TT], lhsT=K_[:, ks], rhs=Qn_[:, c0:TT],
                                                        start=True, stop=False), r=[K_, Qn_], w=[sp_], ms=False)
                            PE(lambda: nc.tensor.matmul(sp_[:, c0:TT], lhsT=Kr[:, ks], rhs=Qr_[:, c0:TT],
                                                        start=False, stop=(j < 0)), r=[Kr, Qr_], w=[sp_], ms=(j < 0))
                            if j >= 0:
                                PE(lambda: nc.tensor.matmul(sp_[:, c0:c0 + 128], lhsT=ident_b[:], rhs=maskb[:],
                                                            start=False, stop=True), r=[ident_b, maskb], w=[sp_])
                            return c0

                        c0s = {}
                        for k0 in range(min(LOOK, nk)):
                            c0s[k0] = qk(k0, sbank[(ti + k0) % NSB])
                        for kb in range(nk):
                            sp_ = sbank[(ti + kb) % NSB]
                            p_ = pt[(ti + kb) % NPT]
                            c0 = c0s[kb]
                            ACT(lambda: nc.scalar.activation(out=p_[:, c0:TT], in_=sp_[:, c0:TT], func=AF.Exp,
                                                             scale=SCALE), r=[sp_], w=[p_])
                            if kb + LOOK < nk:
                                c0s[kb + LOOK] = qk(kb + LOOK, sbank[(ti + kb + LOOK) % NSB])
                            PE(lambda: nc.tensor.matmul(acc_o[:, c0:TT], lhsT=V_[:, kb, :], rhs=p_[:, c0:TT],
                                                        start=(kb == 0), stop=(kb == nk - 1)), r=[V_, p_], w=[acc_o])
                            if ATT_MODE == 0:
                                PE(lambda: nc.tensor.matmul(acc_d[:, c0:TT], lhsT=ones_b[:], rhs=p_[:, c0:TT],
                                                            start=(kb == 0), stop=(kb == nk - 1)), r=[ones_b, p_], w=[acc_d])
                            elif kb % 2 == 0 or ATT_MODE == 2:
                                DVE(lambda: nc.vector.tensor_tensor(out=pa[0][:, c0:TT], in0=pa[0][:, c0:TT], in1=p_[:, c0:TT],
                                                                    op=ALU.add), r=[pa[0], p_], w=[pa[0]])
                            else:
                                POOL(lambda: nc.gpsimd.tensor_tensor(out=pa[1][:, c0:TT], in0=pa[1][:, c0:TT], in1=p_[:, c0:TT],
                                                                     op=ALU.add), r=[pa[1], p_], w=[pa[1]])
                        ti += nk
                        rc = rcp[blk % 2]
                        o_ = ob[blk % 2]
                        ds_ = dsum[blk % 2]
                        if ATT_MODE != 0:
                            DVE(lambda: nc.vector.tensor_tensor(out=ds_[:], in0=pa[0][:], in1=pa[1][:], op=ALU.add),
                                r=[pa[0], pa[1]], w=[ds_])
                            PE(lambda: nc.tensor.matmul(acc_d[:], lhsT=ones_f[:], rhs=ds_[:], start=True, stop=True),
                               r=[ones_f, ds_], w=[acc_d])
                        DVE(lambda: nc.vector.reciprocal(out=rc[:], in_=acc_d[:]), r=[acc_d], w=[rc])
                        DVE(lambda: nc.vector.tensor_tensor(out=o_[:], in0=acc_o[:], in1=rc[:], op=ALU.mult),
                            r=[acc_o, rc], w=[o_])
                        sch.dma("sp", sc_o[h * 128:(h + 1) * 128, qb * TT:(qb + 1) * TT], o_[:], r=[o_], w=[sc_o])
                        blk += 1
                sch.barrier()


        def ln_tile(y, scr, scr_t, st, tmp, gcol, bcol, nb=8):
            mean, msq, lnv = st
            for c in range(8):
                POOL(lambda: nc.gpsimd.tensor_copy(out=scr[:, c, :], in_=y[:, c, :]), r=[y], w=[scr_t])
                ACT(lambda: nc.scalar.activation(out=scr[:, 8 + c, :], in_=y[:, c, :], func=AF.Square), r=[y], w=[scr_t])
            s1 = nextps(nb)
            s2 = nextps(nb)
            for c in range(8):
                PE(lambda: nc.tensor.matmul(s1[:], lhsT=ones_b[:], rhs=scr[:, c, :], start=(c == 0), stop=(c == 7)),
                   r=[ones_b, scr_t], w=[s1], ms=(c == 7))
            for c in range(8):
                PE(lambda: nc.tensor.matmul(s2[:], lhsT=ones_b[:], rhs=scr[:, 8 + c, :], start=(c == 0), stop=(c == 7)),
                   r=[ones_b, scr_t], w=[s2], ms=(c == 7))
            ACT(lambda: nc.scalar.activation(out=mean[:], in_=s1[:], func=AF.Copy, scale=1.0 / D), r=[s1], w=[mean])
            DVE(lambda: nc.vector.tensor_tensor(out=msq[:], in0=mean[:], in1=mean[:], op=ALU.mult), r=[mean], w=[msq])
            DVE(lambda: nc.vector.scalar_tensor_tensor(out=msq[:], in0=s2[:], scalar=1.0 / D, in1=msq[:],
                                                       op0=ALU.mult, op1=ALU.subtract), r=[s2, msq], w=[msq])
            ACT(lambda: nc.scalar.activation(out=lnv[:], in_=msq[:], func=AF.Ln, bias=LN_EPS), r=[msq], w=[lnv])
            ACT(lambda: nc.scalar.activation(out=lnv[:], in_=lnv[:], func=AF.Exp, scale=-0.5), r=[lnv], w=[lnv])
            for c in range(8):
                DVE(lambda: nc.vector.tensor_tensor(out=tmp[:], in0=y[:, c, :], in1=mean[:], op=ALU.subtract),
                    r=[y, mean], w=[tmp])
                DVE(lambda: nc.vector.tensor_tensor(out=tmp[:], in0=tmp[:], in1=lnv[:], op=ALU.mult),
                    r=[tmp, lnv], w=[tmp])
                ACT(lambda: nc.scalar.activation(out=y[:, c, :], in_=tmp[:], func=AF.Identity,
                                                 scale=pv[:, gcol + c:gcol + c + 1], bias=pv[:, bcol + c:bcol + c + 1]),
                    r=[tmp, pv], w=[y])

        def fm(ap2d, t0):
            return ap2d[:, t0:t0 + TT].rearrange("(c p) t -> p c t", p=128)

        def hm(ap3d, t0):
            return ap3d[:, :, t0:t0 + TT].rearrange("h p t -> p h t")

        def wload(dst, src2d, nkc):
            v = src2d.rearrange("(c p) n -> p c n", p=128)
            for kc in range(nkc):
                sch.dma("pool", dst[:, kc, :], v[:, kc, :], w=[dst])

        def stage_proj_ln(w_o_ap, src_o, src_x, dst, gcol, bcol):
            with ExitStack() as e1:
                Wo = sb(e1, "Wo", [128, 8, D], BF16)
                wload(Wo, w_o_ap, 8)
                oT = [sb(e1, "oT%d" % i, [128, 8, TT], BF16) for i in range(2)]
                yy = [sb(e1, "yy%d" % i, [128, 8, TT]) for i in range(2)]
                scr = sb(e1, "scr", [128, 16, TT], BF16)
                st = [sb(e1, "st%d" % i, [128, TT]) for i in range(3)]
                tmp = sb(e1, "tmp", [128, TT])

                def ld(it):
                    sch.dma("sp", oT[it % 2][:], fm(src_o.t, it * TT), r=[src_o], w=[oT[it % 2]])
                    sch.dma("sp", yy[it % 2][:], fm(src_x.t, it * TT), r=[src_x], w=[yy[it % 2]])

                ld(0)
                for it in range(NT):
                    if it + 1 < NT:
                        ld(it + 1)
                    o_ = oT[it % 2]
                    y = yy[it % 2]
                    for oc in range(8):
                        bk = nextps()
                        for kc in range(8):
                            PE(lambda: nc.tensor.matmul(bk[:], lhsT=Wo[:, kc, oc * 128:(oc + 1) * 128], rhs=o_[:, kc, :],
                                                        start=(kc == 0), stop=(kc == 7)), r=[Wo, o_], w=[bk], ms=(kc == 7))
                        DVE(lambda: nc.vector.scalar_tensor_tensor(out=y[:, oc, :], in0=y[:, oc, :], scalar=ALPHA, in1=bk[:],
                                                                   op0=ALU.mult, op1=ALU.add), r=[y, bk], w=[y])
                    ln_tile(y, scr, scr, st, tmp, gcol, bcol)
                    sch.dma("sp", fm(dst.t, it * TT), y[:], r=[y], w=[dst])
                sch.barrier()

        def stage_ffn(li, src, dst, gcol, bcol):
            with ExitStack() as e1:
                Win = sb(e1, "Win", [128, 8, 2 * DFF], BF16)
                Wdn = sb(e1, "Wdn", [128, 22, D], BF16)
                wload(Win, ffn_w_in[li], 8)
                wload(Wdn, ffn_w_dn[li], 22)
                y = sb(e1, "y", [128, 8, TT])
                xb = sb(e1, "xb", [128, 8, TT], BF16)
                hh = sb(e1, "hh", [128, 22, TT], BF16)
                sg = [sb(e1, "sg", [128, TT]) for i in range(2)]
                st = [sb(e1, "st", [128, TT]) for i in range(3)]
                tmp = sb(e1, "tmp", [128, TT])
                sc2 = [(sb(e1, "ybf", [128, TT], BF16), sb(e1, "ysq", [128, TT], BF16)) for i in range(2)]
                mean, msq, lnv = st
                s1, s2 = banks[6], banks[7]

                def load_xb(it):
                    sch.dma("pool", xb[:], fm(src.t, it * TT), r=[src], w=[xb])

                def load_y(it):
                    sch.dma("sp", y[:], fm(src.t, it * TT), r=[src], w=[y])

                def ln_gen(t0, nxt):
                    for c in range(8):
                        DVE(lambda: nc.vector.tensor_tensor(out=tmp[:], in0=y[:, c, :], in1=mean[:], op=ALU.subtract),
                            r=[y, mean], w=[tmp])
                        DVE(lambda: nc.vector.tensor_tensor(out=tmp[:], in0=tmp[:], in1=lnv[:], op=ALU.mult),
                            r=[tmp, lnv], w=[tmp])
                        ACT(lambda: nc.scalar.activation(out=y[:, c, :], in_=tmp[:], func=AF.Identity,
                                                         scale=pv[:, gcol + c:gcol + c + 1], bias=pv[:, bcol + c:bcol + c + 1]),
                            r=[tmp, pv], w=[y])
                        yield
                    sch.dma("sp", fm(dst.t, t0), y[:], r=[y], w=[dst])
                    if nxt is not None:
                        load_y(nxt)
                    yield

                def stats_mm(oc, k):
                    PE(lambda: nc.tensor.matmul(s1[:], lhsT=ones_b[:], rhs=sc2[k][0][:], start=(oc == 0), stop=(oc == 7)),
                       r=[ones_b, sc2[k][0]], w=[s1])
                    PE(lambda: nc.tensor.matmul(s2[:], lhsT=ones_b[:], rhs=sc2[k][1][:], start=(oc == 0), stop=(oc == 7)),
                       r=[ones_b, sc2[k][1]], w=[s2])

                load_xb(0)
                load_y(0)
                pending = None
                for it in range(NT):
                    for j in range(22):
                        bg = nextps(6)
                        bu = nextps(6)
                        for kc in range(8):
                            PE(lambda: nc.tensor.matmul(bg[:], lhsT=Win[:, kc, j * 128:(j + 1) * 128], rhs=xb[:, kc, :],
                                                        start=(kc == 0), stop=(kc == 7)), r=[Win, xb], w=[bg], ms=(kc == 7))
                        for kc in range(8):
                            PE(lambda: nc.tensor.matmul(bu[:], lhsT=Win[:, kc, DFF + j * 128:DFF + (j + 1) * 128],
                                                        rhs=xb[:, kc, :], start=(kc == 0), stop=(kc == 7)),
                               r=[Win, xb], w=[bu], ms=(kc == 7))
                        s_ = sg[j % 2]
                        ACT(lambda: nc.scalar.activation(out=s_[:], in_=bg[:], func=AF.Silu), r=[bg], w=[s_])
                        DVE(lambda: nc.vector.tensor_tensor(out=hh[:, j, :], in0=s_[:], in1=bu[:], op=ALU.mult),
                            r=[s_, bu], w=[hh])
                        if pending is not None and j >= 2:
                            next(pending, None)
                    if pending is not None:
                        for _ in pending:
                            pass
                        pending = None
                    if it + 1 < NT:
                        load_xb(it + 1)
                    for oc in range(8):
                        bk = nextps(6)
                        for j in range(22):
                            PE(lambda: nc.tensor.matmul(bk[:], lhsT=Wdn[:, j, oc * 128:(oc + 1) * 128], rhs=hh[:, j, :],
                                                        start=(j == 0), stop=(j == 21)), r=[Wdn, hh], w=[bk], ms=(j == 21))
                        DVE(lambda: nc.vector.scalar_tensor_tensor(out=y[:, oc, :], in0=y[:, oc, :], scalar=ALPHA, in1=bk[:],
                                                                   op0=ALU.mult, op1=ALU.add), r=[y, bk], w=[y])
                        k = oc % 2
                        DVE(lambda: nc.vector.tensor_copy(out=sc2[k][0][:], in_=y[:, oc, :]), r=[y], w=[sc2[k][0]])
                        ACT(lambda: nc.scalar.activation(out=sc2[k][1][:], in_=y[:, oc, :], func=AF.Square), r=[y], w=[sc2[k][1]])
                        if oc >= 1:
                            stats_mm(oc - 1, (oc - 1) % 2)
                    stats_mm(7, 1)
                    ACT(lambda: nc.scalar.activation(out=mean[:], in_=s1[:], func=AF.Copy, scale=1.0 / D), r=[s1], w=[mean])
                    DVE(lambda: nc.vector.tensor_tensor(out=msq[:], in0=mean[:], in1=mean[:], op=ALU.mult), r=[mean], w=[msq])
                    DVE(lambda: nc.vector.scalar_tensor_tensor(out=msq[:], in0=s2[:], scalar=1.0 / D, in1=msq[:],
                                                               op0=ALU.mult, op1=ALU.subtract), r=[s2, msq], w=[msq])
                    ACT(lambda: nc.scalar.activation(out=lnv[:], in_=msq[:], func=AF.Ln, bias=LN_EPS), r=[msq], w=[lnv])
                    ACT(lambda: nc.scalar.activation(out=lnv[:], in_=lnv[:], func=AF.Exp, scale=-0.5), r=[lnv], w=[lnv])
                    pending = ln_gen(it * TT, it + 1 if it + 1 < NT else None)
                for _ in pending:
                    pass
                sch.barrier()

        def stage_ple(li, src, dst, final):
            with ExitStack() as e1:
                Wg = sb(e1, "Wg", [128, 8, D], BF16)
                Wp = sb(e1, "Wp", [128, 2, D], BF16)
                wload(Wg, ple_w_gate[li], 8)
                wload(Wp, ple_w_proj[li], 2)
                if not final:
                    Wh = sb(e1, "Wh", [128, 8, 4096], BF16)
                    wload(Wh, hg_w_in, 8)
                NY = 2 if final else 1
                yy = [sb(e1, "y", [128, 8, TT]) for _ in range(NY)]
                xb = sb(e1, "xb", [128, 8, TT], BF16)
                ptoks = [sb(e1, "ptok", [128, 4, 256]) for _ in range(NY)]
                pT = sb(e1, "pT", [128, 2, TT], BF16)

                def ld_in(it):
                    t0_ = it * TT
                    sch.dma("sp", yy[it % NY][:], fm(src.t, t0_), r=[src], w=[yy[it % NY]])
                    sch.dma("sp", ptoks[it % NY][:], p_in[li, t0_:t0_ + TT, :].rearrange("(n p) d -> p n d", p=128),
                            w=[ptoks[it % NY]])

                ld_in(0)
                sg = [sb(e1, "sg%d" % i, [128, TT]) for i in range(2)]
                tmp = sb(e1, "tmp", [128, TT])
                if final:
                    otok = sb(e1, "otok", [128, 4, D])
                else:
                    FS = [dict((nm, sb(e1, "f_" + nm, [128, TT])) for nm in
                               ("sq", "ft", "kk", "lf", "G", "d1", "d2", "e0", "e1", "e2", "e3")) for _ in range(2)]
                    HO = [dict((nm, sb(e1, "ho_" + nm, [128, TT], BF16)) for nm in ("qa", "qs", "ka", "kh", "gt"))
                          for _ in range(2)]
                    HD = sb(e1, "HD", [128, 8, 4])
                    HV = sb(e1, "HV", [128, 4, D], BF16)
                for it in range(NT):
                    t0 = it * TT
                    y = yy[it % NY]
                    ptok = ptoks[it % NY]
                    if final and it + 1 < NT:
                        ld_in(it + 1)
                    for c in range(8):
                        DVE(lambda: nc.vector.tensor_copy(out=xb[:, c, :], in_=y[:, c, :]), r=[y], w=[xb])
                    for c2 in range(2):
                        bk = nextps()
                        for n in range(4):
                            PE(lambda: nc.tensor.transpose(out=bk[:, n * 128:(n + 1) * 128],
                                                           in_=ptok[:, n, c2 * 128:(c2 + 1) * 128], identity=ident_f),
                               r=[ptok, cst], w=[bk], ms=(n == 3))
                        ACT(lambda: nc.scalar.copy(out=pT[:, c2, :], in_=bk[:]), r=[bk], w=[pT])
                    for oc in range(8):
                        bg = nextps()
                        bp = nextps()
                        for kc in range(8):
                            PE(lambda: nc.tensor.matmul(bg[:], lhsT=Wg[:, kc, oc * 128:(oc + 1) * 128], rhs=xb[:, kc, :],
                                                        start=(kc == 0), stop=(kc == 7)), r=[Wg, xb], w=[bg], ms=(kc == 7))
                        for kc in range(2):
                            PE(lambda: nc.tensor.matmul(bp[:], lhsT=Wp[:, kc, oc * 128:(oc + 1) * 128], rhs=pT[:, kc, :],
                                                        start=(kc == 0), stop=(kc == 1)), r=[Wp, pT], w=[bp], ms=(kc == 1))
                        s_ = sg[oc % 2]
                        ACT(lambda: nc.scalar.activation(out=s_[:], in_=bg[:], func=AF.Sigmoid), r=[bg], w=[s_])
                        DVE(lambda: nc.vector.tensor_tensor(out=tmp[:], in0=s_[:], in1=bp[:], op=ALU.mult),
                            r=[s_, bp], w=[tmp])
                        DVE(lambda: nc.vector.tensor_tensor(out=y[:, oc, :], in0=y[:, oc, :], in1=tmp[:], op=ALU.add),
                            r=[y, tmp], w=[y])
                    if final:
                        for n in range(4):
                            for hf in range(2):
                                bk = nextps()
                                for c in range(4):
                                    PE(lambda: nc.tensor.transpose(out=bk[:, c * 128:(c + 1) * 128],
                                                                   in_=y[:, hf * 4 + c, n * 128:(n + 1) * 128],
                                                                   identity=ident_f), r=[y, cst], w=[bk], ms=(c == 3))
                                ACT(lambda: nc.scalar.copy(out=otok[:, n, hf * 512:(hf + 1) * 512], in_=bk[:]),
                                    r=[bk], w=[otok])
                        sch.dma("sp", out[t0:t0 + TT, :].rearrange("(n p) d -> p n d", p=128), otok[:], r=[otok], w=[dst])
                        continue
                    sch.dma("sp", fm(dst.t, t0), y[:], r=[y], w=[dst])
                    for c in range(8):
                        DVE(lambda: nc.vector.tensor_copy(out=xb[:, c, :], in_=y[:, c, :]), r=[y], w=[xb])
                    if it + 1 < NT:
                        ld_in(it + 1)

                    def proj(col0):
                        bk = nextps()
                        for kc in range(8):
                            PE(lambda: nc.tensor.matmul(bk[:], lhsT=Wh[:, kc, col0:col0 + 128], rhs=xb[:, kc, :],
                                                        start=(kc == 0), stop=(kc == 7)), r=[Wh, xb], w=[bk], ms=(kc == 7))
                        return bk

                    def bc(tl, pos_):
                        base = tl.t[:, pos_:pos_ + 1]
                        return bass.AP(base.tensor, base.offset, [[TT, 128], [128, 4], [0, 128]])

                    def v3(tl):
                        return tl[:].rearrange("p (c t) -> p c t", c=4)

                    def head_a(h):
                        F = FS[h % 2]
                        O = HO[h % 2]
                        bq = proj(h * 128)
                        ACT(lambda: nc.scalar.activation(out=F["sq"][:], in_=bq[:], func=AF.Silu), r=[bq], w=[F["sq"]])
                        bf = proj(1024 + h * 128)
                        ACT(lambda: nc.scalar.activation(out=F["ft"][:], in_=bf[:], func=AF.Sigmoid), r=[bf], w=[F["ft"]])
                        bgt = proj(3072 + h * 128)
                        ACT(lambda: nc.scalar.activation(out=O["gt"][:], in_=bgt[:], func=AF.Silu), r=[bgt], w=[O["gt"]])
                        DVE(lambda: nc.vector.tensor_scalar(out=F["ft"][:], in0=F["ft"][:], scalar1=lbv[:, 8 + h:9 + h],
                                                            scalar2=lbv[:, h:h + 1], op0=ALU.mult, op1=ALU.add),
                            r=[F["ft"], lbv], w=[F["ft"]])
                        DVE(lambda: nc.vector.tensor_scalar(out=F["kk"][:], in0=F["ft"][:], scalar1=-1.0, scalar2=1.0,
                                                            op0=ALU.mult, op1=ALU.add), r=[F["ft"]], w=[F["kk"]])

                    def head_b(h):
                        F = FS[h % 2]
                        O = HO[h % 2]
                        ACT(lambda: nc.scalar.activation(out=F["lf"][:], in_=F["ft"][:], func=AF.Ln), r=[F["ft"]], w=[F["lf"]])
                        DVE(lambda: nc.vector.tensor_tensor_scan(out=F["G"][:], data0=cst[:, 768:1280], data1=F["lf"][:],
                                                                 initial=0.0, op0=ALU.mult, op1=ALU.add),
                            r=[cst, F["lf"]], w=[F["G"]])
                        ACT(lambda: nc.scalar.activation(out=F["e0"][:], in_=F["G"][:], func=AF.Exp), r=[F["G"]], w=[F["e0"]])
                        DVE(lambda: nc.vector.tensor_tensor(out=v3(F["d1"]), in0=v3(F["G"]), in1=bc(F["G"], 63), op=ALU.subtract),
                            r=[F["G"]], w=[F["d1"]])
                        DVE(lambda: nc.vector.tensor_tensor(out=v3(F["d2"]), in0=bc(F["G"], 127), in1=v3(F["G"]), op=ALU.subtract),
                            r=[F["G"]], w=[F["d2"]])
                        ACT(lambda: nc.scalar.activation(out=F["e1"][:], in_=F["d1"][:], func=AF.Exp), r=[F["d1"]], w=[F["e1"]])
                        ACT(lambda: nc.scalar.activation(out=F["e2"][:], in_=F["d1"][:], func=AF.Exp, scale=-1.0),
                            r=[F["d1"]], w=[F["e2"]])
                        ACT(lambda: nc.scalar.activation(out=F["e3"][:], in_=F["d2"][:], func=AF.Exp), r=[F["d2"]], w=[F["e3"]])
                        DVE(lambda: nc.vector.tensor_tensor(out=O["qs"][:], in0=F["sq"][:], in1=F["e0"][:], op=ALU.mult),
                            r=[F["sq"], F["e0"]], w=[O["qs"]])
                        DVE(lambda: nc.vector.tensor_copy(out=HD[:, h, :], in_=F["e0"][:, 127:TT:128]), r=[F["e0"]], w=[HD])
                        POOL(lambda: nc.gpsimd.tensor_tensor(out=O["qa"][:], in0=F["sq"][:], in1=F["e1"][:], op=ALU.mult),
                             r=[F["sq"], F["e1"]], w=[O["qa"]])
                        DVE(lambda: nc.vector.tensor_tensor(out=O["ka"][:], in0=F["kk"][:], in1=F["e2"][:], op=ALU.mult),
                            r=[F["kk"], F["e2"]], w=[O["ka"]])
                        POOL(lambda: nc.gpsimd.tensor_tensor(out=O["kh"][:], in0=F["kk"][:], in1=F["e3"][:], op=ALU.mult),
                             r=[F["kk"], F["e3"]], w=[O["kh"]])
                        for nm, dstt in (("qa", sc_hqa), ("qs", sc_hqs), ("ka", sc_hka), ("kh", sc_hkh), ("gt", sc_hg)):
                            sch.dma("sp", dstt[h, :, t0:t0 + TT], O[nm][:], r=[O[nm]], w=[dstt])

                    head_a(0)
                    for h in range(8):
                        if h + 1 < 8:
                            head_a(h + 1)
                        head_b(h)
                    for n in range(4):
                        for hf in range(2):
                            bk = nextps()
                            for kc in range(8):
                                PE(lambda: nc.tensor.matmul(bk[:], lhsT=xb[:, kc, n * 128:(n + 1) * 128],
                                                            rhs=Wh[:, kc, 2048 + hf * 512:2048 + (hf + 1) * 512],
                                                            start=(kc == 0), stop=(kc == 7)), r=[Wh, xb], w=[bk], ms=(kc == 7))
                            DVE(lambda: nc.vector.tensor_copy(out=HV[:, n, hf * 512:(hf + 1) * 512], in_=bk[:]),
                                r=[bk], w=[HV])
                    sch.dma("sp", sc_hd[:, :, it * 4:(it + 1) * 4], HD[:], r=[HD], w=[sc_hd])
                    sch.dma("sp", sc_hv[t0:t0 + TT, :].rearrange("(n p) d -> p n d", p=128), HV[:], r=[HV], w=[sc_hv])
                sch.barrier()

        def stage_hgrn(src_x, dst, gcol, bcol):
            with ExitStack() as e1:
                Wo = sb(e1, "Who", [128, 8, D], BF16)
                wload(Wo, hg_w_o, 8)
                for kc in range(8):
                    DVE(lambda: nc.vector.tensor_scalar(out=Wo[:, kc, :], in0=Wo[:, kc, :], scalar1=pv[:, c_on + kc:c_on + kc + 1],
                                                        scalar2=None, op0=ALU.mult), r=[Wo, pv], w=[Wo])
                um = sb(e1, "um", [128, 128])
                DVE(lambda: nc.vector.tensor_scalar(out=um[:], in0=cst[:, 128:256], scalar1=-1.0, scalar2=None,
                                                    op0=ALU.is_gt), r=[cst], w=[um])
                um_bc = bass.AP(um.t[:].tensor, um.t[:].offset, [[128, 128], [0, 4], [1, 128]])
                NBUF = 2
                HQA = [sb(e1, "HQA", [128, 8, TT], BF16) for _ in range(NBUF)]
                HQS = [sb(e1, "HQS", [128, 8, TT], BF16) for _ in range(NBUF)]
                HKA = [sb(e1, "HKA", [128, 8, TT], BF16) for _ in range(NBUF)]
                HKH = [sb(e1, "HKH", [128, 8, TT], BF16) for _ in range(NBUF)]
                HGt = [sb(e1, "HGt", [128, 8, TT], BF16) for _ in range(NBUF)]
                HD = [sb(e1, "HD", [128, 8, 4]) for _ in range(NBUF)]
                HV = [sb(e1, "HV", [128, 4, D], BF16) for _ in range(NBUF)]
                yy = [sb(e1, "y", [128, 8, TT]) for _ in range(1)]
                OG = sb(e1, "OG", [128, 8, TT], BF16)
                S32 = [sb(e1, "S32", [128, 4, 128]) for _ in range(2)]
                Sb = [sb(e1, "Sb", [128, 4, 128], BF16) for _ in range(2)]
                khT = [sb(e1, "khT", [128, 4, 128], BF16) for _ in range(8)]
                ATb = [sb(e1, "ATb", [128, 4, 128], BF16) for _ in range(8)]
                osq = [sb(e1, "osq", [128, TT], BF16) for _ in range(2)]
                on = [sb(e1, "on", [128, TT]) for _ in range(2)]
                lnr = [sb(e1, "lnr", [128, TT]) for _ in range(2)]
                scr = sb(e1, "scr", [128, 16, TT], BF16)
                st = [sb(e1, "st", [128, TT]) for i in range(3)]
                tmp = sb(e1, "tmp", [128, TT])
                for g in range(2):
                    DVE(lambda: nc.vector.memset(S32[g][:], 0.0), w=[S32[g]])
                    DVE(lambda: nc.vector.memset(Sb[g][:], 0.0), w=[Sb[g]])

                def ld(it):
                    t0 = it * TT
                    i = it % NBUF
                    sch.dma("sp", HKH[i][:], hm(sc_hkh.t, t0), r=[sc_hkh], w=[HKH[i]])
                    sch.dma("sp", HKA[i][:], hm(sc_hka.t, t0), r=[sc_hka], w=[HKA[i]])
                    sch.dma("sp", HQA[i][:], hm(sc_hqa.t, t0), r=[sc_hqa], w=[HQA[i]])
                    sch.dma("sp", HQS[i][:], hm(sc_hqs.t, t0), r=[sc_hqs], w=[HQS[i]])
                    sch.dma("sp", HV[i][:], sc_hv[t0:t0 + TT, :].rearrange("(n p) d -> p n d", p=128), r=[sc_hv], w=[HV[i]])
                    sch.dma("sp", HD[i][:], sc_hd[:, :, it * 4:(it + 1) * 4], r=[sc_hd], w=[HD[i]])
                    sch.dma("sp", HGt[i][:], hm(sc_hg.t, t0), r=[sc_hg], w=[HGt[i]])

                ld(0)
                kk = 0
                for it in range(NT):
                    t0 = it * TT
                    if it + 1 < NT:
                        ld(it + 1)
                    i = it % NBUF
                    qa, qs, ka, kh, gt, hd, hv, y = HQA[i], HQS[i], HKA[i], HKH[i], HGt[i], HD[i], HV[i], yy[0]
                    sch.dma("sp", y[:], fm(src_x.t, t0), r=[src_x], w=[y])
                    for h in range(8):
                        bt = nextps(4)
                        for c in range(4):
                            cs = slice(c * 128, (c + 1) * 128)
                            PE(lambda: nc.tensor.matmul(bt[:, cs], lhsT=kh[:, h, cs], rhs=ident_b[:], start=True, stop=True),
                               r=[kh, ident_b], w=[bt], ms=(c == 3))
                        ACT(lambda: nc.scalar.copy(out=khT[h][:].rearrange("p c d -> p (c d)"), in_=bt[:]), r=[bt], w=[khT[h]])
                        ba = nextps(4)
                        for c in range(4):
                            cs = slice(c * 128, (c + 1) * 128)
                            PE(lambda: nc.tensor.matmul(ba[:, cs], lhsT=ka[:, h, cs], rhs=qa[:, h, cs], start=True, stop=True),
                               r=[ka, qa], w=[ba], ms=(c == 3))
                        DVE(lambda: nc.vector.tensor_tensor(out=ATb[h][:], in0=ba[:].rearrange("p (c t) -> p c t", c=4),
                                                            in1=um_bc, op=ALU.mult), r=[ba, um], w=[ATb[h]])
                    for c in range(4):
                        cs = slice(c * 128, (c + 1) * 128)
                        for g in range(2):
                            ob_ = banks[4 + (kk % 2)]
                            bd = banks[6]
                            j2 = kk % 2
                            kk += 1
                            for j in range(4):
                                h = 4 * g + j
                                js = slice(j * 128, (j + 1) * 128)
                                hs = slice(h * 128, (h + 1) * 128)
                                PE(lambda: nc.tensor.matmul(ob_[:, js], lhsT=Sb[g][:, j, :], rhs=qs[:, h, cs], start=True, stop=False),
                                   r=[Sb[g], qs], w=[ob_], ms=False)
                                PE(lambda: nc.tensor.matmul(ob_[:, js], lhsT=hv[:, c, hs], rhs=ATb[h][:, c, :], start=False, stop=True),
                                   r=[hv, ATb[h]], w=[ob_], ms=(j == 3))
                            for j in range(4):
                                h = 4 * g + j
                                js = slice(j * 128, (j + 1) * 128)
                                hs = slice(h * 128, (h + 1) * 128)
                                PE(lambda: nc.tensor.matmul(bd[:, js], lhsT=khT[h][:, c, :], rhs=hv[:, c, hs], start=True, stop=True),
                                   r=[khT[h], hv], w=[bd], ms=(j == 3))
                            dec = bass.AP(hd.t[:].tensor, hd.t[:, 4 * g, c:c + 1].offset, [[32, 128], [4, 4], [0, 128]])
                            DVE(lambda: nc.vector.tensor_tensor(out=S32[g][:], in0=S32[g][:], in1=dec, op=ALU.mult),
                                r=[S32[g], hd], w=[S32[g]])
                            DVE(lambda: nc.vector.tensor_tensor(out=S32[g][:], in0=S32[g][:],
                                                                in1=bd[:].rearrange("p (j e) -> p j e", j=4), op=ALU.add),
                                r=[S32[g], bd], w=[S32[g]])
                            ACT(lambda: nc.scalar.copy(out=Sb[g][:], in_=S32[g][:]), r=[S32[g]], w=[Sb[g]])
                            ACT(lambda: nc.scalar.activation(out=osq[j2][:], in_=ob_[:], func=AF.Square), r=[ob_], w=[osq[j2]])
                            bs = banks[7]
                            PE(lambda: nc.tensor.matmul(bs[:], lhsT=ones_b[:], rhs=osq[j2][:], start=True, stop=True),
                               r=[ones_b, osq[j2]], w=[bs])
                            ACT(lambda: nc.scalar.activation(out=lnr[j2][:], in_=bs[:], func=AF.Ln, scale=1.0 / 128.0, bias=RMS_EPS),
                                r=[bs], w=[lnr[j2]])
                            ACT(lambda: nc.scalar.activation(out=lnr[j2][:], in_=lnr[j2][:], func=AF.Exp, scale=-0.5),
                                r=[lnr[j2]], w=[lnr[j2]])
                            DVE(lambda: nc.vector.tensor_tensor(out=on[j2][:], in0=ob_[:], in1=lnr[j2][:], op=ALU.mult),
                                r=[ob_, lnr[j2]], w=[on[j2]])
                            POOL(lambda: nc.gpsimd.tensor_tensor(out=OG[:, 4 * g:4 * g + 4, cs],
                                                                 in0=on[j2][:].rearrange("p (j t) -> p j t", j=4),
                                                                 in1=gt[:, 4 * g:4 * g + 4, cs], op=ALU.mult),
                                 r=[on[j2], gt], w=[OG])
                    if dbg_og is not None:
                        sch.dma("sp", fm(dbg_og.t, t0), OG[:], r=[OG], w=[dbg_og])
                    for oc in range(8):
                        bk = nextps(4)
                        for kc in range(8):
                            PE(lambda: nc.tensor.matmul(bk[:], lhsT=Wo[:, kc, oc * 128:(oc + 1) * 128], rhs=OG[:, kc, :],
                                                        start=(kc == 0), stop=(kc == 7)), r=[Wo, OG], w=[bk], ms=(kc == 7))
                        DVE(lambda: nc.vector.scalar_tensor_tensor(out=y[:, oc, :], in0=y[:, oc, :], scalar=ALPHA, in1=bk[:],
                                                                   op0=ALU.mult, op1=ALU.add), r=[y, bk], w=[y])
                    ln_tile(y, scr, scr, st, tmp, gcol, bcol, nb=4)
                    sch.dma("sp", fm(dst.t, t0), y[:], r=[y], w=[dst])
                sch.barrier()

        out_t = T(out)
        if upto >= 3:
            stage_proj_ln(mla_w_o, sc_o, sc_xT, sc_x1, c_lng[0][0], c_lnb[0][0])
        if upto >= 4:
            stage_ffn(0, sc_x1, sc_xT, c_lng[0][1], c_lnb[0][1])
        if upto >= 5:
            stage_ple(0, sc_xT, sc_x1, False)
        if upto >= 6:
            stage_hgrn(sc_x1, sc_xT, c_lng[1][0], c_lnb[1][0])
        if upto >= 7:
            stage_ffn(1, sc_xT, sc_x1, c_lng[1][1], c_lnb[1][1])
        if upto >= 8:
            stage_ple(1, sc_x1, out_t, True)

        sch.barrier()
    return nc


def make_consts():
    c = np.zeros((128, 1408), np.float32)
    c[:, 0:128] = np.eye(128, dtype=np.float32)
    k = np.arange(128)[:, None]
    q = np.arange(128)[None, :]
    c[:, 128:256] = np.where(k <= q, 0.0, -30000.0)
    s = np.arange(64)[:, None]
    t = np.arange(64)[None, :]
    c[0:64, 256:768] = np.tile((s <= t).astype(np.float32), (1, 8))
    c[:, 768:1280] = (np.arange(512) % 128 != 0).astype(np.float32)[None, :]
    inv = (10000.0 ** (-np.arange(0, 64, 2, dtype=np.float32) / 64.0)).astype(np.float32)
    c[0:64, 1280] = np.concatenate([inv, inv])
    c[0:64, 1281] = np.concatenate([-inv, inv])
    return c


_W_NAMES = ["mla_w_dqkv", "mla_w_uq", "mla_w_ukv", "mla_w_o", "hgrn_w_in", "hgrn_w_o"]
_W_FULL = ["ffn_w_in", "ffn_w_down", "ple_w_proj", "ple_w_gate"]


def make_pvec(inputs):
    def col(v):
        return np.asarray(v, dtype=np.float32).reshape(-1, 128).T
    cols = [col(inputs["mla_q_norm"][0]), col(inputs["mla_kv_norm"][0])]
    for nm in ("ln_mix_g", "ln_ffn_g"):
        pass
    for i in range(2):
        cols += [col(inputs["ln_mix_g"][i]), col(inputs["ln_ffn_g"][i])]
    for i in range(2):
        cols += [col(inputs["ln_mix_b"][i]), col(inputs["ln_ffn_b"][i])]
    cols += [col(inputs["hgrn_out_norm"][0]), col(inputs["hgrn_lb_logits"][0]), col(inputs["hgrn_lb_logits"][1])]
    pv = np.concatenate(cols, axis=1)
    out = np.zeros((128, 96), np.float32)
    out[:, :pv.shape[1]] = pv
    return out


def make_in_maps(inputs, n_cores, S):
    consts = make_consts()
    shared = {"consts": consts, "pvec": make_pvec(inputs)}
    for k in _W_NAMES:
        shared[k] = np.ascontiguousarray(np.asarray(inputs[k], dtype=np.float32)[0])
    for k in _W_FULL:
        shared[k] = np.ascontiguousarray(np.asarray(inputs[k], dtype=np.float32))
    maps = []
    xs = np.asarray(inputs["x"])
    ps = np.asarray(inputs["p"])
    po = np.asarray(inputs["positions"])
    for b in range(n_cores):
        m = dict(shared)
        m["x"] = np.ascontiguousarray(xs[b, :S])
        m["p"] = np.ascontiguousarray(ps[:, b, :S])
        m["pos"] = np.ascontiguousarray(po[b:b + 1, :S]).astype(np.int32)
        maps.append(m)
    return maps


def kernel(**inputs):
    n = 8
    S = 8192
    nc = build(S)
    maps = make_in_maps(inputs, n, S)
    res = run_bass_kernel_spmd(nc, maps, core_ids=list(range(n)))
    return np.stack([np.asarray(r["out"]) for r in res.results], axis=0).astype(np.float32)
```

```python
import math
from contextlib import ExitStack

import numpy as np
import concourse.bass as bass
import concourse.mybir as mybir
from concourse.bass_utils import run_bass_kernel_spmd

F32 = mybir.dt.float32
BF16 = mybir.dt.bfloat16
I32 = mybir.dt.int32
AF = mybir.ActivationFunctionType
ALU = mybir.AluOpType

D = 1024
DFF = 2816
ALPHA = 4.0 ** 0.25
LN_EPS = 1e-5
RMS_EPS = 1e-6
TT = 512


class Buf:
    __slots__ = ("w", "r")

    def __init__(self):
        self.w = None
        self.r = {}


class T:
    def __init__(self, t, excl=False):
        self.t = t
        self.b = Buf()
        self.excl = excl

    def __getitem__(self, k):
        return self.t[k]


class Sched:
    def __init__(self, nc, es):
        self.nc = nc
        self.eng = {"pe": nc.tensor, "act": nc.scalar, "dve": nc.vector, "pool": nc.gpsimd, "sp": nc.sync}
        self.sems = {}
        self.cnt = {}
        self.seen = {k: {} for k in self.eng}
        for k in self.eng:
            self.sems[k] = es.enter_context(nc.semaphore("s_" + k))
            self.cnt[k] = 0
        self.ring = {}
        for q, n in (("sp", 8), ("pool", 8), ("act", 4)):
            lst = []
            for i in range(n):
                key = (q, i)
                self.sems[key] = es.enter_context(nc.semaphore("d_%s%d" % (q, i)))
                self.cnt[key] = 0
                lst.append(key)
            self.ring[q] = [lst, 0]
        self.nops = 0

    def _need(self, e, reads, writes):
        need = {}

        def add(ev, raw):
            if ev is None:
                return
            key, val = ev
            if key == e and e == "pe":
                return
            if need.get(key, 0) < val:
                need[key] = val

        for b in reads:
            add(b.b.w, True)
            if b.excl:
                for k, v in b.b.r.items():
                    if k != e:
                        add((k, v), False)
        for b in writes:
            add(b.b.w, False)
            for k, v in b.b.r.items():
                add((k, v), False)
        seen = self.seen[e]
        for key, val in need.items():
            assert val <= self.cnt[key], ("wait on a milestone not yet emitted", e, key, val, self.cnt[key])
            if seen.get(key, 0) < val:
                self.eng[e].wait_ge(self.sems[key], val)
                seen[key] = val

    def _record(self, ev, reads, writes):
        key, val = ev
        for b in reads:
            b.b.r[key] = val
        for b in writes:
            b.b.w = ev
            b.b.r = {}

    def op(self, e, fn, r=(), w=(), ms=True):
        self._need(e, r, w)
        inst = fn()
        val = self.cnt[e] + 1
        if ms:
            inst.then_inc(self.sems[e], 1)
            self.cnt[e] = val
        self._record((e, val), r, w)
        self.nops += 1
        return inst

    def dma(self, q, out, in_, r=(), w=()):
        self._need(q, r, w)
        lst, i = self.ring[q]
        key = lst[i % len(lst)]
        self.ring[q][1] = i + 1
        prior = self.cnt[key]
        if prior > 0 and self.seen[q].get(key, 0) < prior:
            self.eng[q].wait_ge(self.sems[key], prior)
            self.seen[q][key] = prior
        inst = self.eng[q].dma_start(out=out, in_=in_)
        inst.then_inc(self.sems[key], 16)
        self.cnt[key] = prior + 16
        self._record((key, prior + 16), r, w)
        self.nops += 1
        return inst

    def barrier(self):
        for e in self.eng:
            for key, val in self.cnt.items():
                if key == e or val == 0:
                    continue
                if self.seen[e].get(key, 0) < val:
                    self.eng[e].wait_ge(self.sems[key], val)
                    self.seen[e][key] = val


def build(S=8192, upto=99, debug=False):
    nc = bass.Bass("TRN2", target_bir_lowering=False)
    NT = S // TT
    NB = S // 128
    dbg_kind = "ExternalOutput" if debug else "Internal"

    def dram(name, shape, dt, kind=None):
        return nc.dram_tensor(name, list(shape), dt, kind=kind or dbg_kind).ap()

    def din(name, shape, dt=F32):
        return dram(name, shape, dt, "ExternalInput")

    x = din("x", [S, D])
    p_in = din("p", [2, S, 256])
    pos = din("pos", [1, S], I32)
    consts = din("consts", [128, 1408])
    pvec = din("pvec", [128, 96])
    w_dqkv = din("mla_w_dqkv", [D, 576])
    w_uq = din("mla_w_uq", [256, 1536])
    w_ukv = din("mla_w_ukv", [256, 2048])
    mla_w_o = din("mla_w_o", [D, D])
    hg_w_in = din("hgrn_w_in", [D, 4096])
    hg_w_o = din("hgrn_w_o", [D, D])
    ffn_w_in = din("ffn_w_in", [2, D, 2 * DFF])
    ffn_w_dn = din("ffn_w_down", [2, DFF, D])
    ple_w_proj = din("ple_w_proj", [2, 256, D])
    ple_w_gate = din("ple_w_gate", [2, D, D])
    out = dram("out", [S, D], F32, "ExternalOutput")

    sc_cc = T(dram("sc_cc", [64, S], F32))
    sc_ss = T(dram("sc_ss", [64, S], F32))
    sc_xT = T(dram("sc_xT", [D, S], F32))
    sc_x1 = T(dram("sc_x1", [D, S], F32))
    sc_qn = T(dram("sc_qn", [8, 128, S], BF16))
    sc_qr = T(dram("sc_qr", [8, 64, S], BF16))
    sc_kn = T(dram("sc_kn", [8, 128, S], BF16))
    sc_kr = T(dram("sc_kr", [64, S], BF16))
    sc_v = T(dram("sc_v", [S, D], BF16))
    sc_o = T(dram("sc_o", [D, S], BF16))
    sc_hqa = T(dram("sc_hqa", [8, 128, S], BF16))
    sc_hqs = T(dram("sc_hqs", [8, 128, S], BF16))
    sc_hka = T(dram("sc_hka", [8, 128, S], BF16))
    sc_hkh = T(dram("sc_hkh", [8, 128, S], BF16))
    sc_hg = T(dram("sc_hg", [8, 128, S], BF16))
    sc_hv = T(dram("sc_hv", [S, D], BF16))
    sc_hd = T(dram("sc_hd", [128, 8, S // 128], F32))
    dbg_og = T(dram("dbg_og", [D, S], BF16)) if debug else None

    es = ExitStack()
    with es:
        sch = Sched(nc, es)
        PE = lambda fn, r=(), w=(), ms=True: sch.op("pe", fn, r, w, ms)
        ACT = lambda fn, r=(), w=(): sch.op("act", fn, r, w)
        DVE = lambda fn, r=(), w=(): sch.op("dve", fn, r, w)
        POOL = lambda fn, r=(), w=(): sch.op("pool", fn, r, w)

        uniq = [0]

        def sb(es_, name, shape, dt=F32):
            uniq[0] += 1
            return T(es_.enter_context(nc.sbuf_tensor("%s_%d" % (name, uniq[0]), list(shape), dt)))

        banks = [T(es.enter_context(nc.psum_tensor("ps%d" % i, [128, 512], F32)), excl=True) for i in range(8)]
        bank_i = [0]

        def nextps(n=8):
            b = banks[bank_i[0] % n]
            bank_i[0] += 1
            return b

        cst = sb(es, "cst", [128, 1408])
        sch.dma("sp", cst[:], consts[:, :], w=[cst])
        ident_f = cst[:, 0:128]
        ident_b = sb(es, "ident_b", [128, 128], BF16)
        maskb = sb(es, "maskb", [128, 128], BF16)
        amask = sb(es, "amask", [64, 512], BF16)
        ones_b = sb(es, "ones_b", [128, 128], BF16)
        DVE(lambda: nc.vector.tensor_copy(out=ident_b[:], in_=cst[:, 0:128]), r=[cst], w=[ident_b])
        DVE(lambda: nc.vector.tensor_copy(out=maskb[:], in_=cst[:, 128:256]), r=[cst], w=[maskb])
        DVE(lambda: nc.vector.tensor_copy(out=amask[:], in_=cst[0:64, 256:768]), r=[cst], w=[amask])
        DVE(lambda: nc.vector.memset(ones_b[:], 1.0), w=[ones_b])
        rmask = cst[:, 768:1280]
        pv = sb(es, "pv", [128, 96])
        sch.dma("sp", pv[:], pvec[:, :], w=[pv])
        c_qn, c_kvn = 0, 2
        c_lng = [[4, 12], [20, 28]]
        c_lnb = [[36, 44], [52, 60]]
        c_on, c_l0, c_l1 = 68, 76, 84
        lbv = sb(es, "lbv", [128, 24])
        DVE(lambda: nc.vector.tensor_tensor(out=lbv[:, 16:24], in0=pv[:, c_l1:c_l1 + 8], in1=pv[:, c_l0:c_l0 + 8],
                                            op=ALU.subtract), r=[pv], w=[lbv])
        ACT(lambda: nc.scalar.activation(out=lbv[:, 0:8], in_=lbv[:, 16:24], func=AF.Sigmoid), r=[lbv], w=[lbv])
        DVE(lambda: nc.vector.tensor_scalar(out=lbv[:, 8:16], in0=lbv[:, 0:8], scalar1=-1.0, scalar2=1.0,
                                            op0=ALU.mult, op1=ALU.add), r=[lbv], w=[lbv])

        TWO_PI = 2.0 * math.pi
        C1 = 6.28125
        C2 = TWO_PI - C1
        with ExitStack() as e1:
            RC = min(S, 2048)
            pi_ = sb(e1, "r_pi", [64, RC], I32)
            pf = sb(e1, "r_pf", [64, RC])
            for tab in range(2):
                kf = sb(e1, "r_kf%d" % tab, [64, RC])
                ki = sb(e1, "r_ki%d" % tab, [64, RC], I32)
                ang = sb(e1, "r_ang%d" % tab, [64, RC])
                tm = sb(e1, "r_tm%d" % tab, [64, RC])
                res = sb(e1, "r_res%d" % tab, [64, RC])
                for c0 in range(0, S, RC):
                    if tab == 0:
                        pass
                    src = bass.AP(pos.tensor, c0, [[0, 64], [1, RC]])
                    sch.dma("sp", pi_[:], src, w=[pi_])
                    DVE(lambda: nc.vector.tensor_copy(out=pf[:], in_=pi_[:]), r=[pi_], w=[pf])
                    fcol = cst[0:64, 1280 + tab:1281 + tab]
                    DVE(lambda: nc.vector.tensor_scalar(out=ang[:], in0=pf[:], scalar1=fcol, scalar2=None,
                                                        op0=ALU.mult), r=[pf, cst], w=[ang])
                    DVE(lambda: nc.vector.tensor_scalar(out=kf[:], in0=ang[:], scalar1=1.0 / TWO_PI, scalar2=None,
                                                        op0=ALU.mult), r=[ang], w=[kf])
                    DVE(lambda: nc.vector.tensor_copy(out=ki[:], in_=kf[:]), r=[kf], w=[ki])
                    DVE(lambda: nc.vector.tensor_copy(out=kf[:], in_=ki[:]), r=[ki], w=[kf])
                    DVE(lambda: nc.vector.scalar_tensor_tensor(out=tm[:], in0=kf[:], scalar=-C1, in1=ang[:],
                                                               op0=ALU.mult, op1=ALU.add), r=[kf, ang], w=[tm])
                    DVE(lambda: nc.vector.scalar_tensor_tensor(out=ang[:], in0=kf[:], scalar=-C2, in1=tm[:],
                                                               op0=ALU.mult, op1=ALU.add), r=[kf, tm], w=[ang])
                    if tab == 0:
                        DVE(lambda: nc.vector.tensor_scalar(out=ang[:], in0=ang[:], scalar1=math.pi / 2, scalar2=None,
                                                            op0=ALU.add), r=[ang], w=[ang])
                    for _ in range(2):
                        DVE(lambda: nc.vector.tensor_scalar(out=tm[:], in0=ang[:], scalar1=math.pi, scalar2=-TWO_PI,
                                                            op0=ALU.is_gt, op1=ALU.mult), r=[ang], w=[tm])
                        DVE(lambda: nc.vector.tensor_tensor(out=ang[:], in0=ang[:], in1=tm[:], op=ALU.add),
                            r=[ang, tm], w=[ang])
                        DVE(lambda: nc.vector.tensor_scalar(out=tm[:], in0=ang[:], scalar1=-math.pi, scalar2=TWO_PI,
                                                            op0=ALU.is_lt, op1=ALU.mult), r=[ang], w=[tm])
                        DVE(lambda: nc.vector.tensor_tensor(out=ang[:], in0=ang[:], in1=tm[:], op=ALU.add),
                            r=[ang, tm], w=[ang])
                    DVE(lambda: nc.vector.tensor_scalar(out=ang[:], in0=ang[:], scalar1=3.14159, scalar2=-3.14159,
                                                        op0=ALU.min, op1=ALU.max), r=[ang], w=[ang])
                    ACT(lambda: nc.scalar.activation(out=res[:], in_=ang[:], func=AF.Sin), r=[ang], w=[res])
                    dst = sc_cc if tab == 0 else sc_ss
                    sch.dma("sp", dst[:, c0:c0 + RC], res[:], r=[res], w=[dst])
            sch.barrier()

        def load_w(q, dst_tile, dst_ap, src_ap):
            sch.dma(q, dst_ap, src_ap, w=[dst_tile])

        if upto >= 1:
            with ExitStack() as e1:
                Wd = sb(e1, "Wd", [128, 8, 640], BF16)
                Wuq = sb(e1, "Wuq", [128, 2, 8, 256], BF16)
                Wkv = sb(e1, "Wkv", [128, 2, 2, 8, 128], BF16)
                wdv = w_dqkv.rearrange("(c p) n -> p c n", p=128)
                load_w("pool", Wd, Wd[:, :, 0:576], wdv)
                load_w("pool", Wd, Wd[:, :, 576:608], wdv[:, :, 544:576])
                load_w("pool", Wd, Wd[:, :, 608:640], wdv[:, :, 512:544])
                for kc in range(2):
                    wv = w_uq[kc * 128:(kc + 1) * 128, :].rearrange("p (h d) -> p h d", h=8)
                    load_w("pool", Wuq, Wuq[:, kc, :, 0:192], wv)
                    load_w("pool", Wuq, Wuq[:, kc, :, 192:224], wv[:, :, 160:192])
                    load_w("pool", Wuq, Wuq[:, kc, :, 224:256], wv[:, :, 128:160])
                    wk = w_ukv[kc * 128:(kc + 1) * 128, :].rearrange("p (h kv d) -> p kv h d", h=8, kv=2)
                    for kv in range(2):
                        load_w("pool", Wkv, Wkv[:, kc, kv, :, :], wk[:, kv, :, :])
                xtok = [sb(e1, "xtok%d" % i, [128, 4, D]) for i in range(2)]
                xT32 = [sb(e1, "xT32_%d" % i, [128, 8, TT]) for i in range(2)]
                xTb = sb(e1, "xTb", [128, 8, TT], BF16)
                sq = sb(e1, "sq", [128, 4, TT], BF16)
                lnt = sb(e1, "lnt", [128, 2, TT])
                rstd = sb(e1, "rstd", [128, 2, TT])
                cl = sb(e1, "cl", [128, 4, TT], BF16)
                cct = [sb(e1, "cct%d" % i, [64, TT]) for i in range(2)]
                sst = [sb(e1, "sst%d" % i, [64, TT]) for i in range(2)]
                t1 = sb(e1, "t1", [64, TT])
                t2 = sb(e1, "t2", [64, TT])
                krb = sb(e1, "krb", [64, TT], BF16)
                Qn = sb(e1, "Qn", [128, 8, TT], BF16)
                Qr = sb(e1, "Qr", [64, 8, TT], BF16)
                Kn = sb(e1, "Kn", [128, 8, TT], BF16)
                Vt = sb(e1, "Vt", [128, 4, D], BF16)

                def load_x(it):
                    t0 = it * TT
                    sch.dma("sp", xtok[it % 2][:], x[t0:t0 + TT, :].rearrange("(n p) d -> p n d", p=128),
                            w=[xtok[it % 2]])
                    sch.dma("sp", cct[it % 2][:], sc_cc[:, t0:t0 + TT], r=[sc_cc], w=[cct[it % 2]])
                    sch.dma("sp", sst[it % 2][:], sc_ss[:, t0:t0 + TT], r=[sc_ss], w=[sst[it % 2]])

                import os as _os
                _kstop = int(_os.environ.get("KSTOP", "99"))
                load_x(0)
                for it in range(NT if _kstop > 0 else 0):
                    t0 = it * TT
                    if it + 1 < NT:
                        load_x(it + 1)
                    xt = xtok[it % 2]
                    x32 = xT32[it % 2]
                    CCt = cct[it % 2]
                    SSt = sst[it % 2]
                    for c in range(8):
                        bk = nextps()
                        for n in range(4):
                            PE(lambda: nc.tensor.transpose(out=bk[:, n * 128:(n + 1) * 128],
                                                           in_=xt[:, n, c * 128:(c + 1) * 128], identity=ident_f),
                               r=[xt, cst], w=[bk])
                        ACT(lambda: nc.scalar.copy(out=x32[:, c, :], in_=bk[:]), r=[bk], w=[x32])
                        DVE(lambda: nc.vector.tensor_copy(out=xTb[:, c, :], in_=x32[:, c, :]), r=[x32], w=[xTb])
                    sch.dma("sp", sc_xT[:, t0:t0 + TT].rearrange("(c p) t -> p c t", p=128), x32[:], r=[x32], w=[sc_xT])
                    if _kstop <= 1:
                        continue
                    dps = []
                    for oc in range(4):
                        bk = nextps()
                        for kc in range(8):
                            PE(lambda: nc.tensor.matmul(bk[:], lhsT=Wd[:, kc, oc * 128:(oc + 1) * 128], rhs=xTb[:, kc, :],
                                                        start=(kc == 0), stop=(kc == 7)), r=[Wd, xTb], w=[bk])
                        ACT(lambda: nc.scalar.activation(out=sq[:, oc, :], in_=bk[:], func=AF.Square), r=[bk], w=[sq])
                        dps.append(bk)
                    for g in range(2):
                        bs = nextps()
                        for j in range(2):
                            PE(lambda: nc.tensor.matmul(bs[:], lhsT=ones_b[:], rhs=sq[:, 2 * g + j, :],
                                                        start=(j == 0), stop=(j == 1)), r=[ones_b, sq], w=[bs])
                        ACT(lambda: nc.scalar.activation(out=lnt[:, g, :], in_=bs[:], func=AF.Ln, scale=1.0 / 256.0,
                                                         bias=RMS_EPS), r=[bs], w=[lnt])
                        ACT(lambda: nc.scalar.activation(out=rstd[:, g, :], in_=lnt[:, g, :], func=AF.Exp, scale=-0.5),
                            r=[lnt], w=[rstd])
                        for j in range(2):
                            col = (c_qn if g == 0 else c_kvn) + j
                            DVE(lambda: nc.vector.scalar_tensor_tensor(out=cl[:, 2 * g + j, :], in0=dps[2 * g + j][:],
                                                                       scalar=pv[:, col:col + 1], in1=rstd[:, g, :],
                                                                       op0=ALU.mult, op1=ALU.mult),
                                r=[dps[2 * g + j], pv, rstd], w=[cl])

                    def rope(b1, b2, dst_ap, dst_t):
                        DVE(lambda: nc.vector.tensor_tensor(out=t1[:], in0=b1[0:64, :], in1=CCt[:], op=ALU.mult),
                            r=[b1, CCt], w=[t1])
                        DVE(lambda: nc.vector.tensor_tensor(out=t2[:], in0=b2[0:64, :], in1=SSt[:], op=ALU.mult),
                            r=[b2, SSt], w=[t2])
                        DVE(lambda: nc.vector.tensor_tensor(out=dst_ap, in0=t1[:], in1=t2[:], op=ALU.add),
                            r=[t1, t2], w=[dst_t])

                    if _kstop <= 2:
                        continue
                    b1 = nextps()
                    b2 = nextps()
                    for bk, c0 in ((b1, 512), (b2, 576)):
                        for kc in range(8):
                            PE(lambda: nc.tensor.matmul(bk[0:64, :], lhsT=Wd[:, kc, c0:c0 + 64], rhs=xTb[:, kc, :],
                                                        start=(kc == 0), stop=(kc == 7)), r=[Wd, xTb], w=[bk])
                    rope(b1, b2, krb[:], krb)
                    sch.dma("sp", sc_kr[:, t0:t0 + TT], krb[:], r=[krb], w=[sc_kr])
                    if _kstop <= 3:
                        continue
                    for h in range(8):
                        bn = nextps()
                        b1 = nextps()
                        b2 = nextps()
                        for kc in range(2):
                            PE(lambda: nc.tensor.matmul(bn[:], lhsT=Wuq[:, kc, h, 0:128], rhs=cl[:, kc, :],
                                                        start=(kc == 0), stop=(kc == 1)), r=[Wuq, cl], w=[bn])
                        for bk, c0 in ((b1, 128), (b2, 192)):
                            for kc in range(2):
                                PE(lambda: nc.tensor.matmul(bk[0:64, :], lhsT=Wuq[:, kc, h, c0:c0 + 64], rhs=cl[:, kc, :],
                                                            start=(kc == 0), stop=(kc == 1)), r=[Wuq, cl], w=[bk])
                        ACT(lambda: nc.scalar.copy(out=Qn[:, h, :], in_=bn[:]), r=[bn], w=[Qn])
                        rope(b1, b2, Qr[:, h, :], Qr)
                    sch.dma("sp", sc_qn[:, :, t0:t0 + TT].rearrange("h p t -> p h t"), Qn[:], r=[Qn], w=[sc_qn])
                    sch.dma("sp", sc_qr[:, :, t0:t0 + TT].rearrange("h p t -> p h t"), Qr[:], r=[Qr], w=[sc_qr])
                    if _kstop <= 4:
                        continue
                    for h in range(8):
                        bk = nextps()
                        for kc in range(2):
                            PE(lambda: nc.tensor.matmul(bk[:], lhsT=Wkv[:, kc, 0, h, :], rhs=cl[:, 2 + kc, :],
                                                        start=(kc == 0), stop=(kc == 1)), r=[Wkv, cl], w=[bk])
                        ACT(lambda: nc.scalar.copy(out=Kn[:, h, :], in_=bk[:]), r=[bk], w=[Kn])
                    sch.dma("sp", sc_kn[:, :, t0:t0 + TT].rearrange("h p t -> p h t"), Kn[:], r=[Kn], w=[sc_kn])
                    for n in range(4):
                        for hf in range(2):
                            bk = nextps()
                            for kc in range(2):
                                PE(lambda: nc.tensor.matmul(bk[:], lhsT=cl[:, 2 + kc, n * 128:(n + 1) * 128],
                                                            rhs=Wkv[:, kc, 1, 4 * hf:4 * hf + 4, :],
                                                            start=(kc == 0), stop=(kc == 1)), r=[Wkv, cl], w=[bk])
                            DVE(lambda: nc.vector.tensor_copy(out=Vt[:, n, hf * 512:(hf + 1) * 512], in_=bk[:]),
                                r=[bk], w=[Vt])
                    sch.dma("sp", sc_v[t0:t0 + TT, :].rearrange("(n p) d -> p n d", p=128), Vt[:], r=[Vt], w=[sc_v])
                sch.barrier()

        if upto >= 2:
            SCALE = 192.0 ** -0.5
            import os as _os2
            ATT_MODE = int(_os2.environ.get("ATT_MODE", "0"))
            with ExitStack() as e1:
                Kr = sb(e1, "Kr", [128, S], BF16)
                Knh = [sb(e1, "Knh%d" % i, [128, S], BF16) for i in range(2)]
                Vh = [sb(e1, "Vh%d" % i, [128, NB, 128], BF16) for i in range(2)]
                Qnb = [sb(e1, "Qnb%d" % i, [128, TT], BF16) for i in range(2)]
                Qrb = [sb(e1, "Qrb%d" % i, [128, TT], BF16) for i in range(2)]
                NPT = 4
                pt = [sb(e1, "pt%d" % i, [128, TT], BF16) for i in range(NPT)]
                pacc = [[sb(e1, "pacc", [128, TT]) for _ in range(2)] for _ in range(2)]
                dsum = [sb(e1, "dsum", [128, TT]) for _ in range(2)]
                ones_f = sb(e1, "ones_f", [128, 128])
                DVE(lambda: nc.vector.memset(ones_f[:], 1.0), w=[ones_f])
                rcp = [sb(e1, "rcp%d" % i, [128, TT]) for i in range(2)]
                ob = [sb(e1, "ob%d" % i, [128, TT], BF16) for i in range(2)]
                NSB = 4
                LOOK = 2
                sbank = banks[0:NSB]
                accs = [(banks[4], banks[5]), (banks[6], banks[7])]
                POOL(lambda: nc.gpsimd.memset(Kr[64:128, :], 0.0), w=[Kr])
                for i_ in range(2):
                    POOL(lambda: nc.gpsimd.memset(Qrb[i_][64:128, :], 0.0), w=[Qrb[i_]])
                sch.dma("sp", Kr[0:64, :], sc_kr[:, :], r=[sc_kr], w=[Kr])

                def load_head(h):
                    sch.dma("sp", Knh[h % 2][:], sc_kn[h, :, :], r=[sc_kn], w=[Knh[h % 2]])
                    sch.dma("sp", Vh[h % 2][:], sc_v[:, h * 128:(h + 1) * 128].rearrange("(n p) d -> p n d", p=128),
                            r=[sc_v], w=[Vh[h % 2]])

                def load_q(h, qb, i):
                    sch.dma("sp", Qnb[i % 2][:], sc_qn[h, :, qb * TT:(qb + 1) * TT], r=[sc_qn], w=[Qnb[i % 2]])
                    sch.dma("sp", Qrb[i % 2][0:64, :], sc_qr[h, :, qb * TT:(qb + 1) * TT], r=[sc_qr], w=[Qrb[i % 2]])

                load_head(0)
                load_q(0, 0, 0)
                blk = 0
                ti = 0
                for h in range(8):
                    if h + 1 < 8:
                        load_head(h + 1)
                    K_ = Knh[h % 2]
                    V_ = Vh[h % 2]
                    for qb in range(NT):
                        nh, nq = (h, qb + 1) if qb + 1 < NT else (h + 1, 0)
                        if nh < 8:
                            load_q(nh, nq, blk + 1)
                        Qn_ = Qnb[blk % 2]
                        Qr_ = Qrb[blk % 2]
                        acc_o, acc_d = accs[blk % 2]
                        nk = 4 * qb + 4
                        pa = pacc[blk % 2]
                        DVE(lambda: nc.vector.memset(pa[0][:], 0.0), w=[pa[0]])
                        POOL(lambda: nc.gpsimd.memset(pa[1][:], 0.0), w=[pa[1]])

                        def qk(kb, sp_):
                            j = kb - 4 * qb
                            c0 = max(j, 0) * 128
                            ks = slice(kb * 128, (kb + 1) * 128)
                            PE(lambda: nc.tensor.matmul(sp_[:, c0:TT], lhsT=K_[:, ks], rhs=Qn_[:, c0:TT],
                                                        start=True, stop=False), r=[K_, Qn_], w=[sp_], ms=False)
                            PE(lambda: nc.tensor.matmul(sp_[:, c0:TT], lhsT=Kr[:, ks], rhs=Qr_[:, c0:TT],
                                                        start=False, stop=(j < 0)), r=[Kr, Qr_], w=[sp_], ms=(j < 0))
                            if j >= 0:
                                PE(lambda: nc.tensor.matmul(sp_[:, c0:c0 + 128], lhsT=ident_b[:], rhs=maskb[:],
                                                            start=False, stop=True), r=[ident_b, maskb], w=[sp_])
                            return c0

                        c0s = {}
                        for k0 in range(min(LOOK, nk)):
                            c0s[k0] = qk(k0, sbank[(ti + k0) % NSB])
                        for kb in range(nk):
                            sp_ = sbank[(ti + kb) % NSB]
                            p_ = pt[(ti + kb) % NPT]
                            c0 = c0s[kb]
                            ACT(lambda: nc.scalar.activation(out=p_[:, c0:TT], in_=sp_[:, c0:TT], func=AF.Exp,
                                                             scale=SCALE), r=[sp_], w=[p_])
                            if kb + LOOK < nk:
                                c0s[kb + LOOK] = qk(kb + LOOK, sbank[(ti + kb + LOOK) % NSB])
                            PE(lambda: nc.tensor.matmul(acc_o[:, c0:TT], lhsT=V_[:, kb, :], rhs=p_[:, c0:TT],
                                                        start=(kb == 0), stop=(kb == nk - 1)), r=[V_, p_], w=[acc_o])
                            if ATT_MODE == 0:
                                PE(lambda: nc.tensor.matmul(acc_d[:, c0:TT], lhsT=ones_b[:], rhs=p_[:, c0:TT],
                                                            start=(kb == 0), stop=(kb == nk - 1)), r=[ones_b, p_], w=[acc_d])
                            elif kb % 2 == 0 or ATT_MODE == 2:
                                DVE(lambda: nc.vector.tensor_tensor(out=pa[0][:, c0:TT], in0=pa[0][:, c0:TT], in1=p_[:, c0:TT],
                                                                    op=ALU.add), r=[pa[0], p_], w=[pa[0]])
                            else:
                                POOL(lambda: nc.gpsimd.tensor_tensor(out=pa[1][:, c0:TT], in0=pa[1][:, c0:TT], in1=p_[:, c0:TT],
                                                                     op=ALU.add), r=[pa[1], p_], w=[pa[1]])
                        ti += nk
                        rc = rcp[blk % 2]
                        o_ = ob[blk % 2]
                        ds_ = dsum[blk % 2]
                        if ATT_MODE != 0:
                            DVE(lambda: nc.vector.tensor_tensor(out=ds_[:], in0=pa[0][:], in1=pa[1][:], op=ALU.add),
                                r=[pa[0], pa[1]], w=[ds_])
                            PE(lambda: nc.tensor.matmul(acc_d[:], lhsT=ones_f[:], rhs=ds_[:], start=True, stop=True),
                               r=[ones_f, ds_], w=[acc_d])
                        DVE(lambda: nc.vector.reciprocal(out=rc[:], in_=acc_d[:]), r=[acc_d], w=[rc])
                        DVE(lambda: nc.vector.tensor_tensor(out=o_[:], in0=acc_o[:], in1=rc[:], op=ALU.mult),
                            r=[acc_o, rc], w=[o_])
                        sch.dma("sp", sc_o[h * 128:(h + 1) * 128, qb * TT:(qb + 1) * TT], o_[:], r=[o_], w=[sc_o])
                        blk += 1
                sch.barrier()


        def ln_tile(y, scr, scr_t, st, tmp, gcol, bcol, nb=8):
            mean, msq, lnv = st
            for c in range(8):
                DVE(lambda: nc.vector.tensor_copy(out=scr[:, c, :], in_=y[:, c, :]), r=[y], w=[scr_t])
                ACT(lambda: nc.scalar.activation(out=scr[:, 8 + c, :], in_=y[:, c, :], func=AF.Square), r=[y], w=[scr_t])
            s1 = nextps(nb)
            s2 = nextps(nb)
            for c in range(8):
                PE(lambda: nc.tensor.matmul(s1[:], lhsT=ones_b[:], rhs=scr[:, c, :], start=(c == 0), stop=(c == 7)),
                   r=[ones_b, scr_t], w=[s1], ms=(c == 7))
            for c in range(8):
                PE(lambda: nc.tensor.matmul(s2[:], lhsT=ones_b[:], rhs=scr[:, 8 + c, :], start=(c == 0), stop=(c == 7)),
                   r=[ones_b, scr_t], w=[s2], ms=(c == 7))
            ACT(lambda: nc.scalar.activation(out=mean[:], in_=s1[:], func=AF.Copy, scale=1.0 / D), r=[s1], w=[mean])
            DVE(lambda: nc.vector.tensor_tensor(out=msq[:], in0=mean[:], in1=mean[:], op=ALU.mult), r=[mean], w=[msq])
            DVE(lambda: nc.vector.scalar_tensor_tensor(out=msq[:], in0=s2[:], scalar=1.0 / D, in1=msq[:],
                                                       op0=ALU.mult, op1=ALU.subtract), r=[s2, msq], w=[msq])
            ACT(lambda: nc.scalar.activation(out=lnv[:], in_=msq[:], func=AF.Ln, bias=LN_EPS), r=[msq], w=[lnv])
            ACT(lambda: nc.scalar.activation(out=lnv[:], in_=lnv[:], func=AF.Exp, scale=-0.5), r=[lnv], w=[lnv])
            for c in range(8):
                DVE(lambda: nc.vector.tensor_tensor(out=tmp[:], in0=y[:, c, :], in1=mean[:], op=ALU.subtract),
                    r=[y, mean], w=[tmp])
                DVE(lambda: nc.vector.tensor_tensor(out=tmp[:], in0=tmp[:], in1=lnv[:], op=ALU.mult),
                    r=[tmp, lnv], w=[tmp])
                ACT(lambda: nc.scalar.activation(out=y[:, c, :], in_=tmp[:], func=AF.Identity,
                                                 scale=pv[:, gcol + c:gcol + c + 1], bias=pv[:, bcol + c:bcol + c + 1]),
                    r=[tmp, pv], w=[y])

        def fm(ap2d, t0):
            return ap2d[:, t0:t0 + TT].rearrange("(c p) t -> p c t", p=128)

        def hm(ap3d, t0):
            return ap3d[:, :, t0:t0 + TT].rearrange("h p t -> p h t")

        def wload(dst, src2d, nkc):
            v = src2d.rearrange("(c p) n -> p c n", p=128)
            for kc in range(nkc):
                sch.dma("pool", dst[:, kc, :], v[:, kc, :], w=[dst])

        def stage_proj_ln(w_o_ap, src_o, src_x, dst, gcol, bcol):
            with ExitStack() as e1:
                Wo = sb(e1, "Wo", [128, 8, D], BF16)
                wload(Wo, w_o_ap, 8)
                oT = [sb(e1, "oT%d" % i, [128, 8, TT], BF16) for i in range(2)]
                yy = [sb(e1, "yy%d" % i, [128, 8, TT]) for i in range(2)]
                scr = sb(e1, "scr", [128, 16, TT], BF16)
                st = [sb(e1, "st%d" % i, [128, TT]) for i in range(3)]
                tmp = sb(e1, "tmp", [128, TT])

                def ld(it):
                    sch.dma("sp", oT[it % 2][:], fm(src_o.t, it * TT), r=[src_o], w=[oT[it % 2]])
                    sch.dma("sp", yy[it % 2][:], fm(src_x.t, it * TT), r=[src_x], w=[yy[it % 2]])

                ld(0)
                for it in range(NT):
                    if it + 1 < NT:
                        ld(it + 1)
                    o_ = oT[it % 2]
                    y = yy[it % 2]
                    for oc in range(8):
                        bk = nextps()
                        for kc in range(8):
                            PE(lambda: nc.tensor.matmul(bk[:], lhsT=Wo[:, kc, oc * 128:(oc + 1) * 128], rhs=o_[:, kc, :],
                                                        start=(kc == 0), stop=(kc == 7)), r=[Wo, o_], w=[bk], ms=(kc == 7))
                        DVE(lambda: nc.vector.scalar_tensor_tensor(out=y[:, oc, :], in0=y[:, oc, :], scalar=ALPHA, in1=bk[:],
                                                                   op0=ALU.mult, op1=ALU.add), r=[y, bk], w=[y])
                    ln_tile(y, scr, scr, st, tmp, gcol, bcol)
                    sch.dma("sp", fm(dst.t, it * TT), y[:], r=[y], w=[dst])
                sch.barrier()

        def stage_ffn(li, src, dst, gcol, bcol):
            with ExitStack() as e1:
                Win = sb(e1, "Win", [128, 8, 2 * DFF], BF16)
                Wdn = sb(e1, "Wdn", [128, 22, D], BF16)
                wload(Win, ffn_w_in[li], 8)
                wload(Wdn, ffn_w_dn[li], 22)
                y = sb(e1, "y", [128, 8, TT])
                xb = sb(e1, "xb", [128, 8, TT], BF16)
                hh = sb(e1, "hh", [128, 22, TT], BF16)
                sg = [sb(e1, "sg", [128, TT]) for i in range(2)]
                st = [sb(e1, "st", [128, TT]) for i in range(3)]
                tmp = sb(e1, "tmp", [128, TT])
                sc2 = [(sb(e1, "ybf", [128, TT], BF16), sb(e1, "ysq", [128, TT], BF16)) for i in range(2)]
                mean, msq, lnv = st
                s1, s2 = banks[6], banks[7]

                def load_xb(it):
                    sch.dma("pool", xb[:], fm(src.t, it * TT), r=[src], w=[xb])

                def load_y(it):
                    sch.dma("sp", y[:], fm(src.t, it * TT), r=[src], w=[y])

                def ln_gen(t0, nxt):
                    for c in range(8):
                        DVE(lambda: nc.vector.tensor_tensor(out=tmp[:], in0=y[:, c, :], in1=mean[:], op=ALU.subtract),
                            r=[y, mean], w=[tmp])
                        DVE(lambda: nc.vector.tensor_tensor(out=tmp[:], in0=tmp[:], in1=lnv[:], op=ALU.mult),
                            r=[tmp, lnv], w=[tmp])
                        ACT(lambda: nc.scalar.activation(out=y[:, c, :], in_=tmp[:], func=AF.Identity,
                                                         scale=pv[:, gcol + c:gcol + c + 1], bias=pv[:, bcol + c:bcol + c + 1]),
                            r=[tmp, pv], w=[y])
                        yield
                    sch.dma("sp", fm(dst.t, t0), y[:], r=[y], w=[dst])
                    if nxt is not None:
                        load_y(nxt)
                    yield

                def stats_mm(oc, k):
                    PE(lambda: nc.tensor.matmul(s1[:], lhsT=ones_b[:], rhs=sc2[k][0][:], start=(oc == 0), stop=(oc == 7)),
                       r=[ones_b, sc2[k][0]], w=[s1])
                    PE(lambda: nc.tensor.matmul(s2[:], lhsT=ones_b[:], rhs=sc2[k][1][:], start=(oc == 0), stop=(oc == 7)),
                       r=[ones_b, sc2[k][1]], w=[s2])

                load_xb(0)
                load_y(0)
                pending = None
                for it in range(NT):
                    for j in range(22):
                        bg = nextps(6)
                        bu = nextps(6)
                        for kc in range(8):
                            PE(lambda: nc.tensor.matmul(bg[:], lhsT=Win[:, kc, j * 128:(j + 1) * 128], rhs=xb[:, kc, :],
                                                        start=(kc == 0), stop=(kc == 7)), r=[Win, xb], w=[bg], ms=(kc == 7))
                        for kc in range(8):
                            PE(lambda: nc.tensor.matmul(bu[:], lhsT=Win[:, kc, DFF + j * 128:DFF + (j + 1) * 128],
                                                        rhs=xb[:, kc, :], start=(kc == 0), stop=(kc == 7)),
                               r=[Win, xb], w=[bu], ms=(kc == 7))
                        s_ = sg[j % 2]
                        ACT(lambda: nc.scalar.activation(out=s_[:], in_=bg[:], func=AF.Silu), r=[bg], w=[s_])
                        DVE(lambda: nc.vector.tensor_tensor(out=hh[:, j, :], in0=s_[:], in1=bu[:], op=ALU.mult),
                            r=[s_, bu], w=[hh])
                        if pending is not None and j >= 2:
                            next(pending, None)
                    if pending is not None:
                        for _ in pending:
                            pass
                        pending = None
                    if it + 1 < NT:
                        load_xb(it + 1)
                    for oc in range(8):
                        bk = nextps(6)
                        for j in range(22):
                            PE(lambda: nc.tensor.matmul(bk[:], lhsT=Wdn[:, j, oc * 128:(oc + 1) * 128], rhs=hh[:, j, :],
                                                        start=(j == 0), stop=(j == 21)), r=[Wdn, hh], w=[bk], ms=(j == 21))
                        DVE(lambda: nc.vector.scalar_tensor_tensor(out=y[:, oc, :], in0=y[:, oc, :], scalar=ALPHA, in1=bk[:],
                                                                   op0=ALU.mult, op1=ALU.add), r=[y, bk], w=[y])
                        k = oc % 2
                        DVE(lambda: nc.vector.tensor_copy(out=sc2[k][0][:], in_=y[:, oc, :]), r=[y], w=[sc2[k][0]])
                        ACT(lambda: nc.scalar.activation(out=sc2[k][1][:], in_=y[:, oc, :], func=AF.Square), r=[y], w=[sc2[k][1]])
                        if oc >= 1:
                            stats_mm(oc - 1, (oc - 1) % 2)
                    stats_mm(7, 1)
                    ACT(lambda: nc.scalar.activation(out=mean[:], in_=s1[:], func=AF.Copy, scale=1.0 / D), r=[s1], w=[mean])
                    DVE(lambda: nc.vector.tensor_tensor(out=msq[:], in0=mean[:], in1=mean[:], op=ALU.mult), r=[mean], w=[msq])
                    DVE(lambda: nc.vector.scalar_tensor_tensor(out=msq[:], in0=s2[:], scalar=1.0 / D, in1=msq[:],
                                                               op0=ALU.mult, op1=ALU.subtract), r=[s2, msq], w=[msq])
                    ACT(lambda: nc.scalar.activation(out=lnv[:], in_=msq[:], func=AF.Ln, bias=LN_EPS), r=[msq], w=[lnv])
                    ACT(lambda: nc.scalar.activation(out=lnv[:], in_=lnv[:], func=AF.Exp, scale=-0.5), r=[lnv], w=[lnv])
                    pending = ln_gen(it * TT, it + 1 if it + 1 < NT else None)
                for _ in pending:
                    pass
                sch.barrier()

        def stage_ple(li, src, dst, final):
            with ExitStack() as e1:
                Wg = sb(e1, "Wg", [128, 8, D], BF16)
                Wp = sb(e1, "Wp", [128, 2, D], BF16)
                wload(Wg, ple_w_gate[li], 8)
                wload(Wp, ple_w_proj[li], 2)
                if not final:
                    Wh = sb(e1, "Wh", [128, 8, 4096], BF16)
                    wload(Wh, hg_w_in, 8)
                NY = 2 if final else 1
                yy = [sb(e1, "y", [128, 8, TT]) for _ in range(NY)]
                xb = sb(e1, "xb", [128, 8, TT], BF16)
                ptoks = [sb(e1, "ptok", [128, 4, 256]) for _ in range(NY)]
                pT = sb(e1, "pT", [128, 2, TT], BF16)

                def ld_in(it):
                    t0_ = it * TT
                    sch.dma("sp", yy[it % NY][:], fm(src.t, t0_), r=[src], w=[yy[it % NY]])
                    sch.dma("sp", ptoks[it % NY][:], p_in[li, t0_:t0_ + TT, :].rearrange("(n p) d -> p n d", p=128),
                            w=[ptoks[it % NY]])

                ld_in(0)
                sg = [sb(e1, "sg%d" % i, [128, TT]) for i in range(2)]
                tmp = sb(e1, "tmp", [128, TT])
                if final:
                    otok = sb(e1, "otok", [128, 4, D])
                else:
                    FS = [dict((nm, sb(e1, "f_" + nm, [128, TT])) for nm in
                               ("sq", "ft", "kk", "lf", "G", "d1", "d2", "e0", "e1", "e2", "e3")) for _ in range(2)]
                    HO = [dict((nm, sb(e1, "ho_" + nm, [128, TT], BF16)) for nm in ("qa", "qs", "ka", "kh", "gt"))
                          for _ in range(2)]
                    HD = sb(e1, "HD", [128, 8, 4])
                    HV = sb(e1, "HV", [128, 4, D], BF16)
                for it in range(NT):
                    t0 = it * TT
                    y = yy[it % NY]
                    ptok = ptoks[it % NY]
                    if final and it + 1 < NT:
                        ld_in(it + 1)
                    for c in range(8):
                        DVE(lambda: nc.vector.tensor_copy(out=xb[:, c, :], in_=y[:, c, :]), r=[y], w=[xb])
                    for c2 in range(2):
                        bk = nextps()
                        for n in range(4):
                            PE(lambda: nc.tensor.transpose(out=bk[:, n * 128:(n + 1) * 128],
                                                           in_=ptok[:, n, c2 * 128:(c2 + 1) * 128], identity=ident_f),
                               r=[ptok, cst], w=[bk], ms=(n == 3))
                        ACT(lambda: nc.scalar.copy(out=pT[:, c2, :], in_=bk[:]), r=[bk], w=[pT])
                    for oc in range(8):
                        bg = nextps()
                        bp = nextps()
                        for kc in range(8):
                            PE(lambda: nc.tensor.matmul(bg[:], lhsT=Wg[:, kc, oc * 128:(oc + 1) * 128], rhs=xb[:, kc, :],
                                                        start=(kc == 0), stop=(kc == 7)), r=[Wg, xb], w=[bg], ms=(kc == 7))
                        for kc in range(2):
                            PE(lambda: nc.tensor.matmul(bp[:], lhsT=Wp[:, kc, oc * 128:(oc + 1) * 128], rhs=pT[:, kc, :],
                                                        start=(kc == 0), stop=(kc == 1)), r=[Wp, pT], w=[bp], ms=(kc == 1))
                        s_ = sg[oc % 2]
                        ACT(lambda: nc.scalar.activation(out=s_[:], in_=bg[:], func=AF.Sigmoid), r=[bg], w=[s_])
                        DVE(lambda: nc.vector.tensor_tensor(out=tmp[:], in0=s_[:], in1=bp[:], op=ALU.mult),
                            r=[s_, bp], w=[tmp])
                        DVE(lambda: nc.vector.tensor_tensor(out=y[:, oc, :], in0=y[:, oc, :], in1=tmp[:], op=ALU.add),
                            r=[y, tmp], w=[y])
                    if final:
                        for n in range(4):
                            for hf in range(2):
                                bk = nextps()
                                for c in range(4):
                                    PE(lambda: nc.tensor.transpose(out=bk[:, c * 128:(c + 1) * 128],
                                                                   in_=y[:, hf * 4 + c, n * 128:(n + 1) * 128],
                                                                   identity=ident_f), r=[y, cst], w=[bk], ms=(c == 3))
                                ACT(lambda: nc.scalar.copy(out=otok[:, n, hf * 512:(hf + 1) * 512], in_=bk[:]),
                                    r=[bk], w=[otok])
                        sch.dma("sp", out[t0:t0 + TT, :].rearrange("(n p) d -> p n d", p=128), otok[:], r=[otok], w=[dst])
                        continue
                    sch.dma("sp", fm(dst.t, t0), y[:], r=[y], w=[dst])
                    for c in range(8):
                        DVE(lambda: nc.vector.tensor_copy(out=xb[:, c, :], in_=y[:, c, :]), r=[y], w=[xb])
                    if it + 1 < NT:
                        ld_in(it + 1)

                    def proj(col0):
                        bk = nextps()
                        for kc in range(8):
                            PE(lambda: nc.tensor.matmul(bk[:], lhsT=Wh[:, kc, col0:col0 + 128], rhs=xb[:, kc, :],
                                                        start=(kc == 0), stop=(kc == 7)), r=[Wh, xb], w=[bk], ms=(kc == 7))
                        return bk

                    def bc(tl, pos_):
                        base = tl.t[:, pos_:pos_ + 1]
                        return bass.AP(base.tensor, base.offset, [[TT, 128], [128, 4], [0, 128]])

                    def v3(tl):
                        return tl[:].rearrange("p (c t) -> p c t", c=4)

                    def head_a(h):
                        F = FS[h % 2]
                        O = HO[h % 2]
                        bq = proj(h * 128)
                        ACT(lambda: nc.scalar.activation(out=F["sq"][:], in_=bq[:], func=AF.Silu), r=[bq], w=[F["sq"]])
                        bf = proj(1024 + h * 128)
                        ACT(lambda: nc.scalar.activation(out=F["ft"][:], in_=bf[:], func=AF.Sigmoid), r=[bf], w=[F["ft"]])
                        bgt = proj(3072 + h * 128)
                        ACT(lambda: nc.scalar.activation(out=O["gt"][:], in_=bgt[:], func=AF.Silu), r=[bgt], w=[O["gt"]])
                        DVE(lambda: nc.vector.tensor_scalar(out=F["ft"][:], in0=F["ft"][:], scalar1=lbv[:, 8 + h:9 + h],
                                                            scalar2=lbv[:, h:h + 1], op0=ALU.mult, op1=ALU.add),
                            r=[F["ft"], lbv], w=[F["ft"]])
                        DVE(lambda: nc.vector.tensor_scalar(out=F["kk"][:], in0=F["ft"][:], scalar1=-1.0, scalar2=1.0,
                                                            op0=ALU.mult, op1=ALU.add), r=[F["ft"]], w=[F["kk"]])

                    def head_b(h):
                        F = FS[h % 2]
                        O = HO[h % 2]
                        ACT(lambda: nc.scalar.activation(out=F["lf"][:], in_=F["ft"][:], func=AF.Ln), r=[F["ft"]], w=[F["lf"]])
                        DVE(lambda: nc.vector.tensor_tensor_scan(out=F["G"][:], data0=cst[:, 768:1280], data1=F["lf"][:],
                                                                 initial=0.0, op0=ALU.mult, op1=ALU.add),
                            r=[cst, F["lf"]], w=[F["G"]])
                        ACT(lambda: nc.scalar.activation(out=F["e0"][:], in_=F["G"][:], func=AF.Exp), r=[F["G"]], w=[F["e0"]])
                        DVE(lambda: nc.vector.tensor_tensor(out=v3(F["d1"]), in0=v3(F["G"]), in1=bc(F["G"], 63), op=ALU.subtract),
                            r=[F["G"]], w=[F["d1"]])
                        DVE(lambda: nc.vector.tensor_tensor(out=v3(F["d2"]), in0=bc(F["G"], 127), in1=v3(F["G"]), op=ALU.subtract),
                            r=[F["G"]], w=[F["d2"]])
                        ACT(lambda: nc.scalar.activation(out=F["e1"][:], in_=F["d1"][:], func=AF.Exp), r=[F["d1"]], w=[F["e1"]])
                        ACT(lambda: nc.scalar.activation(out=F["e2"][:], in_=F["d1"][:], func=AF.Exp, scale=-1.0),
                            r=[F["d1"]], w=[F["e2"]])
                        ACT(lambda: nc.scalar.activation(out=F["e3"][:], in_=F["d2"][:], func=AF.Exp), r=[F["d2"]], w=[F["e3"]])
                        DVE(lambda: nc.vector.tensor_tensor(out=O["qs"][:], in0=F["sq"][:], in1=F["e0"][:], op=ALU.mult),
                            r=[F["sq"], F["e0"]], w=[O["qs"]])
                        DVE(lambda: nc.vector.tensor_copy(out=HD[:, h, :], in_=F["e0"][:, 127:TT:128]), r=[F["e0"]], w=[HD])
                        POOL(lambda: nc.gpsimd.tensor_tensor(out=O["qa"][:], in0=F["sq"][:], in1=F["e1"][:], op=ALU.mult),
                             r=[F["sq"], F["e1"]], w=[O["qa"]])
                        DVE(lambda: nc.vector.tensor_tensor(out=O["ka"][:], in0=F["kk"][:], in1=F["e2"][:], op=ALU.mult),
                            r=[F["kk"], F["e2"]], w=[O["ka"]])
                        POOL(lambda: nc.gpsimd.tensor_tensor(out=O["kh"][:], in0=F["kk"][:], in1=F["e3"][:], op=ALU.mult),
                             r=[F["kk"], F["e3"]], w=[O["kh"]])
                        for nm, dstt in (("qa", sc_hqa), ("qs", sc_hqs), ("ka", sc_hka), ("kh", sc_hkh), ("gt", sc_hg)):
                            sch.dma("sp", dstt[h, :, t0:t0 + TT], O[nm][:], r=[O[nm]], w=[dstt])

                    head_a(0)
                    for h in range(8):
                        if h + 1 < 8:
                            head_a(h + 1)
                        head_b(h)
                    for n in range(4):
                        for hf in range(2):
                            bk = nextps()
                            for kc in range(8):
                                PE(lambda: nc.tensor.matmul(bk[:], lhsT=xb[:, kc, n * 128:(n + 1) * 128],
                                                            rhs=Wh[:, kc, 2048 + hf * 512:2048 + (hf + 1) * 512],
                                                            start=(kc == 0), stop=(kc == 7)), r=[Wh, xb], w=[bk], ms=(kc == 7))
                            DVE(lambda: nc.vector.tensor_copy(out=HV[:, n, hf * 512:(hf + 1) * 512], in_=bk[:]),
                                r=[bk], w=[HV])
                    sch.dma("sp", sc_hd[:, :, it * 4:(it + 1) * 4], HD[:], r=[HD], w=[sc_hd])
                    sch.dma("sp", sc_hv[t0:t0 + TT, :].rearrange("(n p) d -> p n d", p=128), HV[:], r=[HV], w=[sc_hv])
                sch.barrier()

        def stage_hgrn(src_x, dst, gcol, bcol):
            with ExitStack() as e1:
                Wo = sb(e1, "Who", [128, 8, D], BF16)
                wload(Wo, hg_w_o, 8)
                for kc in range(8):
                    DVE(lambda: nc.vector.tensor_scalar(out=Wo[:, kc, :], in0=Wo[:, kc, :], scalar1=pv[:, c_on + kc:c_on + kc + 1],
                                                        scalar2=None, op0=ALU.mult), r=[Wo, pv], w=[Wo])
                um = sb(e1, "um", [128, 128])
                DVE(lambda: nc.vector.tensor_scalar(out=um[:], in0=cst[:, 128:256], scalar1=-1.0, scalar2=None,
                                                    op0=ALU.is_gt), r=[cst], w=[um])
                um_bc = bass.AP(um.t[:].tensor, um.t[:].offset, [[128, 128], [0, 4], [1, 128]])
                NBUF = 2
                HQA = [sb(e1, "HQA", [128, 8, TT], BF16) for _ in range(NBUF)]
                HQS = [sb(e1, "HQS", [128, 8, TT], BF16) for _ in range(NBUF)]
                HKA = [sb(e1, "HKA", [128, 8, TT], BF16) for _ in range(NBUF)]
                HKH = [sb(e1, "HKH", [128, 8, TT], BF16) for _ in range(NBUF)]
                HGt = [sb(e1, "HGt", [128, 8, TT], BF16) for _ in range(NBUF)]
                HD = [sb(e1, "HD", [128, 8, 4]) for _ in range(NBUF)]
                HV = [sb(e1, "HV", [128, 4, D], BF16) for _ in range(NBUF)]
                yy = [sb(e1, "y", [128, 8, TT]) for _ in range(1)]
                OG = sb(e1, "OG", [128, 8, TT], BF16)
                S32 = [sb(e1, "S32", [128, 4, 128]) for _ in range(2)]
                Sb = [sb(e1, "Sb", [128, 4, 128], BF16) for _ in range(2)]
                khT = [sb(e1, "khT", [128, 4, 128], BF16) for _ in range(8)]
                ATb = [sb(e1, "ATb", [128, 4, 128], BF16) for _ in range(8)]
                osq = [sb(e1, "osq", [128, TT], BF16) for _ in range(2)]
                on = [sb(e1, "on", [128, TT]) for _ in range(2)]
                lnr = [sb(e1, "lnr", [128, TT]) for _ in range(2)]
                scr = sb(e1, "scr", [128, 16, TT], BF16)
                st = [sb(e1, "st", [128, TT]) for i in range(3)]
                tmp = sb(e1, "tmp", [128, TT])
                for g in range(2):
                    DVE(lambda: nc.vector.memset(S32[g][:], 0.0), w=[S32[g]])
                    DVE(lambda: nc.vector.memset(Sb[g][:], 0.0), w=[Sb[g]])

                def ld(it):
                    t0 = it * TT
                    i = it % NBUF
                    sch.dma("sp", HKH[i][:], hm(sc_hkh.t, t0), r=[sc_hkh], w=[HKH[i]])
                    sch.dma("sp", HKA[i][:], hm(sc_hka.t, t0), r=[sc_hka], w=[HKA[i]])
                    sch.dma("sp", HQA[i][:], hm(sc_hqa.t, t0), r=[sc_hqa], w=[HQA[i]])
                    sch.dma("sp", HQS[i][:], hm(sc_hqs.t, t0), r=[sc_hqs], w=[HQS[i]])
                    sch.dma("sp", HV[i][:], sc_hv[t0:t0 + TT, :].rearrange("(n p) d -> p n d", p=128), r=[sc_hv], w=[HV[i]])
                    sch.dma("sp", HD[i][:], sc_hd[:, :, it * 4:(it + 1) * 4], r=[sc_hd], w=[HD[i]])
                    sch.dma("sp", HGt[i][:], hm(sc_hg.t, t0), r=[sc_hg], w=[HGt[i]])

                ld(0)
                kk = 0
                for it in range(NT):
                    t0 = it * TT
                    if it + 1 < NT:
                        ld(it + 1)
                    i = it % NBUF
                    qa, qs, ka, kh, gt, hd, hv, y = HQA[i], HQS[i], HKA[i], HKH[i], HGt[i], HD[i], HV[i], yy[0]
                    sch.dma("sp", y[:], fm(src_x.t, t0), r=[src_x], w=[y])
                    for h in range(8):
                        bt = nextps(3)
                        for c in range(4):
                            cs = slice(c * 128, (c + 1) * 128)
                            PE(lambda: nc.tensor.matmul(bt[:, cs], lhsT=kh[:, h, cs], rhs=ident_b[:], start=True, stop=True),
                               r=[kh, ident_b], w=[bt], ms=(c == 3))
                        ACT(lambda: nc.scalar.copy(out=khT[h][:].rearrange("p c d -> p (c d)"), in_=bt[:]), r=[bt], w=[khT[h]])
                        ba = nextps(3)
                        for c in range(4):
                            cs = slice(c * 128, (c + 1) * 128)
                            PE(lambda: nc.tensor.matmul(ba[:, cs], lhsT=ka[:, h, cs], rhs=qa[:, h, cs], start=True, stop=True),
                               r=[ka, qa], w=[ba], ms=(c == 3))
                        DVE(lambda: nc.vector.tensor_tensor(out=ATb[h][:], in0=ba[:].rearrange("p (c t) -> p c t", c=4),
                                                            in1=um_bc, op=ALU.mult), r=[ba, um], w=[ATb[h]])
                    epi = None
                    for c in range(4):
                        cs = slice(c * 128, (c + 1) * 128)
                        for g in range(2):
                            ob_ = banks[3 + (kk % 3)]
                            bd = banks[6]
                            j2 = kk % 2
                            kk += 1
                            for j in range(4):
                                h = 4 * g + j
                                js = slice(j * 128, (j + 1) * 128)
                                hs = slice(h * 128, (h + 1) * 128)
                                PE(lambda: nc.tensor.matmul(ob_[:, js], lhsT=Sb[g][:, j, :], rhs=qs[:, h, cs], start=True, stop=False),
                                   r=[Sb[g], qs], w=[ob_], ms=False)
                                PE(lambda: nc.tensor.matmul(ob_[:, js], lhsT=hv[:, c, hs], rhs=ATb[h][:, c, :], start=False, stop=True),
                                   r=[hv, ATb[h]], w=[ob_], ms=(j == 3))
                            for j in range(4):
                                h = 4 * g + j
                                js = slice(j * 128, (j + 1) * 128)
                                hs = slice(h * 128, (h + 1) * 128)
                                PE(lambda: nc.tensor.matmul(bd[:, js], lhsT=khT[h][:, c, :], rhs=hv[:, c, hs], start=True, stop=True),
                                   r=[khT[h], hv], w=[bd], ms=(j == 3))
                            dec = bass.AP(hd.t[:].tensor, hd.t[:, 4 * g, c:c + 1].offset, [[32, 128], [4, 4], [0, 128]])
                            DVE(lambda: nc.vector.tensor_tensor(out=S32[g][:], in0=S32[g][:], in1=dec, op=ALU.mult),
                                r=[S32[g], hd], w=[S32[g]])
                            DVE(lambda: nc.vector.tensor_tensor(out=S32[g][:], in0=S32[g][:],
                                                                in1=bd[:].rearrange("p (j e) -> p j e", j=4), op=ALU.add),
                                r=[S32[g], bd], w=[S32[g]])
                            ACT(lambda: nc.scalar.copy(out=Sb[g][:], in_=S32[g][:]), r=[S32[g]], w=[Sb[g]])

                            def make_epi(ob_=ob_, j2=j2, g=g, cs=cs):
                                def run():
                                    ACT(lambda: nc.scalar.activation(out=osq[j2][:], in_=ob_[:], func=AF.Square),
                                        r=[ob_], w=[osq[j2]])
                                    bs = banks[7]
                                    PE(lambda: nc.tensor.matmul(bs[:], lhsT=ones_b[:], rhs=osq[j2][:], start=True, stop=True),
                                       r=[ones_b, osq[j2]], w=[bs])
                                    ACT(lambda: nc.scalar.activation(out=lnr[j2][:], in_=bs[:], func=AF.Ln, scale=1.0 / 128.0,
                                                                     bias=RMS_EPS), r=[bs], w=[lnr[j2]])
                                    ACT(lambda: nc.scalar.activation(out=lnr[j2][:], in_=lnr[j2][:], func=AF.Exp, scale=-0.5),
                                        r=[lnr[j2]], w=[lnr[j2]])
                                    DVE(lambda: nc.vector.tensor_tensor(out=on[j2][:], in0=ob_[:], in1=lnr[j2][:], op=ALU.mult),
                                        r=[ob_, lnr[j2]], w=[on[j2]])
                                    POOL(lambda: nc.gpsimd.tensor_tensor(out=OG[:, 4 * g:4 * g + 4, cs],
                                                                         in0=on[j2][:].rearrange("p (j t) -> p j t", j=4),
                                                                         in1=gt[:, 4 * g:4 * g + 4, cs], op=ALU.mult),
                                         r=[on[j2], gt], w=[OG])
                                return run

                            if epi is not None:
                                epi()
                            epi = make_epi()
                    epi()
                    if dbg_og is not None:
                        sch.dma("sp", fm(dbg_og.t, t0), OG[:], r=[OG], w=[dbg_og])
                    for oc in range(8):
                        bk = nextps(3)
                        for kc in range(8):
                            PE(lambda: nc.tensor.matmul(bk[:], lhsT=Wo[:, kc, oc * 128:(oc + 1) * 128], rhs=OG[:, kc, :],
                                                        start=(kc == 0), stop=(kc == 7)), r=[Wo, OG], w=[bk], ms=(kc == 7))
                        DVE(lambda: nc.vector.scalar_tensor_tensor(out=y[:, oc, :], in0=y[:, oc, :], scalar=ALPHA, in1=bk[:],
                                                                   op0=ALU.mult, op1=ALU.add), r=[y, bk], w=[y])
                    ln_tile(y, scr, scr, st, tmp, gcol, bcol, nb=3)
                    sch.dma("sp", fm(dst.t, t0), y[:], r=[y], w=[dst])
                sch.barrier()

        out_t = T(out)
        if upto >= 3:
            stage_proj_ln(mla_w_o, sc_o, sc_xT, sc_x1, c_lng[0][0], c_lnb[0][0])
        if upto >= 4:
            stage_ffn(0, sc_x1, sc_xT, c_lng[0][1], c_lnb[0][1])
        if upto >= 5:
            stage_ple(0, sc_xT, sc_x1, False)
        if upto >= 6:
            stage_hgrn(sc_x1, sc_xT, c_lng[1][0], c_lnb[1][0])
        if upto >= 7:
            stage_ffn(1, sc_xT, sc_x1, c_lng[1][1], c_lnb[1][1])
        if upto >= 8:
            stage_ple(1, sc_x1, out_t, True)

        sch.barrier()
    return nc


def make_consts():
    c = np.zeros((128, 1408), np.float32)
    c[:, 0:128] = np.eye(128, dtype=np.float32)
    k = np.arange(128)[:, None]
    q = np.arange(128)[None, :]
    c[:, 128:256] = np.where(k <= q, 0.0, -30000.0)
    s = np.arange(64)[:, None]
    t = np.arange(64)[None, :]
    c[0:64, 256:768] = np.tile((s <= t).astype(np.float32), (1, 8))
    c[:, 768:1280] = (np.arange(512) % 128 != 0).astype(np.float32)[None, :]
    inv = (10000.0 ** (-np.arange(0, 64, 2, dtype=np.float32) / 64.0)).astype(np.float32)
    c[0:64, 1280] = np.concatenate([inv, inv])
    c[0:64, 1281] = np.concatenate([-inv, inv])
    return c


_W_NAMES = ["mla_w_dqkv", "mla_w_uq", "mla_w_ukv", "mla_w_o", "hgrn_w_in", "hgrn_w_o"]
_W_FULL = ["ffn_w_in", "ffn_w_down", "ple_w_proj", "ple_w_gate"]


def make_pvec(inputs):
    def col(v):
        return np.asarray(v, dtype=np.float32).reshape(-1, 128).T
    cols = [col(inputs["mla_q_norm"][0]), col(inputs["mla_kv_norm"][0])]
    for nm in ("ln_mix_g", "ln_ffn_g"):
        pass
    for i in range(2):
        cols += [col(inputs["ln_mix_g"][i]), col(inputs["ln_ffn_g"][i])]
    for i in range(2):
        cols += [col(inputs["ln_mix_b"][i]), col(inputs["ln_ffn_b"][i])]
    cols += [col(inputs["hgrn_out_norm"][0]), col(inputs["hgrn_lb_logits"][0]), col(inputs["hgrn_lb_logits"][1])]
    pv = np.concatenate(cols, axis=1)
    out = np.zeros((128, 96), np.float32)
    out[:, :pv.shape[1]] = pv
    return out


def make_in_maps(inputs, n_cores, S):
    consts = make_consts()
    shared = {"consts": consts, "pvec": make_pvec(inputs)}
    for k in _W_NAMES:
        shared[k] = np.ascontiguousarray(np.asarray(inputs[k], dtype=np.float32)[0])
    for k in _W_FULL:
        shared[k] = np.ascontiguousarray(np.asarray(inputs[k], dtype=np.float32))
    maps = []
    xs = np.asarray(inputs["x"])
    ps = np.asarray(inputs["p"])
    po = np.asarray(inputs["positions"])
    for b in range(n_cores):
        m = dict(shared)
        m["x"] = np.ascontiguousarray(xs[b, :S])
        m["p"] = np.ascontiguousarray(ps[:, b, :S])
        m["pos"] = np.ascontiguousarray(po[b:b + 1, :S]).astype(np.int32)
        maps.append(m)
    return maps


def kernel(**inputs):
    n = 8
    S = 8192
    nc = build(S)
    maps = make_in_maps(inputs, n, S)
    res = run_bass_kernel_spmd(nc, maps, core_ids=list(range(n)))
    return np.stack([np.asarray(r["out"]) for r in res.results], axis=0).astype(np.float32)
```
